# Optimizing a Trainium2 kernel written in Bass

```python
import jax, jax.numpy as jnp
from jax import lax
import numpy as np

D_MODEL = 1024
BATCH = 8
SEQ = 4096
DEPTH = 2

HEAD_DIM = 64
D_A = D_MODEL // 4
D_B = 3 * D_MODEL // 8
D_C = 3 * D_MODEL // 8
D_MIX = D_A + D_B + D_C
N_HEADS_A = D_A // HEAD_DIM
N_BLOCKS_B = D_B // HEAD_DIM
N_HEADS_C = D_C // HEAD_DIM
CHUNK = 128
CONV_B = 4
LRU_C = 8.0
LORA_W = 64
LORA_A = 64
LORA_G = 128
D_FF = 2816
CONV_FF = 3
N_MOD = 6
EPS = 1e-6
LN_EPS = 1e-5
GN_EPS = 64e-5
P_A = 2 * D_A
P_B = 2 * D_B
P_C = 3 * D_C + LORA_W + LORA_A + LORA_G
P_IN = P_A + P_B + P_C

kernel_name = "hybrid_sgu_rglru_rwkv7_adaln"


def rms_norm(x, g):
    xf = x.astype(jnp.float32)
    y = xf * lax.rsqrt(jnp.mean(xf * xf, axis=-1, keepdims=True) + EPS)
    return (y * g.astype(jnp.float32)).astype(x.dtype)


def causal_dwconv(x, w, b):
    k_w = w.shape[0]
    s = x.shape[1]
    xp = jnp.pad(x, ((0, 0), (k_w - 1, 0), (0, 0)))
    y = b + w[0] * xp[:, 0:s]
    for j in range(1, k_w):
        y = y + w[j] * xp[:, j:j + s]
    return y


def token_shift(x):
    return jnp.pad(x[:, :-1], ((0, 0), (1, 0), (0, 0)))


def chunked_sgu(p, ln_g, ln_b, w_s, b_s):
    z = jax.nn.gelu(p)
    u, v = jnp.split(z, 2, axis=-1)
    vf = v.astype(jnp.float32)
    mu = jnp.mean(vf, axis=-1, keepdims=True)
    var = jnp.mean(jnp.square(vf - mu), axis=-1, keepdims=True)
    v = ((vf - mu) * lax.rsqrt(var + LN_EPS) * ln_g.astype(jnp.float32) + ln_b.astype(jnp.float32)).astype(p.dtype)
    bn, s, _ = v.shape
    v = v.reshape(bn, s // CHUNK, CHUNK, N_HEADS_A, HEAD_DIM)
    mask = jnp.tril(jnp.ones((CHUNK, CHUNK), dtype=bool))
    w = jnp.where(mask, w_s, 0.0)
    mixed = jnp.einsum('hts,bnshd->bnthd', w, v) + b_s.T[None, None, :, :, None]
    return u * mixed.reshape(bn, s, D_A)


def rg_lru_block(p, conv_w, conv_b, w_ra, b_ra, w_ix, b_ix, lam):
    xr, yg = jnp.split(p, 2, axis=-1)
    xr = causal_dwconv(xr, conv_w, conv_b)
    bn, s, _ = xr.shape
    xh = xr.reshape(bn, s, N_BLOCKS_B, HEAD_DIM)
    r = jax.nn.sigmoid(jnp.einsum('bshi,hij->bshj', xh, w_ra).reshape(bn, s, D_B) + b_ra)
    i = jax.nn.sigmoid(jnp.einsum('bshi,hij->bshj', xh, w_ix).reshape(bn, s, D_B) + b_ix)
    log_a = (-LRU_C * r.astype(jnp.float32)) * jax.nn.softplus(-lam.astype(jnp.float32))
    a = jnp.exp(log_a)
    bterm = jnp.sqrt(-jnp.expm1(2.0 * log_a)) * (i * xr).astype(jnp.float32)

    def combine(left, right):
        a1, b1 = left
        a2, b2 = right
        return a1 * a2, a2 * b1 + b2

    _, h = lax.associative_scan(combine, (a, bterm), axis=1)
    return jax.nn.gelu(yg) * h.astype(p.dtype)


def rwkv7_time_mix(p, mu, w0, w2, a0, a2, g2, k_k, k_a, r_k, ln_w, ln_b):
    p = p + (token_shift(p) - p) * mu
    r, k, v, xw, xa, xg = jnp.split(
        p, [D_C, 2 * D_C, 3 * D_C, 3 * D_C + LORA_W, 3 * D_C + LORA_W + LORA_A], axis=-1)
    w = -jax.nn.softplus(-(w0 + jnp.tanh(xw) @ w2)) - 0.5
    decay = jnp.exp(-jnp.exp(w.astype(jnp.float32)))
    a = jax.nn.sigmoid(a0 + xa @ a2)
    g = jax.nn.sigmoid(xg) @ g2
    bn, s, _ = r.shape

    def heads(t):
        return t.astype(jnp.float32).reshape(bn, s, N_HEADS_C, HEAD_DIM)

    kk = heads(k * k_k)
    kk = kk * lax.rsqrt(jnp.maximum(jnp.sum(kk * kk, axis=-1, keepdims=True), 1e-24))
    k = k * (1.0 + (a - 1.0) * k_a)
    rh, kh, vh, wh, ah = heads(r), heads(k), heads(v), heads(decay), heads(a)

    def step(state, inp):
        r_t, w_t, k_t, v_t, kk_t, a_t = inp
        sa = jnp.einsum('bhvk,bhk->bhv', state, kk_t)
        state = (state * w_t[:, :, None, :]
                 - sa[..., :, None] * (kk_t * a_t)[..., None, :]
                 + v_t[..., :, None] * k_t[..., None, :])
        return state, jnp.einsum('bhvk,bhk->bhv', state, r_t)

    seq_first = [jnp.moveaxis(t, 1, 0) for t in (rh, wh, kh, vh, kk, ah)]
    state0 = jnp.zeros((bn, N_HEADS_C, HEAD_DIM, HEAD_DIM), jnp.float32)
    _, out = lax.scan(step, state0, tuple(seq_first))
    out = jnp.moveaxis(out, 0, 1)
    mean = jnp.mean(out, axis=-1, keepdims=True)
    var = jnp.mean(jnp.square(out - mean), axis=-1, keepdims=True)
    out = ((out - mean) * lax.rsqrt(var + GN_EPS) * ln_w.astype(jnp.float32).reshape(N_HEADS_C, HEAD_DIM)
           + ln_b.astype(jnp.float32).reshape(N_HEADS_C, HEAD_DIM))
    bonus = jnp.sum(rh * kh * r_k.astype(jnp.float32), axis=-1, keepdims=True) * vh
    out = (out + bonus).reshape(bn, s, D_C).astype(p.dtype)
    return out * g


def conv_glu_ffn(h, w_up, conv_w, conv_b, w_down):
    gate, val = jnp.split(h @ w_up, 2, axis=-1)
    gate = causal_dwconv(gate, conv_w, conv_b)
    return (jax.nn.silu(gate) * val) @ w_down


def setup_inputs(seed: int = 0) -> dict:
    key = jax.random.key(seed)
    ks = jax.random.split(key, 40)
    f32 = jnp.float32
    nrm = lambda k, shape, s: jax.random.normal(k, shape, f32) * s
    L = DEPTH
    u_a = jax.random.uniform(ks[15], (L, D_B), f32, 0.9, 0.999)
    s_a = u_a ** (1.0 / LRU_C)
    return {
        "x": nrm(ks[0], (BATCH, SEQ, D_MODEL), 1.0),
        "c": nrm(ks[1], (BATCH, D_MODEL), 1.0),
        "w_mod": nrm(ks[2], (L, D_MODEL, N_MOD * D_MODEL), 0.5 * D_MODEL ** -0.5),
        "b_mod": nrm(ks[3], (L, N_MOD * D_MODEL), 0.02),
        "norm_mix": 1.0 + nrm(ks[4], (L, D_MODEL), 0.02),
        "w_in": nrm(ks[5], (L, D_MODEL, P_IN), D_MODEL ** -0.5),
        "w_out": nrm(ks[6], (L, D_MIX, D_MODEL), D_MIX ** -0.5),
        "sgu_ln_g": 1.0 + nrm(ks[7], (L, D_A), 0.02),
        "sgu_ln_b": nrm(ks[8], (L, D_A), 0.02),
        "sgu_w": nrm(ks[9], (L, N_HEADS_A, CHUNK, CHUNK), 0.05),
        "sgu_b": 1.0 + nrm(ks[10], (L, N_HEADS_A, CHUNK), 0.02),
        "lru_conv_w": nrm(ks[11], (L, CONV_B, D_B), CONV_B ** -0.5),
        "lru_conv_b": nrm(ks[12], (L, D_B), 0.02),
        "lru_w_a": nrm(ks[13], (L, N_BLOCKS_B, HEAD_DIM, HEAD_DIM), HEAD_DIM ** -0.5),
        "lru_b_a": nrm(ks[14], (L, D_B), 0.02),
        "lru_w_x": nrm(ks[16], (L, N_BLOCKS_B, HEAD_DIM, HEAD_DIM), HEAD_DIM ** -0.5),
        "lru_b_x": nrm(ks[17], (L, D_B), 0.02),
        "lru_lambda": jnp.log(s_a) - jnp.log1p(-s_a),
        "rwkv_mu": jax.random.uniform(ks[18], (L, P_C), f32, 0.0, 1.0),
        "rwkv_w0": jax.random.uniform(ks[19], (L, D_C), f32, -6.0, -1.0),
        "rwkv_w2": nrm(ks[20], (L, LORA_W, D_C), 0.5 * LORA_W ** -0.5),
        "rwkv_a0": nrm(ks[21], (L, D_C), 0.1),
        "rwkv_a2": nrm(ks[22], (L, LORA_A, D_C), 0.5 * LORA_A ** -0.5),
        "rwkv_g2": nrm(ks[23], (L, LORA_G, D_C), LORA_G ** -0.5),
        "rwkv_k_k": 0.85 + nrm(ks[24], (L, D_C), 0.02),
        "rwkv_k_a": 1.0 + nrm(ks[25], (L, D_C), 0.02),
        "rwkv_r_k": nrm(ks[26], (L, N_HEADS_C, HEAD_DIM), 0.1),
        "rwkv_ln_w": 1.0 + nrm(ks[27], (L, D_C), 0.02),
        "rwkv_ln_b": nrm(ks[28], (L, D_C), 0.02),
        "norm_ffn": 1.0 + nrm(ks[29], (L, D_MODEL), 0.02),
        "ffn_w_up": nrm(ks[30], (L, D_MODEL, 2 * D_FF), D_MODEL ** -0.5),
        "ffn_conv_w": nrm(ks[31], (L, CONV_FF, D_FF), CONV_FF ** -0.5),
        "ffn_conv_b": nrm(ks[32], (L, D_FF), 0.02),
        "ffn_w_down": nrm(ks[33], (L, D_FF, D_MODEL), D_FF ** -0.5),
        "norm_final": 1.0 + nrm(ks[34], (D_MODEL,), 0.02),
    }


def reference(x, c, w_mod, b_mod, norm_mix, w_in, w_out,
              sgu_ln_g, sgu_ln_b, sgu_w, sgu_b,
              lru_conv_w, lru_conv_b, lru_w_a, lru_b_a, lru_w_x, lru_b_x, lru_lambda,
              rwkv_mu, rwkv_w0, rwkv_w2, rwkv_a0, rwkv_a2, rwkv_g2, rwkv_k_k, rwkv_k_a,
              rwkv_r_k, rwkv_ln_w, rwkv_ln_b,
              norm_ffn, ffn_w_up, ffn_conv_w, ffn_conv_b, ffn_w_down, norm_final):
    c_act = jax.nn.silu(c)
    for l in range(DEPTH):
        mod = c_act @ w_mod[l] + b_mod[l]
        sh1, sc1, gt1, sh2, sc2, gt2 = [m[:, None, :] for m in jnp.split(mod, N_MOD, axis=-1)]
        h = rms_norm(x, norm_mix[l]) * (1.0 + sc1) + sh1
        p = h @ w_in[l]
        p_a, p_b, p_c = jnp.split(p, [P_A, P_A + P_B], axis=-1)
        y_a = chunked_sgu(p_a, sgu_ln_g[l], sgu_ln_b[l], sgu_w[l], sgu_b[l])
        y_b = rg_lru_block(p_b, lru_conv_w[l], lru_conv_b[l], lru_w_a[l], lru_b_a[l],
                           lru_w_x[l], lru_b_x[l], lru_lambda[l])
        y_c = rwkv7_time_mix(p_c, rwkv_mu[l], rwkv_w0[l], rwkv_w2[l], rwkv_a0[l], rwkv_a2[l],
                             rwkv_g2[l], rwkv_k_k[l], rwkv_k_a[l], rwkv_r_k[l],
                             rwkv_ln_w[l], rwkv_ln_b[l])
        y = jnp.concatenate([y_a, y_b, y_c], axis=-1) @ w_out[l]
        x = x + gt1 * y
        h = rms_norm(x, norm_ffn[l]) * (1.0 + sc2) + sh2
        x = x + gt2 * conv_glu_ffn(h, ffn_w_up[l], ffn_conv_w[l], ffn_conv_b[l], ffn_w_down[l])
    return rms_norm(x, norm_final)
```

```python
import numpy as np
from contextlib import ExitStack
import concourse.bass as bass
import concourse.mybir as mybir
from concourse.bass_utils import run_bass_kernel_spmd

F32 = mybir.dt.float32
BF16 = mybir.dt.bfloat16
AF = mybir.ActivationFunctionType
ALU = mybir.AluOpType

COMPUTE = ("pe", "dve", "act", "pool")
ALLQ = ("pe", "dve", "act", "pool", "sp")


class TA:
    __slots__ = ("ap", "key")

    def __init__(self, ap, key):
        self.ap = ap
        self.key = key


class Tile:
    def __init__(self, h, name):
        self.h = h
        self.name = name

    def __getitem__(self, idx):
        return TA(self.h[idx], (self.name, ""))

    def s(self, sub, idx):
        return TA(self.h[idx], (self.name, str(sub)))


class Sched:
    def __init__(self, nc, es, n_epochs=8):
        self.nc = nc
        self.es = es
        self.ops = {q: [] for q in ALLQ}
        self.cnt = {e: 0 for e in COMPUTE}
        self.state = {}
        self.seen = {q: {} for q in ALLQ}
        self.epoch = 0
        self.n_epochs = n_epochs
        self.sems = {}
        for e in COMPUTE:
            for ep in range(n_epochs):
                self.sems[(e, ep)] = es.enter_context(nc.semaphore(f"s_{e}_{ep}"))
        self.dma_sem = {}
        self.dma_cnt = {}
        self.need_inc = {}
        self.n_instr = 0

    def _entries(self, key):
        name, sub = key
        d = self.state.setdefault(name, {})
        if sub == "":
            if "" not in d:
                d[""] = [None, []]
            return list(d.values())
        out = []
        if "" in d:
            out.append(d[""])
        if sub not in d:
            d[sub] = [None, []]
        out.append(d[sub])
        return out

    def _own(self, key):
        name, sub = key
        d = self.state.setdefault(name, {})
        if sub not in d:
            d[sub] = [None, []]
        return d[sub]

    def _collect(self, reads, writes, q=None):
        deps = {}

        def add(t):
            if t is None:
                return
            src, idx = t
            if deps.get(src, 0) < idx:
                deps[src] = idx

        for k in reads:
            excl = k[0].startswith("ps")
            for ent in self._entries(k):
                add(ent[0])
                if excl:
                    for r in ent[1]:
                        if r[0] != q:
                            add(r)
        for k in writes:
            for ent in self._entries(k):
                add(ent[0])
                for r in ent[1]:
                    add(r)
        return deps

    def _record(self, me, reads, writes):
        for k in reads:
            self._own(k)[1].append(me)
        for k in writes:
            name, sub = k
            if sub == "":
                self.state[name] = {"": [me, []]}
            else:
                ent = self._own(k)
                ent[0] = me
                ent[1] = []

    def _waits(self, q, deps):
        for src, idx in deps.items():
            if src == q and q == "pe":
                continue
            if self.seen[q].get(src, 0) >= idx:
                continue
            self.seen[q][src] = idx
            if src in COMPUTE:
                self.need_inc.setdefault((src, self.epoch), set()).add(idx)
                self.ops[q].append(("wait", src, self.epoch, idx))
            else:
                idx = self.dma_cnt[src[4:]]
                self.seen[q][src] = idx
                self.ops[q].append(("waitdma", src, idx))

    def op(self, q, fn, reads, writes):
        deps = self._collect(reads, writes, q)
        self._waits(q, deps)
        self.cnt[q] += 1
        me = (q, self.cnt[q])
        self.ops[q].append(("op", fn, self.epoch, self.cnt[q]))
        self._record(me, reads, writes)
        self.n_instr += 1

    def dma(self, q, out, in_, key, **kw):
        if key not in self.dma_sem:
            self.dma_sem[key] = self.es.enter_context(self.nc.semaphore("d_" + key))
            self.dma_cnt[key] = 0
        reads = [in_.key] if isinstance(in_, TA) else []
        writes = [out.key] if isinstance(out, TA) else []
        deps = self._collect(reads, writes)
        self._waits(q, deps)
        self.dma_cnt[key] += 1
        n = self.dma_cnt[key]
        o = out.ap if isinstance(out, TA) else out
        i = in_.ap if isinstance(in_, TA) else in_
        sem = self.dma_sem[key]
        self.ops[q].append(("dma", lambda e: e.dma_start(out=o, in_=i, **kw), sem))
        self._record(("dma:" + key, n), reads, writes)
        self.n_instr += 1

    def barrier(self):
        for q in ALLQ:
            deps = {}
            for e in COMPUTE:
                if self.cnt[e] > 0 and not (e == q == "pe"):
                    deps[e] = self.cnt[e]
            for key, n in self.dma_cnt.items():
                if n > 0:
                    deps["dma:" + key] = n
            self._waits(q, deps)
        self.epoch += 1
        assert self.epoch < self.n_epochs
        self.state = {}
        self.seen = {q: {k: v for k, v in self.seen[q].items() if k not in COMPUTE} for q in ALLQ}
        self.cnt = {e: 0 for e in COMPUTE}

    def wait_all_dma(self, q, keys):
        deps = {"dma:" + k: self.dma_cnt[k] for k in keys if self.dma_cnt.get(k, 0) > 0}
        self._waits(q, deps)

    def flush(self):
        nc = self.nc
        cum = {}
        for (e, ep), idxs in self.need_inc.items():
            s = sorted(idxs)
            cum[(e, ep)] = {idx: i + 1 for i, idx in enumerate(s)}

        def run(q, eng):
            for rec in self.ops[q]:
                kind = rec[0]
                if kind == "op":
                    _, fn, ep, idx = rec
                    ins = fn(eng)
                    c = cum.get((q, ep))
                    if c is not None and idx in c:
                        ins.then_inc(self.sems[(q, ep)], 1)
                elif kind == "wait":
                    _, src, ep, idx = rec
                    eng.wait_ge(self.sems[(src, ep)], cum[(src, ep)][idx])
                elif kind == "waitdma":
                    _, src, idx = rec
                    eng.wait_ge(self.dma_sem[src[4:]], 16 * idx)
                elif kind == "dma":
                    _, fn, sem = rec
                    fn(eng).then_inc(sem, 16)

        with nc.Block() as block:
            @block.sync
            def _(e):
                run("sp", e)

            @block.tensor
            def _(e):
                run("pe", e)

            @block.vector
            def _(e):
                run("dve", e)

            @block.scalar
            def _(e):
                run("act", e)

            @block.gpsimd
            def _(e):
                run("pool", e)
        self.ops = {q: [] for q in ALLQ}

    @staticmethod
    def _k(*tas):
        return [t.key for t in tas if isinstance(t, TA)]

    @staticmethod
    def _v(x):
        return x.ap if isinstance(x, TA) else x

    def mm(self, out, lhsT, rhs, start=True, stop=True):
        o, l, r = out.ap, lhsT.ap, rhs.ap
        self.op("pe", lambda e: e.matmul(o, l, r, start=start, stop=stop),
                self._k(lhsT, rhs), self._k(out))

    def tr(self, out, in_, ident):
        o, i, d = out.ap, in_.ap, ident.ap
        self.op("pe", lambda e: e.transpose(o, i, d), self._k(in_, ident), self._k(out))

    def act(self, out, in_, func, bias=None, scale=None):
        o, i = out.ap, in_.ap
        kw = {}
        if bias is not None:
            kw["bias"] = self._v(bias)
        if scale is not None:
            kw["scale"] = self._v(scale)
        self.op("act", lambda e: e.activation(o, i, func, **kw),
                self._k(in_, bias, scale), self._k(out))

    def ts(self, q, out, in0, s1, s2, op0, op1=None):
        o, i = out.ap, in0.ap
        a, b = self._v(s1), self._v(s2)
        if op1 is None:
            f = lambda e: e.tensor_scalar(o, i, a, None, op0)
        else:
            f = lambda e: e.tensor_scalar(o, i, a, b, op0, op1)
        self.op(q, f, self._k(in0, s1, s2), self._k(out))

    def tt(self, q, out, in0, in1, op):
        o, a, b = out.ap, in0.ap, in1.ap
        self.op(q, lambda e: e.tensor_tensor(o, a, b, op), self._k(in0, in1), self._k(out))

    def stt(self, out, in0, scalar, in1, op0, op1):
        o, a, b = out.ap, in0.ap, in1.ap
        s = self._v(scalar)
        self.op("dve", lambda e: e.scalar_tensor_tensor(o, a, s, b, op0, op1),
                self._k(in0, scalar, in1), self._k(out))

    def scan(self, out, d0, d1, init, op0, op1):
        o, a, b = out.ap, d0.ap, d1.ap
        s = self._v(init)
        self.op("dve", lambda e: e.tensor_tensor_scan(o, a, b, s, op0, op1),
                self._k(d0, d1, init), self._k(out))

    def copy(self, q, out, in_):
        o, i = out.ap, in_.ap
        if q == "act":
            f = lambda e: e.activation(o, i, AF.Copy)
        else:
            f = lambda e: e.tensor_copy(o, i)
        self.op(q, f, self._k(in_), self._k(out))

    def memset(self, q, ta, val):
        a = ta.ap
        self.op(q, lambda e: e.memset(a, val), [], self._k(ta))

    def recip(self, out, in_):
        o, i = out.ap, in_.ap
        self.op("dve", lambda e: e.reciprocal(o, i), self._k(in_), self._k(out))

    def bn_stats(self, out, in_):
        o, i = out.ap, in_.ap
        self.op("dve", lambda e: e.bn_stats(o, i), self._k(in_), self._k(out))

    def bn_aggr(self, out, in_):
        o, i = out.ap, in_.ap
        self.op("dve", lambda e: e.bn_aggr(o, i), self._k(in_), self._k(out))


D = 1024
SEQ = 4096
NB = 8
P_IN = 2688
DFF = 2816
T = 256
TF = 512
NT = SEQ // T
NBLK = T // 128
EPS = 1e-6
LN_EPS = 1e-5
GN_EPS = 64e-5
PV_NAMES = [("bmod", 48), ("nmix", 8), ("nffn", 8), ("lcw", 12), ("lcb", 3), ("lba", 3),
            ("lbx", 3), ("llam", 3), ("mu", 11), ("w0", 3), ("a0", 3), ("kk", 3), ("ka", 3),
            ("rk", 3), ("fcw", 66), ("fcb", 22), ("nfin", 8)]
PV_OFF = {}
_o = 0
for _n, _k in PV_NAMES:
    PV_OFF[_n] = (_o, _k)
    _o += _k
NPV = _o
NPB = 256 + 256 + 384 + 384


def _fm(v):
    return np.ascontiguousarray(np.asarray(v, np.float32).reshape(-1, 128).T)


def pack_params(inp):
    L = 2
    pv = np.zeros((L, 128, NPV), np.float32)
    pb = np.zeros((L, 128, NPB), np.float32)
    wsT = np.zeros((L, 128, 512), np.float32)
    bsrow = np.zeros((L, 1, 512), np.float32)
    lruW = np.zeros((L, 128, 2, 3, 128), np.float32)
    w2a2 = np.zeros((L, 128, 384), np.float32)

    def put(l, name, arr):
        o, k = PV_OFF[name]
        assert arr.shape == (128, k), (name, arr.shape)
        pv[l, :, o:o + k] = arr

    for l in range(L):
        put(l, "bmod", _fm(inp["b_mod"][l]))
        put(l, "nmix", _fm(inp["norm_mix"][l]))
        put(l, "nffn", _fm(inp["norm_ffn"][l]))
        put(l, "lcw", np.asarray(inp["lru_conv_w"][l]).reshape(4, 3, 128).transpose(2, 1, 0).reshape(128, 12))
        put(l, "lcb", _fm(inp["lru_conv_b"][l]))
        put(l, "lba", _fm(inp["lru_b_a"][l]))
        put(l, "lbx", _fm(inp["lru_b_x"][l]))
        put(l, "llam", _fm(inp["lru_lambda"][l]))
        put(l, "mu", _fm(inp["rwkv_mu"][l]))
        put(l, "w0", _fm(inp["rwkv_w0"][l]))
        put(l, "a0", _fm(inp["rwkv_a0"][l]))
        put(l, "kk", _fm(inp["rwkv_k_k"][l]))
        put(l, "ka", _fm(inp["rwkv_k_a"][l]))
        put(l, "rk", _fm(np.asarray(inp["rwkv_r_k"][l]).reshape(-1)))
        put(l, "fcw", np.asarray(inp["ffn_conv_w"][l]).reshape(3, 22, 128).transpose(2, 1, 0).reshape(128, 66))
        put(l, "fcb", _fm(inp["ffn_conv_b"][l]))
        put(l, "nfin", _fm(inp["norm_final"]))
        pb[l, :, 0:256] = np.asarray(inp["sgu_ln_g"][l])[None, :]
        pb[l, :, 256:512] = np.asarray(inp["sgu_ln_b"][l])[None, :]
        pb[l, :, 512:896] = np.asarray(inp["rwkv_ln_w"][l])[None, :]
        pb[l, :, 896:1280] = np.asarray(inp["rwkv_ln_b"][l])[None, :]
        wsT[l] = np.asarray(inp["sgu_w"][l]).transpose(2, 0, 1).reshape(128, 512)
        bsrow[l, 0] = np.asarray(inp["sgu_b"][l]).reshape(512)
        for wi, nm in enumerate(("lru_w_a", "lru_w_x")):
            W = np.asarray(inp[nm][l])
            for c in range(3):
                for hh in range(2):
                    lruW[l, hh * 64:(hh + 1) * 64, wi, c, hh * 64:(hh + 1) * 64] = W[2 * c + hh]
        w2a2[l, 0:64] = np.asarray(inp["rwkv_w2"][l])
        w2a2[l, 64:128] = np.asarray(inp["rwkv_a2"][l])
    return dict(pv=pv, pb=pb, wsT=wsT, bsrow=bsrow, lruW=lruW.reshape(L, 128, 768), w2a2=w2a2)


def build_program(n_tiles=NT, n_pass=4):
    nc = bass.Bass("TRN2", target_bir_lowering=False)

    def din(name, shape):
        return nc.dram_tensor(name, shape, F32, kind="ExternalInput").ap()

    x_d = din("x", [SEQ, D])
    cT_d = din("cT", [128, 8])
    wmod_d = din("w_mod", [2, D, 6 * D])
    win_d = din("w_in", [2, D, P_IN])
    wout_d = din("w_out", [2, D, D])
    wup_d = din("w_up", [2, D, 2 * DFF])
    wdn_d = din("w_dn", [2, DFF, D])
    pv_d = din("pv", [2, 128, NPV])
    pb_d = din("pb", [2, 128, NPB])
    wsT_d = din("wsT", [2, 128, 512])
    bs_d = din("bsrow", [2, 1, 512])
    lruW_d = din("lruW", [2, 128, 768])
    w2a2_d = din("w2a2", [2, 128, 384])
    g2_d = din("g2", [2, 128, 384])
    y_d = nc.dram_tensor("y", [SEQ, D], F32, kind="ExternalOutput").ap()
    xA_d = nc.dram_tensor("xA", [D, SEQ], F32, kind="ExternalOutput").ap()
    xB_d = nc.dram_tensor("xB", [D, SEQ], F32, kind="ExternalOutput").ap()

    def fm_tile(dr, name, i, TT=T):
        return TA(dr.rearrange("(c p) t -> p c t", p=128)[:, :, i * TT:(i + 1) * TT], (name, str(i)))

    with ExitStack() as es0:
        S = Sched(nc, es0)

        mkc = [0]

        def mk(es):
            pre = f"q{mkc[0]}_"
            mkc[0] += 1

            def sb(name, shape, dt=F32):
                nm = pre + name
                return Tile(es.enter_context(nc.sbuf_tensor(nm, shape, dt)), nm)
            return sb

        sb0 = mk(es0)
        ps = [Tile(es0.enter_context(nc.psum_tensor(f"ps{i}", [128, 512], F32)), f"ps{i}") for i in range(8)]
        psi = [0, 8]

        def nps():
            t = ps[psi[0] % psi[1]]
            psi[0] += 1
            return t

        ident = sb0("ident", [128, 128])
        onesm = sb0("onesm", [128, 128])
        bones = sb0("bones", [128, 128])
        sel = sb0("sel", [128, 2])
        mask4 = sb0("mask4", [128, 512])
        maskL = sb0("maskL", [128, 128])
        keep = sb0("keep", [128, T])
        negh = sb0("negh", [128, 8])
        identb = sb0("identb", [128, 128], BF16)
        onesb = sb0("onesb", [128, 128], BF16)
        bonesb = sb0("bonesb", [128, 128], BF16)
        selb = sb0("selb", [128, 2], BF16)
        onesr = sb0("onesr", [1, 64])
        epsc = sb0("epsc", [128, 1])
        pvt = [sb0("pv0", [128, NPV]), sb0("pv1", [128, NPV])]
        modt = [sb0("mod0", [128, 48]), sb0("mod1", [128, 48])]
        c_sb = sb0("c_sb", [128, 8])
        c_act = sb0("c_act", [128, 8])

        def asel(ta, pattern, cmp, cm):
            a = ta.ap
            S.op("pool", lambda e: e.affine_select(a, a, pattern, cmp, 0.0, base=0, channel_multiplier=cm),
                 [ta.key], [ta.key])

        S.memset("pool", ident[:], 1.0)
        asel(ident[:], [[-1, 128]], ALU.is_equal, 1)
        S.memset("dve", onesm[:], 1.0 / D)
        S.memset("dve", bones[:], 0.0)
        S.memset("dve", bones[0:64, 0:64], 1.0)
        S.memset("dve", bones[64:128, 64:128], 1.0)
        S.memset("dve", sel[:], 0.0)
        S.memset("dve", sel[0:64, 0:1], 1.0)
        S.memset("dve", sel[64:128, 1:2], 1.0)
        S.memset("pool", mask4[:], 1.0)
        S.memset("pool", maskL[:], 1.0)
        for b in range(4):
            asel(mask4[:, b * 128:(b + 1) * 128], [[1, 128]], ALU.is_gt if b % 2 == 0 else ALU.is_ge, -1)
        asel(maskL[:], [[-1, 128]], ALU.is_gt, 1)
        S.memset("dve", keep[:], 1.0)
        for b in range(NBLK):
            S.memset("dve", keep[:, b * 128:b * 128 + 1], 0.0)
        S.memset("pool", negh[:], -0.5)
        S.copy("pool", identb[:], ident[:])
        S.copy("pool", onesb[:], onesm[:])
        S.copy("pool", bonesb[:], bones[:])
        S.copy("pool", selb[:], sel[:])
        S.memset("dve", onesr[:], 1.0)
        S.memset("dve", epsc[:], EPS)
        for l in range(2):
            S.dma("sp", pvt[l][:], pv_d[l], "small")
        S.dma("sp", c_sb[:], cT_d, "small")
        S.act(c_act[:], c_sb[:], AF.Silu)

        def pvc(l, name, j=None, n=1):
            o, k = PV_OFF[name]
            if j is None:
                return pvt[l][:, o:o + k]
            return pvt[l][:, o + j:o + j + n]

        with ExitStack() as es1:
            sb1 = mk(es1)
            wm = [sb1("wm0", [128, 8, 768]), sb1("wm1", [128, 8, 768])]
            for l in range(2):
                pm = nps()
                for g in range(8):
                    w = wm[g % 2]
                    S.dma("sp", w[:], wmod_d[l].rearrange("(k p) n -> p k n", p=128)[:, :, g * 768:(g + 1) * 768],
                          f"wm{g % 2}")
                    for j in range(6):
                        m = g * 6 + j
                        for kc in range(8):
                            S.mm(pm[:, m:m + 1], w[:, kc, j * 128:(j + 1) * 128], c_act[:, kc:kc + 1],
                                 start=(kc == 0), stop=(kc == 7))
                S.tt("dve", modt[l][:], pm[:, 0:48], pvc(l, "bmod"), ALU.add)
            S.barrier()
            S.flush()

        def rms_to(sbw, xt, gcol, shcol, hout, eng_bias, T=T):
            pss = nps()
            for c in range(8):
                sq = sbw["sqb"][c % 2]
                S.act(sq[:], xt.s(c, (slice(None), c, slice(None))), AF.Square)
                S.mm(pss[:, 0:T], onesb[:], sq[:], start=(c == 0), stop=(c == 7))
            rstd = sbw["rstd"]
            S.act(rstd[:], pss[:, 0:T], AF.Sqrt, bias=epsc[:, 0:1], scale=1.0)
            S.recip(rstd[:], rstd[:])
            for c in range(8):
                tmp = sbw["sq"][c % 2]
                S.stt(tmp[:], xt.s(c, (slice(None), c, slice(None))), gcol(c), rstd[:], ALU.mult, ALU.mult)
                if eng_bias == "act":
                    S.act(hout.s(c, (slice(None), c, slice(None))), tmp[:], AF.Identity, bias=shcol(c), scale=1.0)
                else:
                    S.ts("pool", hout.s(c, (slice(None), c, slice(None))), tmp[:], 1.0, shcol(c), ALU.mult, ALU.add)

        def load_x(l_first, xt, src_name, src_d, i, sbw):
            if l_first:
                xtok = sbw["xtok"]
                for b in range(NBLK):
                    S.dma("sp", xtok.s(b, (slice(None), b, slice(None))),
                          x_d[i * T + b * 128:i * T + (b + 1) * 128, :], "xin")
                for c in range(8):
                    pt = nps()
                    for b in range(NBLK):
                        S.tr(pt[:, b * 128:(b + 1) * 128], xtok.s(b, (slice(None), b, slice(c * 128, (c + 1) * 128))),
                             ident[:])
                    S.copy("act" if c % 2 else "dve", xt.s(c, (slice(None), c, slice(None))), pt[:, 0:T])
            else:
                S.dma("sp", xt[:], fm_tile(src_d, src_name, i), "xin")

        def gelu2(sbw, out, P):
            sqt = sbw["g_sq"]
            th = sbw["g_th"]
            S.act(sqt[:], P, AF.Square)
            S.ts("pool", sqt[:], sqt[:], 0.044715, 1.0, ALU.mult, ALU.add)
            S.tt("pool", sqt[:], sqt[:], P, ALU.mult)
            S.act(th[:], sqt[:], AF.Tanh, scale=0.7978845608028654)
            S.stt(out, th[:], 1.0, P, ALU.add, ALU.mult)

        ctx = dict(nc=nc, S=S, mk=mk, nps=nps, ps=ps, psi=psi, epsc=epsc, ident=ident, onesm=onesm, bones=bones, sel=sel, mask4=mask4,
                   maskL=maskL, keep=keep, negh=negh, identb=identb, onesb=onesb, bonesb=bonesb, selb=selb, onesr=onesr, pvc=pvc, modt=modt,
                   rms_to=rms_to, load_x=load_x, gelu2=gelu2, fm_tile=fm_tile, n_tiles=n_tiles,
                   win_d=win_d, wout_d=wout_d, wup_d=wup_d, wdn_d=wdn_d, pb_d=pb_d, wsT_d=wsT_d, bs_d=bs_d,
                   lruW_d=lruW_d, w2a2_d=w2a2_d, g2_d=g2_d, y_d=y_d)

        plan = [("mix", 0, None, None, "xA", xA_d), ("ffn", 0, "xA", xA_d, "xB", xB_d),
                ("mix", 1, "xB", xB_d, "xA", xA_d), ("ffn", 1, "xA", xA_d, None, None)]
        for pi, (kind, l, sn, sd, dn, dd) in enumerate(plan[:n_pass]):
            psi[1] = 5 if kind == "mix" else 8
            if kind == "mix":
                mixer_pass(ctx, l, sn, sd, dn, dd)
            else:
                ffn_pass(ctx, l, sn, sd, dn, dd, final=(l == 1))
            S.barrier()
            S.flush()
    return nc


def _sl(*a):
    return tuple(a)


ALL = slice(None)


def ffn_pass(ctx, l, sn, sd, dn, dd, final):
    S = ctx["S"]
    nps = ctx["nps"]
    pvc = ctx["pvc"]
    ident = ctx["ident"]
    negh = ctx["negh"]
    onesm = ctx["onesm"]
    modt = ctx["modt"][l]
    T = TF
    NBLK = T // 128
    with ExitStack() as es:
        sb = ctx["mk"](es)
        wup = sb("wup", [128, 8, 2 * DFF], BF16)
        wdn = sb("wdn", [128, 22, D], BF16)
        wu_src = ctx["wup_d"][l].rearrange("(k p) n -> p k n", p=128)
        wd_src = ctx["wdn_d"][l].rearrange("(k p) n -> p k n", p=128)
        for k in range(8):
            for q in range(4):
                cs = slice(q * 1408, (q + 1) * 1408)
                S.dma("pool", wup.s(f"{k}_{q}", (ALL, k, cs)), wu_src[:, k, cs], "w")
            if k % 2 == 1:
                S.wait_all_dma("pool", ["w"])
        for k in range(22):
            S.dma("pool", wdn.s(f"{k}", (ALL, k, ALL)), wd_src[:, k, :], "w")
            if k % 8 == 7:
                S.wait_all_dma("pool", ["w"])
        xt = sb("xt", [128, 8, T])
        hb = sb("hb", [128, 8, T], BF16)
        hid = sb("hid", [128, 22, T], BF16)
        raw = [sb(f"raw{j}", [128, 2 + T]) for j in range(2)]
        halo = sb("halo", [128, 22, 2])
        accs = [sb(f"acc{j}", [128, T]) for j in range(2)]
        sls = [sb(f"sl{j}", [128, T]) for j in range(2)]
        sbw = dict(sq=accs, rstd=sb("rstd", [128, T]),
                   sqb=[sb("sqb0", [128, T], BF16), sb("sqb1", [128, T], BF16)])
        g2v = sb("g2v", [128, 8])
        if final:
            fbuf = xt
            ytok = sb("ytok", [128, D])
        S.memset("pool", halo[:], 0.0)
        S.ts("dve", g2v[:], modt[:, 32:40], 1.0, None, ALU.add)
        S.tt("dve", g2v[:], g2v[:], pvc(l, "nffn"), ALU.mult)

        def fcw(j, t):
            return pvc(l, "fcw", j * 3 + t)

        for i in range(ctx["n_tiles"] * 256 // T):
            S.dma("sp", xt[:], ctx["fm_tile"](sd, sn, i, T), "xin")
            ctx["rms_to"](sbw, xt, lambda c: g2v[:, c:c + 1], lambda c: modt[:, 24 + c:25 + c], hb, "act", T)
            for j in range(22):
                pg = nps()
                pv_ = nps()
                for k in range(8):
                    S.mm(pg[:, 0:T], wup[:, k, j * 128:(j + 1) * 128], hb.s(k, (ALL, k, ALL)),
                         start=(k == 0), stop=(k == 7))
                for k in range(8):
                    S.mm(pv_[:, 0:T], wup[:, k, DFF + j * 128:DFF + (j + 1) * 128], hb.s(k, (ALL, k, ALL)),
                         start=(k == 0), stop=(k == 7))
                r = raw[j % 2]
                S.copy("pool", r.s("h", (ALL, slice(0, 2))), halo.s(j, (ALL, j, ALL)))
                S.copy("act", r.s("d", (ALL, slice(2, 2 + T))), pg[:, 0:T])
                S.copy("pool", halo.s(j, (ALL, j, ALL)), r.s("d", (ALL, slice(T, T + 2))))
                acc = accs[j % 2]
                sl = sls[j % 2]
                S.ts("dve", acc[:], r[:, 0:T], fcw(j, 0), pvc(l, "fcb", j), ALU.mult, ALU.add)
                S.stt(acc[:], r[:, 1:T + 1], fcw(j, 1), acc[:], ALU.mult, ALU.add)
                S.stt(acc[:], r[:, 2:T + 2], fcw(j, 2), acc[:], ALU.mult, ALU.add)
                S.act(sl[:], acc[:], AF.Silu)
                S.tt("dve", hid.s(j, (ALL, j, ALL)), sl[:], pv_[:, 0:T], ALU.mult)
            for m in range(8):
                po = nps()
                for k in range(22):
                    S.mm(po[:, 0:T], wdn[:, k, m * 128:(m + 1) * 128], hid.s(k, (ALL, k, ALL)),
                         start=(k == 0), stop=(k == 21))
                S.stt(xt.s(m, (ALL, m, ALL)), po[:, 0:T], modt[:, 40 + m:41 + m], xt.s(m, (ALL, m, ALL)),
                      ALU.mult, ALU.add)
            if not final:
                S.dma("sp", ctx["fm_tile"](dd, dn, i, T), xt[:], "xout")
            else:
                pss = nps()
                for c in range(8):
                    sq = sbw["sqb"][c % 2]
                    S.act(sq[:], xt.s(c, (ALL, c, ALL)), AF.Square)
                    S.mm(pss[:, 0:T], ctx["onesb"][:], sq[:], start=(c == 0), stop=(c == 7))
                rstd = sbw["rstd"]
                S.act(rstd[:], pss[:, 0:T], AF.Sqrt, bias=ctx["epsc"][:, 0:1], scale=1.0)
                S.recip(rstd[:], rstd[:])
                for c in range(8):
                    S.stt(fbuf.s(c, (ALL, c, ALL)), xt.s(c, (ALL, c, ALL)), pvc(l, "nfin", c), rstd[:],
                          ALU.mult, ALU.mult)
                for b in range(NBLK):
                    for cg in range(2):
                        pt = nps()
                        for cc in range(4):
                            c = cg * 4 + cc
                            S.tr(pt[:, cc * 128:(cc + 1) * 128], fbuf.s(c, (ALL, c, slice(b * 128, (b + 1) * 128))),
                                 ident[:])
                        S.copy("act" if cg else "dve", ytok.s(cg, (ALL, slice(cg * 512, (cg + 1) * 512))), pt[:, 0:512])
                    S.dma("sp", ctx["y_d"][i * T + b * 128:i * T + (b + 1) * 128, :], ytok[:], "yout")
        S.wait_all_dma("sp", ["xout", "yout"])


def mixer_pass(ctx, l, sn, sd, dn, dd):
    S = ctx["S"]
    nps = ctx["nps"]
    pvc = ctx["pvc"]
    ident = ctx["ident"]
    negh = ctx["negh"]
    identb = ctx["identb"]
    bonesb = ctx["bonesb"]
    selb = ctx["selb"]
    mask4 = ctx["mask4"]
    maskL = ctx["maskL"]
    keep = ctx["keep"]
    bones = ctx["bones"]
    sel = ctx["sel"]
    onesr = ctx["onesr"]
    modt = ctx["modt"][l]
    gelu2 = ctx["gelu2"]
    first = sn is None
    W = 3 + T
    with ExitStack() as es:
        sb = ctx["mk"](es)
        wint = sb("wint", [128, 8, P_IN], BF16)
        woutt = sb("woutt", [128, 8, D], BF16)
        wi_src = ctx["win_d"][l].rearrange("(k p) n -> p k n", p=128)
        wo_src = ctx["wout_d"][l].rearrange("(k p) n -> p k n", p=128)
        for k in range(8):
            for q in range(2):
                cs = slice(q * 1344, (q + 1) * 1344)
                S.dma("pool", wint.s(f"{k}_{q}", (ALL, k, cs)), wi_src[:, k, cs], "w")
            if k % 4 == 3:
                S.wait_all_dma("pool", ["w"])
        for k in range(8):
            S.dma("pool", woutt.s(f"{k}", (ALL, k, ALL)), wo_src[:, k, :], "w")
        lruWt = sb("lruWt", [128, 768], BF16)
        w2a2t = sb("w2a2t", [128, 384], BF16)
        g2t = sb("g2t", [128, 384], BF16)
        wsTt = sb("wsTt", [128, 512], BF16)
        bsr = sb("bsr", [1, 512])
        pbt = sb("pbt", [128, NPB])
        S.dma("pool", lruWt[:], ctx["lruW_d"][l], "w")
        S.dma("pool", w2a2t[:], ctx["w2a2_d"][l], "w")
        S.dma("pool", g2t[:], ctx["g2_d"][l], "w")
        S.dma("sp", bsr[:], ctx["bs_d"][l], "small")
        S.dma("sp", pbt[:], ctx["pb_d"][l], "small")

        xt = sb("xt", [128, 8, T])
        hb = sb("hb", [128, 8, T], BF16)
        ycat = sb("ycat", [128, 8, T], BF16)
        p_sb = sb("p_sb", [128, 21, W])
        gx = sb("gx", [128, 4, T])
        sbw = dict(sq=[sb("sq0", [128, T]), sb("sq1", [128, T])], rstd=sb("rstd", [128, T]),
                   sqb=[sb("sqb0", [128, T], BF16), sb("sqb1", [128, T], BF16)],
                   g_sq=sb("g_sq", [128, T]), g_th=sb("g_th", [128, T]))
        if first:
            sbw["xtok"] = sb("xtok", [128, NBLK, D])
        wsTf = TA(xt.h[:, 0:2, :].rearrange("p c t -> p (c t)"), (xt.name, ""))
        S.dma("sp", wsTf, ctx["wsT_d"][l], "small")
        for h in range(4):
            S.tt("dve", wsTt[:, h * 128:(h + 1) * 128],
                 TA(wsTf.ap[:, h * 128:(h + 1) * 128], (xt.name, "")), mask4[:, 128:256], ALU.mult)
        S.memset("pool", p_sb[:], 0.0)
        bs_bc = [sb(f"bs_bc{c}", [128, T]) for c in range(2)]
        for c in range(2):
            pbs = nps()
            for hh in range(2):
                h = 2 * c + hh
                for n in range(NBLK):
                    S.mm(pbs[64 * hh:64 * hh + 64, n * 128:(n + 1) * 128], onesr[0:1, 0:64],
                         bsr[0:1, h * 128:(h + 1) * 128], start=True, stop=True)
            S.copy("dve", bs_bc[c][:], pbs[:, 0:T])

        dp = sb("dp", [128, 64])
        S.ts("dve", dp[:, 0:8], modt[:, 8:16], 1.0, None, ALU.add)
        S.tt("dve", dp[:, 0:8], dp[:, 0:8], pvc(l, "nmix"), ALU.mult)
        S.ts("dve", dp[:, 8:11], pvc(l, "lba"), 0.5, None, ALU.mult)
        S.ts("dve", dp[:, 11:14], pvc(l, "lbx"), 0.5, None, ALU.mult)
        S.act(dp[:, 14:17], pvc(l, "llam"), AF.Exp, scale=-1.0)
        S.act(dp[:, 14:17], dp[:, 14:17], AF.Ln, bias=1.0)
        S.ts("dve", dp[:, 17:20], dp[:, 14:17], -4.0, None, ALU.mult)
        S.ts("dve", dp[:, 14:17], dp[:, 14:17], -8.0, None, ALU.mult)
        S.ts("dve", dp[:, 20:31], pvc(l, "mu"), -1.0, 1.0, ALU.mult, ALU.add)
        S.ts("dve", dp[:, 31:34], pvc(l, "w0"), 0.5, None, ALU.mult)
        S.ts("dve", dp[:, 34:37], pvc(l, "a0"), 0.5, None, ALU.mult)
        S.ts("dve", dp[:, 37:40], pvc(l, "ka"), 0.5, None, ALU.mult)
        S.ts("dve", dp[:, 40:43], pvc(l, "ka"), -0.5, 1.0, ALU.mult, ALU.add)

        def dpc(o, j=0):
            return dp[:, o + j:o + j + 1]

        hstate = sb("hstate", [128, 3])
        S.memset("dve", hstate[:], 0.0)
        sT = [[sb(f"sT{p}_{b}", [128, 64]) for b in range(2)] for p in range(3)]
        for p in range(3):
            S.memset("dve", sT[p][0][:], 0.0)

        def wt(name, dt=F32, shape=None):
            return sb(name, shape or [128, T], dt)

        xcb = wt("xcb", BF16)
        st6 = sb("st6", [128, 6]); mv2 = sb("mv2", [128, 2]); vr = sb("vr", [128, 1])
        vn = sb("vn", [128, 256]); vtok = sb("vtok", [128, 256], BF16)
        xwa = wt("xwa", BF16); sgx = wt("sgx", BF16); mixtmp = wt("mixtmp")
        PMV = [wt(f"pmv{p}") for p in range(3)]
        E = [wt(f"E{p}") for p in range(3)]
        BT = [wt(f"BT{p}") for p in range(3)]
        KT = [wt(f"KT{p}") for p in range(3)]
        RK = [wt(f"RK{p}", BF16) for p in range(3)]
        BTb = [wt(f"BTb{p}", BF16) for p in range(3)]
        KTb = [wt(f"KTb{p}", BF16) for p in range(3)]
        ARb = [sb(f"ARb{p}", [128, NBLK, 2, 128], BF16) for p in range(3)]

        AR = [sb(f"AR{p}", [128, NBLK, 2, 128]) for p in range(3)]
        pmr = wt("pmr"); pmk = wt("pmk"); tw = wt("tw"); ta = wt("ta"); lw = wt("lw"); cum = wt("cum")
        Ei = wt("Ei"); Ep = wt("Ep"); rn = wt("rn"); kkn = wt("kkn"); ff = wt("ff"); kp = wt("kp"); t1 = wt("t1")
        xc = kp; tr_ = tw; ti_ = ta
        gsb = [sb(f"gsb{n}", [128, 384]) for n in range(NBLK)]
        tok3 = [sb(f"tok3_{p}", [128, 3, 128], BF16) for p in range(3)]
        AM = [sb(f"AM{p}", [128, 512], BF16) for p in range(3)]
        QPM = [[sb(f"QPM{p}_{b}", [128, 384]) for b in range(2)] for p in range(3)]
        X = [sb(f"X{p}", [128, 64]) for p in range(3)]
        U = [sb(f"U{p}", [128, 64], BF16) for p in range(3)]
        On = [sb(f"On{p}", [128, 128]) for p in range(3)]
        bon = [sb(f"bon{p}", [128, 2]) for p in range(3)]
        stc = [sb(f"stc{p}", [128, 2, 6]) for p in range(3)]
        mvc = [sb(f"mvc{p}", [128, 2, 2]) for p in range(3)]
        rsc = [sb(f"rsc{p}", [128, 2]) for p in range(3)]
        yct = [sb(f"yct{p}", [128, 128]) for p in range(3)]

        def P(c, lo=3, hi=None):
            hi = W if hi is None else hi
            return p_sb.s(c, (ALL, c, slice(lo, hi)))

        def mix(c, out):
            j = c - 10
            S.act(mixtmp[:], P(c), AF.Identity, scale=dpc(20, j))
            S.stt(out, P(c, 2, 2 + T), pvc(l, "mu", j), mixtmp[:], ALU.mult, ALU.add)

        rn3 = [wt(f"rn3_{p}") for p in range(3)]
        la = [lw, t1, rn]
        lm = [cum, pmr, wt("lm2")]
        lg = [Ei, Ep, wt("lg2")]
        hs_ = kkn
        gy = ff

        def lru_part1():
            for c in range(3):
                xr = 4 + c
                S.ts("dve", xc[:], P(xr, 0, T), pvc(l, "lcw", c * 4 + 0), pvc(l, "lcb", c), ALU.mult, ALU.add)
                for j in range(1, 4):
                    S.stt(xc[:], P(xr, j, j + T), pvc(l, "lcw", c * 4 + j), xc[:], ALU.mult, ALU.add)
                S.copy("pool", xcb[:], xc[:])
                pr = nps()
                pi_ = nps()
                S.mm(pr[:, 0:T], lruWt[:, c * 128:(c + 1) * 128], xcb[:])
                S.mm(pi_[:, 0:T], lruWt[:, 384 + c * 128:384 + (c + 1) * 128], xcb[:])
                S.act(tr_[:], pr[:, 0:T], AF.Tanh, bias=dpc(8, c), scale=0.5)
                S.act(ti_[:], pi_[:, 0:T], AF.Tanh, bias=dpc(11, c), scale=0.5)
                S.act(la[c][:], tr_[:], AF.Exp, bias=dpc(17, c), scale=dpc(17, c))
                S.act(lm[c][:], tr_[:], AF.Exp, bias=dpc(14, c), scale=dpc(14, c))
                S.ts("dve", lm[c][:], lm[c][:], -1.0, 1.0, ALU.mult, ALU.add)
                S.ts("dve", lm[c][:], lm[c][:], 0.0, None, ALU.max)
                S.stt(lg[c][:], ti_[:], 1.0, xc[:], ALU.add, ALU.mult)

        for i in range(ctx["n_tiles"]):
            ctx["load_x"](first, xt, sn, sd, i, sbw)
            ctx["rms_to"](sbw, xt, lambda c: dp[:, c:c + 1], lambda c: modt[:, c:c + 1], hb, "pool")
            order = list(range(10, 21)) + list(range(4, 10)) + list(range(0, 4))
            for oc in order:
                pp = nps()
                for k in range(8):
                    S.mm(pp[:, 0:T], wint[:, k, oc * 128:(oc + 1) * 128], hb.s(k, (ALL, k, ALL)),
                         start=(k == 0), stop=(k == 7))
                S.copy("act", P(oc), pp[:, 0:T])

            mix(19, xwa[:])
            S.act(xwa[0:64, :], xwa[0:64, :], AF.Tanh)
            mix(20, mixtmp[:])
            S.act(sgx[:], mixtmp[:], AF.Tanh, scale=0.5)
            S.ts("pool", sgx[:], sgx[:], 0.5, 0.5, ALU.mult, ALU.add)
            for n in range(NBLK):
                pg = nps()
                S.mm(pg[:, 0:384], sgx[:, n * 128:(n + 1) * 128], g2t[:])
                S.copy("act", gsb[n][:], pg[:, 0:384])

            lru_part1()
            for p in range(3):
                mix(13 + p, pmk[:])
                S.act(xcb[:], pmk[:], AF.Square, scale=pvc(l, "kk", p))
                S.mm(ctx["ps"][5 + p][:, 0:T], bonesb[:], xcb[:])
            for p in range(3):
                S.act(rn3[p][:], ctx["ps"][5 + p][:, 0:T], AF.Sqrt)
            for c in range(3):
                S.act(lm[c][:], lm[c][:], AF.Sqrt)
            for p in range(3):
                S.ts("dve", rn3[p][:], rn3[p][:], 1e-12, None, ALU.max)
                S.recip(rn3[p][:], rn3[p][:])
            for c in range(3):
                S.stt(lg[c][:], lm[c][:], 0.5, lg[c][:], ALU.mult, ALU.mult)
                S.scan(hs_[:], la[c][:], lg[c][:], hstate[:, c:c + 1], ALU.mult, ALU.add)
                S.copy("pool", hstate[:, c:c + 1], hs_[:, T - 1:T])
                gelu2(sbw, gy[:], P(7 + c))
                S.stt(ycat.s(2 + c, (ALL, 2 + c, ALL)), gy[:], 0.5, hs_[:], ALU.mult, ALU.mult)

            for p in range(3):
                rn = rn3[p]
                mix(10 + p, pmr[:])
                mix(13 + p, pmk[:])
                mix(16 + p, PMV[p][:])
                pw = nps()
                pa_ = nps()
                S.mm(pw[:, 0:T], w2a2t[0:64, p * 128:(p + 1) * 128], xwa[0:64, :])
                S.mm(pa_[:, 0:T], w2a2t[64:128, p * 128:(p + 1) * 128], xwa[64:128, :])
                S.act(tw[:], pw[:, 0:T], AF.Tanh, bias=dpc(31, p), scale=0.5)
                S.act(ta[:], pa_[:, 0:T], AF.Tanh, bias=dpc(34, p), scale=0.5)
                S.ts("dve", lw[:], tw[:], 1.0, -0.5 * 0.6065306597126334, ALU.add, ALU.mult)
                S.scan(cum[:], keep[:], lw[:], 0.0, ALU.mult, ALU.add)
                S.act(E[p][:], cum[:], AF.Exp)
                S.act(Ei[:], cum[:], AF.Exp, scale=-1.0)
                S.tt("pool", Ep[:], cum[:], lw[:], ALU.subtract)
                S.act(Ep[:], Ep[:], AF.Exp)
                S.stt(kkn[:], pmk[:], pvc(l, "kk", p), rn[:], ALU.mult, ALU.mult)
                S.ts("dve", ff[:], ta[:], dpc(37, p), dpc(40, p), ALU.mult, ALU.add)
                S.tt("pool", kp[:], pmk[:], ff[:], ALU.mult)
                ar = AR[p]
                a_view = TA(ar.h[:, :, 0, :], (ar.name, ""))
                r_view = TA(ar.h[:, :, 1, :], (ar.name, ""))

                def v3(tile_):
                    return TA(tile_.h[:].rearrange("p (n t) -> p n t", t=128), (tile_.name, ""))

                S.stt(a_view, v3(kkn), -1.0, v3(Ep), ALU.mult, ALU.mult)
                S.tt("dve", r_view, v3(pmr), v3(E[p]), ALU.mult)
                S.stt(t1[:], ta[:], 1.0, kkn[:], ALU.add, ALU.mult)
                S.stt(BT[p][:], t1[:], 0.5, Ei[:], ALU.mult, ALU.mult)
                S.tt("pool", KT[p][:], kp[:], Ei[:], ALU.mult)
                S.stt(RK[p][:], pmr[:], pvc(l, "rk", p), kp[:], ALU.mult, ALU.mult)
                S.copy("pool", BTb[p][:], BT[p][:])
                S.copy("pool", KTb[p][:], KT[p][:])
                S.copy("pool", TA(ARb[p].h[:].rearrange("p n a t -> p (n a t)"), (ARb[p].name, "")),
                       TA(ar.h[:].rearrange("p n a t -> p (n a t)"), (ar.name, "")))

            for c in range(4):
                gelu2(sbw, gx.s(c, (ALL, c, ALL)), P(c))
            psm = [ctx["ps"][5], ctx["ps"][6]]
            for n in range(NBLK):
                cs = slice(n * 128, (n + 1) * 128)
                pt = nps()
                S.tr(pt[:, 0:128], gx.s(2, (ALL, 2, cs)), ident[:])
                S.tr(pt[:, 128:256], gx.s(3, (ALL, 3, cs)), ident[:])
                S.bn_stats(st6[:], pt[:, 0:256])
                S.bn_aggr(mv2[:], st6[:])
                S.ts("dve", vr[:], mv2[:, 1:2], 4.0 * LN_EPS, None, ALU.add)
                S.tt("pool", vr[:], vr[:], negh[:, 0:1], ALU.pow)
                S.ts("dve", vn[:], pt[:, 0:256], mv2[:, 0:1], vr[:], ALU.subtract, ALU.mult)
                S.tt("pool", vn[:], vn[:], pbt[:, 0:256], ALU.mult)
                S.tt("pool", vtok[:], vn[:], pbt[:, 256:512], ALU.add)
                for h in range(4):
                    o = psm[h // 2][64 * (h % 2):64 * (h % 2) + 64, cs]
                    S.mm(o, vtok[:, 64 * h:64 * h + 64], wsTt[:, h * 128:(h + 1) * 128], start=True, stop=True)
            for c in range(2):
                S.tt("dve", vn[:], psm[c][:, 0:T], bs_bc[c][:], ALU.add)
                S.stt(ycat.s(c, (ALL, c, ALL)), vn[:], 0.5, gx.s(c, (ALL, c, ALL)), ALU.mult, ALU.mult)

            def core_unit(p, n):
                g = i * NBLK + n
                par = g % 2
                cs = slice(n * 128, (n + 1) * 128)
                ar = AR[p]
                arb = ARb[p]
                pO = ctx["ps"][5 + p]
                pt = nps()
                S.tr(pt[:, 0:128], BT[p][:, cs], ident[:])
                S.tr(pt[:, 128:256], KT[p][:, cs], ident[:])
                S.tr(pt[:, 256:384], PMV[p][:, cs], ident[:])
                S.copy("act", TA(tok3[p].h[:].rearrange("p a t -> p (a t)"), (tok3[p].name, "")), pt[:, 0:384])
                yield
                for h in range(2):
                    hs = slice(64 * h, 64 * h + 64)
                    arhb = TA(arb.h[hs, n, :, :].rearrange("p a t -> p (a t)"), (arb.name, ""))
                    ahb = TA(arb.h[hs, n, 0, :], (arb.name, ""))
                    ah = TA(ar.h[hs, n, 0, :], (ar.name, ""))
                    rh = TA(ar.h[hs, n, 1, :], (ar.name, ""))
                    s_old = sT[p][par].s(h, (hs, ALL))
                    s_new = sT[p][1 - par].s(h, (hs, ALL))
                    vtk = tok3[p][:, 2, hs]
                    pa = nps()
                    S.mm(pa[:, 0:256], BTb[p][hs, cs], arhb)
                    S.mm(pa[:, 256:512], KTb[p][hs, cs], arhb)
                    S.tt("dve", QPM[p][0][:, 128:256], pa[:, 0:128], mask4[:, 0:128], ALU.mult)
                    S.tt("dve", AM[p][:, 128:512], pa[:, 128:512], mask4[:, 128:512], ALU.mult)
                    pb_ = nps()
                    S.mm(pb_[:, 0:128], ahb, BTb[p][hs, cs])
                    S.tt("dve", QPM[p][0][:, 0:128], pb_[:, 0:128], maskL[:], ALU.mult)
                    S.copy("pool", QPM[p][0][:, 256:384], ident[:])
                    yield
                    cur = 0
                    for k in range(1, 7):
                        pc = nps()
                        nxt = 1 - cur
                        qc = QPM[p][cur]
                        qn = QPM[p][nxt]
                        if k < 6:
                            S.mm(pc[:, 128:384], qc[:, 0:128], qc[:, 128:384])
                        else:
                            S.mm(pc[:, 256:384], qc[:, 0:128], qc[:, 256:384])
                        S.mm(pc[:, 0:128], qc[:, 128:256], qc[:, 0:128])
                        if k < 6:
                            S.copy("act", qn[:, 0:256], pc[:, 0:256])
                        else:
                            S.copy("act", qn[:, 0:128], pc[:, 0:128])
                        S.tt("dve", qn[:, 256:384], qc[:, 256:384], pc[:, 256:384], ALU.add)
                        cur = nxt
                        yield
                    pc = nps()
                    qc = QPM[p][cur]
                    S.mm(pc[:, 0:128], qc[:, 0:128], qc[:, 256:384])
                    S.tt("dve", QPM[p][1 - cur][:, 256:384], qc[:, 256:384], pc[:, 0:128], ALU.add)
                    Tt = QPM[p][1 - cur][:, 256:384]
                    yield
                    px = nps()
                    S.mm(px[:, 0:64], ah, s_old, start=True, stop=False)
                    S.mm(px[:, 0:64], AM[p][:, 256:384], vtk, start=False, stop=True)
                    S.copy("act", X[p][:], px[:, 0:64])
                    yield
                    pu = nps()
                    S.mm(pu[:, 0:64], Tt, X[p][:])
                    S.copy("act", U[p][:], pu[:, 0:64])
                    yield
                    S.mm(pO[:, hs], rh, s_old, start=True, stop=False)
                    S.mm(pO[:, hs], AM[p][:, 128:256], U[p][:], start=False, stop=False)
                    S.mm(pO[:, hs], AM[p][:, 384:512], vtk, start=False, stop=True)
                    pS = nps()
                    S.mm(pS[hs, 0:64], tok3[p][:, 0, hs], U[p][:], start=True, stop=False)
                    S.mm(pS[hs, 0:64], tok3[p][:, 1, hs], vtk, start=False, stop=True)
                    el = E[p][hs, n * 128 + 127:n * 128 + 128]
                    S.ts("pool", s_new, s_old, el, None, ALU.mult)
                    S.stt(s_new, pS[hs, 0:64], el, s_new, ALU.mult, ALU.add)
                    yield
                for h in range(2):
                    hs = slice(64 * h, 64 * h + 64)
                    S.bn_stats(stc[p][:, h, :], pO[:, hs])
                    S.bn_aggr(mvc[p][:, h, :], stc[p][:, h, :])
                S.ts("dve", rsc[p][:], mvc[p][:, :, 1], GN_EPS, None, ALU.add)
                S.tt("pool", rsc[p][:], rsc[p][:], negh[:, 0:2], ALU.pow)
                for h in range(2):
                    hs = slice(64 * h, 64 * h + 64)
                    S.ts("dve", On[p][:, hs], pO[:, hs], mvc[p][:, h, 0:1], rsc[p][:, h:h + 1], ALU.subtract, ALU.mult)
                S.tt("pool", On[p][:], On[p][:], pbt[:, 512 + 128 * p:512 + 128 * (p + 1)], ALU.mult)
                S.tt("pool", On[p][:], On[p][:], pbt[:, 896 + 128 * p:896 + 128 * (p + 1)], ALU.add)
                pbn = nps()
                S.mm(pbn[:, 0:2], RK[p][:, cs], selb[:])
                S.copy("act", bon[p][:], pbn[:, 0:2])
                for h in range(2):
                    hs = slice(64 * h, 64 * h + 64)
                    S.stt(On[p][:, hs], tok3[p][:, 2, hs], bon[p][:, h:h + 1], On[p][:, hs], ALU.mult, ALU.add)
                S.tt("pool", yct[p][:], On[p][:], gsb[n][:, 128 * p:128 * (p + 1)], ALU.mult)
                ptr = nps()
                S.tr(ptr[:, 0:128], yct[p][:], ident[:])
                S.copy("act", ycat.s(5 + p, (ALL, 5 + p, cs)), ptr[:, 0:128])
                yield

            for n in range(NBLK):
                gens = [core_unit(p, n) for p in range(3)]
                while gens:
                    for gnr in list(gens):
                        try:
                            next(gnr)
                        except StopIteration:
                            gens.remove(gnr)

            for m in range(8):
                po = nps()
                for k in range(8):
                    S.mm(po[:, 0:T], woutt[:, k, m * 128:(m + 1) * 128], ycat.s(k, (ALL, k, ALL)),
                         start=(k == 0), stop=(k == 7))
                S.stt(xt.s(m, (ALL, m, ALL)), po[:, 0:T], modt[:, 16 + m:17 + m], xt.s(m, (ALL, m, ALL)),
                      ALU.mult, ALU.add)
            S.dma("sp", ctx["fm_tile"](dd, dn, i), xt[:], "xout")
            hk = [(p_sb.name, str(c)) for c in range(4, 7)]
            a0 = p_sb.h[:, 4:7, 0:3]
            a1 = p_sb.h[:, 4:7, T:T + 3]
            S.op("pool", lambda e, a0=a0, a1=a1: e.tensor_copy(a0, a1), hk, hk)
            hk2 = [(p_sb.name, str(c)) for c in range(10, 21)]
            b0 = p_sb.h[:, 10:21, 0:3]
            b1 = p_sb.h[:, 10:21, T:T + 3]
            S.op("pool", lambda e, b0=b0, b1=b1: e.tensor_copy(b0, b1), hk2, hk2)
        S.wait_all_dma("sp", ["xout"])


_INPUT_ORDER = None


def kernel(**inp):
    inp = {k: np.asarray(v) for k, v in inp.items()}
    pk = pack_params(inp)
    nc = build_program()
    x = np.ascontiguousarray(inp["x"], dtype=np.float32)
    c = np.asarray(inp["c"], np.float32)
    shared = dict(
        w_mod=np.ascontiguousarray(inp["w_mod"], dtype=np.float32),
        w_in=np.ascontiguousarray(inp["w_in"], dtype=np.float32),
        w_out=np.ascontiguousarray(inp["w_out"], dtype=np.float32),
        w_up=np.ascontiguousarray(inp["ffn_w_up"], dtype=np.float32),
        w_dn=np.ascontiguousarray(inp["ffn_w_down"], dtype=np.float32),
        g2=np.ascontiguousarray(inp["rwkv_g2"], dtype=np.float32),
        **pk,
    )
    in_maps = []
    for b in range(NB):
        m = dict(shared)
        m["x"] = x[b]
        m["cT"] = _fm(c[b])
        in_maps.append(m)
    res = run_bass_kernel_spmd(nc, in_maps, core_ids=list(range(NB)))
    out = np.stack([np.asarray(r["y"], dtype=np.float32) for r in res.results], axis=0)
    return out
```

```python
import numpy as np
from contextlib import ExitStack
import concourse.bass as bass
import concourse.mybir as mybir
from concourse.bass_utils import run_bass_kernel_spmd

F32 = mybir.dt.float32
BF16 = mybir.dt.bfloat16
AF = mybir.ActivationFunctionType
ALU = mybir.AluOpType

COMPUTE = ("pe", "dve", "act", "pool")
ALLQ = ("pe", "dve", "act", "pool", "sp")


class TA:
    __slots__ = ("ap", "key")

    def __init__(self, ap, key):
        self.ap = ap
        self.key = key


class Tile:
    def __init__(self, h, name):
        self.h = h
        self.name = name

    def __getitem__(self, idx):
        return TA(self.h[idx], (self.name, ""))

    def s(self, sub, idx):
        return TA(self.h[idx], (self.name, str(sub)))


class Sched:
    def __init__(self, nc, es, n_epochs=8):
        self.nc = nc
        self.es = es
        self.ops = {q: [] for q in ALLQ}
        self.cnt = {e: 0 for e in COMPUTE}
        self.state = {}
        self.seen = {q: {} for q in ALLQ}
        self.epoch = 0
        self.n_epochs = n_epochs
        self.sems = {}
        for e in COMPUTE:
            for ep in range(n_epochs):
                self.sems[(e, ep)] = es.enter_context(nc.semaphore(f"s_{e}_{ep}"))
        self.dma_sem = {}
        self.dma_cnt = {}
        self.need_inc = {}
        self.n_instr = 0

    def _entries(self, key):
        name, sub = key
        d = self.state.setdefault(name, {})
        if sub == "":
            if "" not in d:
                d[""] = [None, []]
            return list(d.values())
        out = []
        if "" in d:
            out.append(d[""])
        if sub not in d:
            d[sub] = [None, []]
        out.append(d[sub])
        return out

    def _own(self, key):
        name, sub = key
        d = self.state.setdefault(name, {})
        if sub not in d:
            d[sub] = [None, []]
        return d[sub]

    def _collect(self, reads, writes, q=None):
        deps = {}

        def add(t):
            if t is None:
                return
            src, idx = t
            if deps.get(src, 0) < idx:
                deps[src] = idx

        for k in reads:
            excl = k[0].startswith("ps")
            for ent in self._entries(k):
                add(ent[0])
                if excl:
                    for r in ent[1]:
                        if r[0] != q:
                            add(r)
        for k in writes:
            for ent in self._entries(k):
                add(ent[0])
                for r in ent[1]:
                    add(r)
        return deps

    def _record(self, me, reads, writes):
        for k in reads:
            self._own(k)[1].append(me)
        for k in writes:
            name, sub = k
            if sub == "":
                self.state[name] = {"": [me, []]}
            else:
                ent = self._own(k)
                ent[0] = me
                ent[1] = []

    def _waits(self, q, deps):
        for src, idx in deps.items():
            if src == q and q == "pe":
                continue
            if self.seen[q].get(src, 0) >= idx:
                continue
            self.seen[q][src] = idx
            if src in COMPUTE:
                self.need_inc.setdefault((src, self.epoch), set()).add(idx)
                self.ops[q].append(("wait", src, self.epoch, idx))
            else:
                idx = self.dma_cnt[src[4:]]
                self.seen[q][src] = idx
                self.ops[q].append(("waitdma", src, idx))

    def op(self, q, fn, reads, writes):
        deps = self._collect(reads, writes, q)
        self._waits(q, deps)
        self.cnt[q] += 1
        me = (q, self.cnt[q])
        self.ops[q].append(("op", fn, self.epoch, self.cnt[q]))
        self._record(me, reads, writes)
        self.n_instr += 1

    def dma(self, q, out, in_, key, **kw):
        if key not in self.dma_sem:
            self.dma_sem[key] = self.es.enter_context(self.nc.semaphore("d_" + key))
            self.dma_cnt[key] = 0
        reads = [in_.key] if isinstance(in_, TA) else []
        writes = [out.key] if isinstance(out, TA) else []
        deps = self._collect(reads, writes)
        self._waits(q, deps)
        self.dma_cnt[key] += 1
        n = self.dma_cnt[key]
        o = out.ap if isinstance(out, TA) else out
        i = in_.ap if isinstance(in_, TA) else in_
        sem = self.dma_sem[key]
        self.ops[q].append(("dma", lambda e: e.dma_start(out=o, in_=i, **kw), sem))
        self._record(("dma:" + key, n), reads, writes)
        self.n_instr += 1

    def barrier(self):
        for q in ALLQ:
            deps = {}
            for e in COMPUTE:
                if self.cnt[e] > 0 and not (e == q == "pe"):
                    deps[e] = self.cnt[e]
            for key, n in self.dma_cnt.items():
                if n > 0:
                    deps["dma:" + key] = n
            self._waits(q, deps)
        self.epoch += 1
        assert self.epoch < self.n_epochs
        self.state = {}
        self.seen = {q: {k: v for k, v in self.seen[q].items() if k not in COMPUTE} for q in ALLQ}
        self.cnt = {e: 0 for e in COMPUTE}

    def wait_all_dma(self, q, keys):
        deps = {"dma:" + k: self.dma_cnt[k] for k in keys if self.dma_cnt.get(k, 0) > 0}
        self._waits(q, deps)

    def flush(self):
        nc = self.nc
        cum = {}
        for (e, ep), idxs in self.need_inc.items():
            s = sorted(idxs)
            cum[(e, ep)] = {idx: i + 1 for i, idx in enumerate(s)}

        def run(q, eng):
            for rec in self.ops[q]:
                kind = rec[0]
                if kind == "op":
                    _, fn, ep, idx = rec
                    ins = fn(eng)
                    c = cum.get((q, ep))
                    if c is not None and idx in c:
                        ins.then_inc(self.sems[(q, ep)], 1)
                elif kind == "wait":
                    _, src, ep, idx = rec
                    eng.wait_ge(self.sems[(src, ep)], cum[(src, ep)][idx])
                elif kind == "waitdma":
                    _, src, idx = rec
                    eng.wait_ge(self.dma_sem[src[4:]], 16 * idx)
                elif kind == "dma":
                    _, fn, sem = rec
                    fn(eng).then_inc(sem, 16)

        with nc.Block() as block:
            @block.sync
            def _(e):
                run("sp", e)

            @block.tensor
            def _(e):
                run("pe", e)

            @block.vector
            def _(e):
                run("dve", e)

            @block.scalar
            def _(e):
                run("act", e)

            @block.gpsimd
            def _(e):
                run("pool", e)
        self.ops = {q: [] for q in ALLQ}

    @staticmethod
    def _k(*tas):
        return [t.key for t in tas if isinstance(t, TA)]

    @staticmethod
    def _v(x):
        return x.ap if isinstance(x, TA) else x

    def mm(self, out, lhsT, rhs, start=True, stop=True):
        o, l, r = out.ap, lhsT.ap, rhs.ap
        self.op("pe", lambda e: e.matmul(o, l, r, start=start, stop=stop),
                self._k(lhsT, rhs), self._k(out))

    def tr(self, out, in_, ident):
        o, i, d = out.ap, in_.ap, ident.ap
        self.op("pe", lambda e: e.transpose(o, i, d), self._k(in_, ident), self._k(out))

    def act(self, out, in_, func, bias=None, scale=None):
        o, i = out.ap, in_.ap
        kw = {}
        if bias is not None:
            kw["bias"] = self._v(bias)
        if scale is not None:
            kw["scale"] = self._v(scale)
        self.op("act", lambda e: e.activation(o, i, func, **kw),
                self._k(in_, bias, scale), self._k(out))

    def ts(self, q, out, in0, s1, s2, op0, op1=None):
        o, i = out.ap, in0.ap
        a, b = self._v(s1), self._v(s2)
        if op1 is None:
            f = lambda e: e.tensor_scalar(o, i, a, None, op0)
        else:
            f = lambda e: e.tensor_scalar(o, i, a, b, op0, op1)
        self.op(q, f, self._k(in0, s1, s2), self._k(out))

    def tt(self, q, out, in0, in1, op):
        o, a, b = out.ap, in0.ap, in1.ap
        self.op(q, lambda e: e.tensor_tensor(o, a, b, op), self._k(in0, in1), self._k(out))

    def stt(self, out, in0, scalar, in1, op0, op1):
        o, a, b = out.ap, in0.ap, in1.ap
        s = self._v(scalar)
        self.op("dve", lambda e: e.scalar_tensor_tensor(o, a, s, b, op0, op1),
                self._k(in0, scalar, in1), self._k(out))

    def scan(self, out, d0, d1, init, op0, op1):
        o, a, b = out.ap, d0.ap, d1.ap
        s = self._v(init)
        self.op("dve", lambda e: e.tensor_tensor_scan(o, a, b, s, op0, op1),
                self._k(d0, d1, init), self._k(out))

    def copy(self, q, out, in_):
        o, i = out.ap, in_.ap
        if q == "act":
            f = lambda e: e.activation(o, i, AF.Copy)
        else:
            f = lambda e: e.tensor_copy(o, i)
        self.op(q, f, self._k(in_), self._k(out))

    def memset(self, q, ta, val):
        a = ta.ap
        self.op(q, lambda e: e.memset(a, val), [], self._k(ta))

    def recip(self, out, in_):
        o, i = out.ap, in_.ap
        self.op("dve", lambda e: e.reciprocal(o, i), self._k(in_), self._k(out))

    def bn_stats(self, out, in_):
        o, i = out.ap, in_.ap
        self.op("dve", lambda e: e.bn_stats(o, i), self._k(in_), self._k(out))

    def bn_aggr(self, out, in_):
        o, i = out.ap, in_.ap
        self.op("dve", lambda e: e.bn_aggr(o, i), self._k(in_), self._k(out))


D = 1024
SEQ = 4096
NB = 8
P_IN = 2688
DFF = 2816
T = 256
TF = 512
NT = SEQ // T
NBLK = T // 128
EPS = 1e-6
LN_EPS = 1e-5
GN_EPS = 64e-5
PV_NAMES = [("bmod", 48), ("nmix", 8), ("nffn", 8), ("lcw", 12), ("lcb", 3), ("lba", 3),
            ("lbx", 3), ("llam", 3), ("mu", 11), ("w0", 3), ("a0", 3), ("kk", 3), ("ka", 3),
            ("rk", 3), ("fcw", 66), ("fcb", 22), ("nfin", 8)]
PV_OFF = {}
_o = 0
for _n, _k in PV_NAMES:
    PV_OFF[_n] = (_o, _k)
    _o += _k
NPV = _o
NPB = 256 + 256 + 384 + 384


def _fm(v):
    return np.ascontiguousarray(np.asarray(v, np.float32).reshape(-1, 128).T)


def pack_params(inp):
    L = 2
    pv = np.zeros((L, 128, NPV), np.float32)
    pb = np.zeros((L, 128, NPB), np.float32)
    wsT = np.zeros((L, 128, 512), np.float32)
    bsrow = np.zeros((L, 1, 512), np.float32)
    lruW = np.zeros((L, 128, 2, 3, 128), np.float32)
    w2a2 = np.zeros((L, 128, 384), np.float32)

    def put(l, name, arr):
        o, k = PV_OFF[name]
        assert arr.shape == (128, k), (name, arr.shape)
        pv[l, :, o:o + k] = arr

    for l in range(L):
        put(l, "bmod", _fm(inp["b_mod"][l]))
        put(l, "nmix", _fm(inp["norm_mix"][l]))
        put(l, "nffn", _fm(inp["norm_ffn"][l]))
        put(l, "lcw", np.asarray(inp["lru_conv_w"][l]).reshape(4, 3, 128).transpose(2, 1, 0).reshape(128, 12))
        put(l, "lcb", _fm(inp["lru_conv_b"][l]))
        put(l, "lba", _fm(inp["lru_b_a"][l]))
        put(l, "lbx", _fm(inp["lru_b_x"][l]))
        put(l, "llam", _fm(inp["lru_lambda"][l]))
        put(l, "mu", _fm(inp["rwkv_mu"][l]))
        put(l, "w0", _fm(inp["rwkv_w0"][l]))
        put(l, "a0", _fm(inp["rwkv_a0"][l]))
        put(l, "kk", _fm(inp["rwkv_k_k"][l]))
        put(l, "ka", _fm(inp["rwkv_k_a"][l]))
        put(l, "rk", _fm(np.asarray(inp["rwkv_r_k"][l]).reshape(-1)))
        put(l, "fcw", np.asarray(inp["ffn_conv_w"][l]).reshape(3, 22, 128).transpose(2, 1, 0).reshape(128, 66))
        put(l, "fcb", _fm(inp["ffn_conv_b"][l]))
        put(l, "nfin", _fm(inp["norm_final"]))
        pb[l, :, 0:256] = np.asarray(inp["sgu_ln_g"][l])[None, :]
        pb[l, :, 256:512] = np.asarray(inp["sgu_ln_b"][l])[None, :]
        pb[l, :, 512:896] = np.asarray(inp["rwkv_ln_w"][l])[None, :]
        pb[l, :, 896:1280] = np.asarray(inp["rwkv_ln_b"][l])[None, :]
        wsT[l] = np.asarray(inp["sgu_w"][l]).transpose(2, 0, 1).reshape(128, 512)
        bsrow[l, 0] = np.asarray(inp["sgu_b"][l]).reshape(512)
        for wi, nm in enumerate(("lru_w_a", "lru_w_x")):
            W = np.asarray(inp[nm][l])
            for c in range(3):
                for hh in range(2):
                    lruW[l, hh * 64:(hh + 1) * 64, wi, c, hh * 64:(hh + 1) * 64] = W[2 * c + hh]
        w2a2[l, 0:64] = np.asarray(inp["rwkv_w2"][l])
        w2a2[l, 64:128] = np.asarray(inp["rwkv_a2"][l])
    return dict(pv=pv, pb=pb, wsT=wsT, bsrow=bsrow, lruW=lruW.reshape(L, 128, 768), w2a2=w2a2)


def build_program(n_tiles=NT, n_pass=4):
    nc = bass.Bass("TRN2", target_bir_lowering=False)

    def din(name, shape):
        return nc.dram_tensor(name, shape, F32, kind="ExternalInput").ap()

    x_d = din("x", [SEQ, D])
    cT_d = din("cT", [128, 8])
    wmod_d = din("w_mod", [2, D, 6 * D])
    win_d = din("w_in", [2, D, P_IN])
    wout_d = din("w_out", [2, D, D])
    wup_d = din("w_up", [2, D, 2 * DFF])
    wdn_d = din("w_dn", [2, DFF, D])
    pv_d = din("pv", [2, 128, NPV])
    pb_d = din("pb", [2, 128, NPB])
    wsT_d = din("wsT", [2, 128, 512])
    bs_d = din("bsrow", [2, 1, 512])
    lruW_d = din("lruW", [2, 128, 768])
    w2a2_d = din("w2a2", [2, 128, 384])
    g2_d = din("g2", [2, 128, 384])
    y_d = nc.dram_tensor("y", [SEQ, D], F32, kind="ExternalOutput").ap()
    xA_d = nc.dram_tensor("xA", [D, SEQ], F32, kind="ExternalOutput").ap()
    xB_d = nc.dram_tensor("xB", [D, SEQ], F32, kind="ExternalOutput").ap()

    def fm_tile(dr, name, i, TT=T):
        return TA(dr.rearrange("(c p) t -> p c t", p=128)[:, :, i * TT:(i + 1) * TT], (name, str(i)))

    with ExitStack() as es0:
        S = Sched(nc, es0)

        mkc = [0]

        def mk(es):
            pre = f"q{mkc[0]}_"
            mkc[0] += 1

            def sb(name, shape, dt=F32):
                nm = pre + name
                return Tile(es.enter_context(nc.sbuf_tensor(nm, shape, dt)), nm)
            return sb

        sb0 = mk(es0)
        ps = [Tile(es0.enter_context(nc.psum_tensor(f"ps{i}", [128, 512], F32)), f"ps{i}") for i in range(8)]
        psi = [0, 8]

        def nps():
            t = ps[psi[0] % psi[1]]
            psi[0] += 1
            return t

        ident = sb0("ident", [128, 128])
        onesm = sb0("onesm", [128, 128])
        bones = sb0("bones", [128, 128])
        sel = sb0("sel", [128, 2])
        mask4 = sb0("mask4", [128, 512])
        maskL = sb0("maskL", [128, 128])
        keep = sb0("keep", [128, T])
        negh = sb0("negh", [128, 8])
        identb = sb0("identb", [128, 128], BF16)
        onesb = sb0("onesb", [128, 128], BF16)
        bonesb = sb0("bonesb", [128, 128], BF16)
        selb = sb0("selb", [128, 2], BF16)
        onesr = sb0("onesr", [1, 64])
        epsc = sb0("epsc", [128, 1])
        pvt = [sb0("pv0", [128, NPV]), sb0("pv1", [128, NPV])]
        modt = [sb0("mod0", [128, 48]), sb0("mod1", [128, 48])]
        c_sb = sb0("c_sb", [128, 8])
        c_act = sb0("c_act", [128, 8])

        def asel(ta, pattern, cmp, cm):
            a = ta.ap
            S.op("pool", lambda e: e.affine_select(a, a, pattern, cmp, 0.0, base=0, channel_multiplier=cm),
                 [ta.key], [ta.key])

        S.memset("pool", ident[:], 1.0)
        asel(ident[:], [[-1, 128]], ALU.is_equal, 1)
        S.memset("dve", onesm[:], 1.0 / D)
        S.memset("dve", bones[:], 0.0)
        S.memset("dve", bones[0:64, 0:64], 1.0)
        S.memset("dve", bones[64:128, 64:128], 1.0)
        S.memset("dve", sel[:], 0.0)
        S.memset("dve", sel[0:64, 0:1], 1.0)
        S.memset("dve", sel[64:128, 1:2], 1.0)
        S.memset("pool", mask4[:], 1.0)
        S.memset("pool", maskL[:], 1.0)
        for b in range(4):
            asel(mask4[:, b * 128:(b + 1) * 128], [[1, 128]], ALU.is_gt if b % 2 == 0 else ALU.is_ge, -1)
        asel(maskL[:], [[-1, 128]], ALU.is_gt, 1)
        S.memset("dve", keep[:], 1.0)
        for b in range(NBLK):
            S.memset("dve", keep[:, b * 128:b * 128 + 1], 0.0)
        S.memset("pool", negh[:], -0.5)
        S.copy("pool", identb[:], ident[:])
        S.copy("pool", onesb[:], onesm[:])
        S.copy("pool", bonesb[:], bones[:])
        S.copy("pool", selb[:], sel[:])
        S.memset("dve", onesr[:], 1.0)
        S.memset("dve", epsc[:], EPS)
        for l in range(2):
            S.dma("sp", pvt[l][:], pv_d[l], "small")
        S.dma("sp", c_sb[:], cT_d, "small")
        S.act(c_act[:], c_sb[:], AF.Silu)

        def pvc(l, name, j=None, n=1):
            o, k = PV_OFF[name]
            if j is None:
                return pvt[l][:, o:o + k]
            return pvt[l][:, o + j:o + j + n]

        with ExitStack() as es1:
            sb1 = mk(es1)
            wm = [sb1("wm0", [128, 8, 768]), sb1("wm1", [128, 8, 768])]
            for l in range(2):
                pm = nps()
                for g in range(8):
                    w = wm[g % 2]
                    S.dma("sp", w[:], wmod_d[l].rearrange("(k p) n -> p k n", p=128)[:, :, g * 768:(g + 1) * 768],
                          f"wm{g % 2}")
                    for j in range(6):
                        m = g * 6 + j
                        for kc in range(8):
                            S.mm(pm[:, m:m + 1], w[:, kc, j * 128:(j + 1) * 128], c_act[:, kc:kc + 1],
                                 start=(kc == 0), stop=(kc == 7))
                S.tt("dve", modt[l][:], pm[:, 0:48], pvc(l, "bmod"), ALU.add)
            S.barrier()
            S.flush()

        def rms_to(sbw, xt, gcol, shcol, hout, eng_bias, T=T):
            pss = nps()
            for c in range(8):
                sq = sbw["sqb"][c % 2]
                S.act(sq[:], xt.s(c, (slice(None), c, slice(None))), AF.Square)
                S.mm(pss[:, 0:T], onesb[:], sq[:], start=(c == 0), stop=(c == 7))
            rstd = sbw["rstd"]
            S.act(rstd[:], pss[:, 0:T], AF.Sqrt, bias=epsc[:, 0:1], scale=1.0)
            S.recip(rstd[:], rstd[:])
            for c in range(8):
                tmp = sbw["sq"][c % 2]
                S.stt(tmp[:], xt.s(c, (slice(None), c, slice(None))), gcol(c), rstd[:], ALU.mult, ALU.mult)
                if eng_bias == "act":
                    S.act(hout.s(c, (slice(None), c, slice(None))), tmp[:], AF.Identity, bias=shcol(c), scale=1.0)
                else:
                    S.ts("pool", hout.s(c, (slice(None), c, slice(None))), tmp[:], 1.0, shcol(c), ALU.mult, ALU.add)

        def load_x(l_first, xt, src_name, src_d, i, sbw):
            if l_first:
                xtok = sbw["xtok"]
                for b in range(NBLK):
                    S.dma("sp", xtok.s(b, (slice(None), b, slice(None))),
                          x_d[i * T + b * 128:i * T + (b + 1) * 128, :], "xin")
                for c in range(8):
                    pt = nps()
                    for b in range(NBLK):
                        S.tr(pt[:, b * 128:(b + 1) * 128], xtok.s(b, (slice(None), b, slice(c * 128, (c + 1) * 128))),
                             ident[:])
                    S.copy("act" if c % 2 else "dve", xt.s(c, (slice(None), c, slice(None))), pt[:, 0:T])
            else:
                S.dma("sp", xt[:], fm_tile(src_d, src_name, i), "xin")

        def gelu2(sbw, out, P):
            sqt = sbw["g_sq"]
            th = sbw["g_th"]
            S.act(sqt[:], P, AF.Square)
            S.ts("pool", sqt[:], sqt[:], 0.044715, 1.0, ALU.mult, ALU.add)
            S.tt("pool", sqt[:], sqt[:], P, ALU.mult)
            S.act(th[:], sqt[:], AF.Tanh, scale=0.7978845608028654)
            S.stt(out, th[:], 1.0, P, ALU.add, ALU.mult)

        ctx = dict(nc=nc, S=S, mk=mk, nps=nps, ps=ps, psi=psi, epsc=epsc, ident=ident, onesm=onesm, bones=bones, sel=sel, mask4=mask4,
                   maskL=maskL, keep=keep, negh=negh, identb=identb, onesb=onesb, bonesb=bonesb, selb=selb, onesr=onesr, pvc=pvc, modt=modt,
                   rms_to=rms_to, load_x=load_x, gelu2=gelu2, fm_tile=fm_tile, n_tiles=n_tiles,
                   win_d=win_d, wout_d=wout_d, wup_d=wup_d, wdn_d=wdn_d, pb_d=pb_d, wsT_d=wsT_d, bs_d=bs_d,
                   lruW_d=lruW_d, w2a2_d=w2a2_d, g2_d=g2_d, y_d=y_d)

        plan = [("mix", 0, None, None, "xA", xA_d), ("ffn", 0, "xA", xA_d, "xB", xB_d),
                ("mix", 1, "xB", xB_d, "xA", xA_d), ("ffn", 1, "xA", xA_d, None, None)]
        for pi, (kind, l, sn, sd, dn, dd) in enumerate(plan[:n_pass]):
            psi[1] = 5 if kind == "mix" else 8
            if kind == "mix":
                mixer_pass(ctx, l, sn, sd, dn, dd)
            else:
                ffn_pass(ctx, l, sn, sd, dn, dd, final=(l == 1))
            S.barrier()
            S.flush()
    return nc


def _sl(*a):
    return tuple(a)


ALL = slice(None)


def ffn_pass(ctx, l, sn, sd, dn, dd, final):
    S = ctx["S"]
    nps = ctx["nps"]
    pvc = ctx["pvc"]
    ident = ctx["ident"]
    negh = ctx["negh"]
    onesm = ctx["onesm"]
    modt = ctx["modt"][l]
    T = TF
    NBLK = T // 128
    with ExitStack() as es:
        sb = ctx["mk"](es)
        wup = sb("wup", [128, 8, 2 * DFF], BF16)
        wdn = sb("wdn", [128, 22, D], BF16)
        wu_src = ctx["wup_d"][l].rearrange("(k p) n -> p k n", p=128)
        wd_src = ctx["wdn_d"][l].rearrange("(k p) n -> p k n", p=128)
        for k in range(8):
            for q in range(4):
                cs = slice(q * 1408, (q + 1) * 1408)
                S.dma("pool", wup.s(f"{k}_{q}", (ALL, k, cs)), wu_src[:, k, cs], "w")
            if k % 2 == 1:
                S.wait_all_dma("pool", ["w"])
        for k in range(22):
            S.dma("pool", wdn.s(f"{k}", (ALL, k, ALL)), wd_src[:, k, :], "w")
            if k % 8 == 7:
                S.wait_all_dma("pool", ["w"])
        xt = sb("xt", [128, 8, T])
        hb = sb("hb", [128, 8, T], BF16)
        hid = sb("hid", [128, 22, T], BF16)
        raw = [sb(f"raw{j}", [128, 2 + T]) for j in range(2)]
        halo = sb("halo", [128, 22, 2])
        accs = [sb(f"acc{j}", [128, T]) for j in range(2)]
        sls = [sb(f"sl{j}", [128, T]) for j in range(2)]
        sbw = dict(sq=accs, rstd=sb("rstd", [128, T]),
                   sqb=[sb("sqb0", [128, T], BF16), sb("sqb1", [128, T], BF16)])
        g2v = sb("g2v", [128, 8])
        if final:
            fbuf = xt
            ytok = sb("ytok", [128, D])
        S.memset("pool", halo[:], 0.0)
        S.ts("dve", g2v[:], modt[:, 32:40], 1.0, None, ALU.add)
        S.tt("dve", g2v[:], g2v[:], pvc(l, "nffn"), ALU.mult)

        def fcw(j, t):
            return pvc(l, "fcw", j * 3 + t)

        n_ft = ctx["n_tiles"] * 256 // T
        sdv = sd.rearrange("(c p) t -> p c t", p=128)
        ddv = dd.rearrange("(c p) t -> p c t", p=128) if dd is not None else None
        for i in range(n_ft):
            if final or i == 0:
                S.dma("sp", xt[:], ctx["fm_tile"](sd, sn, i, T), "xin")
            ctx["rms_to"](sbw, xt, lambda c: g2v[:, c:c + 1], lambda c: modt[:, 24 + c:25 + c], hb, "act", T)
            for j in range(22):
                pg = nps()
                pv_ = nps()
                for k in range(8):
                    S.mm(pg[:, 0:T], wup[:, k, j * 128:(j + 1) * 128], hb.s(k, (ALL, k, ALL)),
                         start=(k == 0), stop=(k == 7))
                for k in range(8):
                    S.mm(pv_[:, 0:T], wup[:, k, DFF + j * 128:DFF + (j + 1) * 128], hb.s(k, (ALL, k, ALL)),
                         start=(k == 0), stop=(k == 7))
                r = raw[j % 2]
                S.copy("pool", r.s("h", (ALL, slice(0, 2))), halo.s(j, (ALL, j, ALL)))
                S.copy("act", r.s("d", (ALL, slice(2, 2 + T))), pg[:, 0:T])
                S.copy("pool", halo.s(j, (ALL, j, ALL)), r.s("d", (ALL, slice(T, T + 2))))
                acc = accs[j % 2]
                sl = sls[j % 2]
                S.ts("dve", acc[:], r[:, 0:T], fcw(j, 0), pvc(l, "fcb", j), ALU.mult, ALU.add)
                S.stt(acc[:], r[:, 1:T + 1], fcw(j, 1), acc[:], ALU.mult, ALU.add)
                S.stt(acc[:], r[:, 2:T + 2], fcw(j, 2), acc[:], ALU.mult, ALU.add)
                S.act(sl[:], acc[:], AF.Silu)
                S.tt("dve", hid.s(j, (ALL, j, ALL)), sl[:], pv_[:, 0:T], ALU.mult)
            for m in range(8):
                po = nps()
                for k in range(22):
                    S.mm(po[:, 0:T], wdn[:, k, m * 128:(m + 1) * 128], hid.s(k, (ALL, k, ALL)),
                         start=(k == 0), stop=(k == 21))
                S.stt(xt.s(m, (ALL, m, ALL)), po[:, 0:T], modt[:, 40 + m:41 + m], xt.s(m, (ALL, m, ALL)),
                      ALU.mult, ALU.add)
                if not final:
                    S.dma("sp", TA(ddv[:, m, i * T:(i + 1) * T], (dn, f"{i}_{m}")), xt.s(m, (ALL, m, ALL)), "xout")
                    if i + 1 < n_ft:
                        S.dma("sp", xt.s(m, (ALL, m, ALL)), TA(sdv[:, m, (i + 1) * T:(i + 2) * T], (sn, f"{i + 1}_{m}")),
                              "xin")
            if not final:
                pass
            else:
                pss = nps()
                for c in range(8):
                    sq = sbw["sqb"][c % 2]
                    S.act(sq[:], xt.s(c, (ALL, c, ALL)), AF.Square)
                    S.mm(pss[:, 0:T], ctx["onesb"][:], sq[:], start=(c == 0), stop=(c == 7))
                rstd = sbw["rstd"]
                S.act(rstd[:], pss[:, 0:T], AF.Sqrt, bias=ctx["epsc"][:, 0:1], scale=1.0)
                S.recip(rstd[:], rstd[:])
                for c in range(8):
                    S.stt(fbuf.s(c, (ALL, c, ALL)), xt.s(c, (ALL, c, ALL)), pvc(l, "nfin", c), rstd[:],
                          ALU.mult, ALU.mult)
                for b in range(NBLK):
                    for cg in range(2):
                        pt = nps()
                        for cc in range(4):
                            c = cg * 4 + cc
                            S.tr(pt[:, cc * 128:(cc + 1) * 128], fbuf.s(c, (ALL, c, slice(b * 128, (b + 1) * 128))),
                                 ident[:])
                        S.copy("act" if cg else "dve", ytok.s(cg, (ALL, slice(cg * 512, (cg + 1) * 512))), pt[:, 0:512])
                    S.dma("sp", ctx["y_d"][i * T + b * 128:i * T + (b + 1) * 128, :], ytok[:], "yout")
        S.wait_all_dma("sp", ["xout", "yout"])


def mixer_pass(ctx, l, sn, sd, dn, dd):
    S = ctx["S"]
    nps = ctx["nps"]
    pvc = ctx["pvc"]
    ident = ctx["ident"]
    negh = ctx["negh"]
    identb = ctx["identb"]
    bonesb = ctx["bonesb"]
    selb = ctx["selb"]
    mask4 = ctx["mask4"]
    maskL = ctx["maskL"]
    keep = ctx["keep"]
    bones = ctx["bones"]
    sel = ctx["sel"]
    onesr = ctx["onesr"]
    modt = ctx["modt"][l]
    gelu2 = ctx["gelu2"]
    first = sn is None
    W = 3 + T
    with ExitStack() as es:
        sb = ctx["mk"](es)
        wint = sb("wint", [128, 8, P_IN], BF16)
        woutt = sb("woutt", [128, 8, D], BF16)
        wi_src = ctx["win_d"][l].rearrange("(k p) n -> p k n", p=128)
        wo_src = ctx["wout_d"][l].rearrange("(k p) n -> p k n", p=128)
        for k in range(8):
            for q in range(2):
                cs = slice(q * 1344, (q + 1) * 1344)
                S.dma("pool", wint.s(f"{k}_{q}", (ALL, k, cs)), wi_src[:, k, cs], "w")
            if k % 4 == 3:
                S.wait_all_dma("pool", ["w"])
        for k in range(8):
            S.dma("pool", woutt.s(f"{k}", (ALL, k, ALL)), wo_src[:, k, :], "w")
        lruWt = sb("lruWt", [128, 768], BF16)
        w2a2t = sb("w2a2t", [128, 384], BF16)
        g2t = sb("g2t", [128, 384], BF16)
        wsTt = sb("wsTt", [128, 512], BF16)
        bsr = sb("bsr", [1, 512])
        pbt = sb("pbt", [128, NPB])
        S.dma("pool", lruWt[:], ctx["lruW_d"][l], "w")
        S.dma("pool", w2a2t[:], ctx["w2a2_d"][l], "w")
        S.dma("pool", g2t[:], ctx["g2_d"][l], "w")
        S.dma("sp", bsr[:], ctx["bs_d"][l], "small")
        S.dma("sp", pbt[:], ctx["pb_d"][l], "small")

        xt = sb("xt", [128, 8, T])
        hb = sb("hb", [128, 8, T], BF16)
        ycat = sb("ycat", [128, 8, T], BF16)
        p_sb = sb("p_sb", [128, 21, W])
        gx = sb("gx", [128, 4, T])
        sbw = dict(sq=[sb("sq0", [128, T]), sb("sq1", [128, T])], rstd=sb("rstd", [128, T]),
                   sqb=[sb("sqb0", [128, T], BF16), sb("sqb1", [128, T], BF16)],
                   g_sq=sb("g_sq", [128, T]), g_th=sb("g_th", [128, T]))
        if first:
            sbw["xtok"] = sb("xtok", [128, NBLK, D])
        wsTf = TA(xt.h[:, 0:2, :].rearrange("p c t -> p (c t)"), (xt.name, ""))
        S.dma("sp", wsTf, ctx["wsT_d"][l], "small")
        for h in range(4):
            S.tt("dve", wsTt[:, h * 128:(h + 1) * 128],
                 TA(wsTf.ap[:, h * 128:(h + 1) * 128], (xt.name, "")), mask4[:, 128:256], ALU.mult)
        S.memset("pool", p_sb[:], 0.0)
        bs_bc = [sb(f"bs_bc{c}", [128, T]) for c in range(2)]
        for c in range(2):
            pbs = nps()
            for hh in range(2):
                h = 2 * c + hh
                for n in range(NBLK):
                    S.mm(pbs[64 * hh:64 * hh + 64, n * 128:(n + 1) * 128], onesr[0:1, 0:64],
                         bsr[0:1, h * 128:(h + 1) * 128], start=True, stop=True)
            S.copy("dve", bs_bc[c][:], pbs[:, 0:T])

        dp = sb("dp", [128, 64])
        S.ts("dve", dp[:, 0:8], modt[:, 8:16], 1.0, None, ALU.add)
        S.tt("dve", dp[:, 0:8], dp[:, 0:8], pvc(l, "nmix"), ALU.mult)
        S.ts("dve", dp[:, 8:11], pvc(l, "lba"), 0.5, None, ALU.mult)
        S.ts("dve", dp[:, 11:14], pvc(l, "lbx"), 0.5, None, ALU.mult)
        S.act(dp[:, 14:17], pvc(l, "llam"), AF.Exp, scale=-1.0)
        S.act(dp[:, 14:17], dp[:, 14:17], AF.Ln, bias=1.0)
        S.ts("dve", dp[:, 17:20], dp[:, 14:17], -4.0, None, ALU.mult)
        S.ts("dve", dp[:, 14:17], dp[:, 14:17], -8.0, None, ALU.mult)
        S.ts("dve", dp[:, 20:31], pvc(l, "mu"), -1.0, 1.0, ALU.mult, ALU.add)
        S.ts("dve", dp[:, 31:34], pvc(l, "w0"), 0.5, None, ALU.mult)
        S.ts("dve", dp[:, 34:37], pvc(l, "a0"), 0.5, None, ALU.mult)
        S.ts("dve", dp[:, 37:40], pvc(l, "ka"), 0.5, None, ALU.mult)
        S.ts("dve", dp[:, 40:43], pvc(l, "ka"), -0.5, 1.0, ALU.mult, ALU.add)

        def dpc(o, j=0):
            return dp[:, o + j:o + j + 1]

        hstate = sb("hstate", [128, 3])
        S.memset("dve", hstate[:], 0.0)
        sT = [[sb(f"sT{p}_{b}", [128, 64]) for b in range(2)] for p in range(3)]
        for p in range(3):
            S.memset("dve", sT[p][0][:], 0.0)

        def wt(name, dt=F32, shape=None):
            return sb(name, shape or [128, T], dt)

        xcb = wt("xcb", BF16)
        st6 = sb("st6", [128, 6]); mv2 = sb("mv2", [128, 2]); vr = sb("vr", [128, 1])
        vn = sb("vn", [128, 256]); vtok = sb("vtok", [128, 256], BF16)
        xwa = wt("xwa", BF16); sgx = wt("sgx", BF16); mixtmp = wt("mixtmp")
        PMV = [wt(f"pmv{p}") for p in range(3)]
        E = [wt(f"E{p}") for p in range(3)]
        BT = [wt(f"BT{p}") for p in range(3)]
        KT = [wt(f"KT{p}") for p in range(3)]
        RK = [wt(f"RK{p}", BF16) for p in range(3)]
        BTb = [wt(f"BTb{p}", BF16) for p in range(3)]
        KTb = [wt(f"KTb{p}", BF16) for p in range(3)]
        ARb = [sb(f"ARb{p}", [128, NBLK, 2, 128], BF16) for p in range(3)]

        AR = [sb(f"AR{p}", [128, NBLK, 2, 128]) for p in range(3)]
        pmr = wt("pmr"); pmk = wt("pmk"); tw = wt("tw"); ta = wt("ta"); lw = wt("lw"); cum = wt("cum")
        Ei = wt("Ei"); Ep = wt("Ep"); rn = wt("rn"); kkn = wt("kkn"); ff = wt("ff"); kp = wt("kp"); t1 = wt("t1")
        xc = kp; tr_ = tw; ti_ = ta
        gsb = [sb(f"gsb{n}", [128, 384]) for n in range(NBLK)]
        tok3 = [sb(f"tok3_{p}", [128, 3, 128], BF16) for p in range(3)]
        AM = [sb(f"AM{p}", [128, 512], BF16) for p in range(3)]
        PM = [[sb(f"PM{p}_{b}", [128, 256]) for b in range(2)] for p in range(3)]
        Q = [[sb(f"Q{p}_{b}", [128, 128]) for b in range(2)] for p in range(3)]
        X = [sb(f"X{p}", [128, 64]) for p in range(3)]
        U = [sb(f"U{p}", [128, 64], BF16) for p in range(3)]
        On = [sb(f"On{p}", [128, 128]) for p in range(3)]
        bon = [sb(f"bon{p}", [128, 2]) for p in range(3)]
        stc = [sb(f"stc{p}", [128, 2, 6]) for p in range(3)]
        mvc = [sb(f"mvc{p}", [128, 2, 2]) for p in range(3)]
        rsc = [sb(f"rsc{p}", [128, 2]) for p in range(3)]
        yct = [sb(f"yct{p}", [128, 128]) for p in range(3)]

        def P(c, lo=3, hi=None):
            hi = W if hi is None else hi
            return p_sb.s(c, (ALL, c, slice(lo, hi)))

        def mix(c, out):
            j = c - 10
            S.act(mixtmp[:], P(c), AF.Identity, scale=dpc(20, j))
            S.stt(out, P(c, 2, 2 + T), pvc(l, "mu", j), mixtmp[:], ALU.mult, ALU.add)

        rn3 = [wt(f"rn3_{p}") for p in range(3)]
        la = [lw, t1, rn]
        lm = [cum, pmr, wt("lm2")]
        lg = [Ei, Ep, wt("lg2")]
        hs_ = kkn
        gy = ff

        def lru_part1():
            for c in range(3):
                xr = 4 + c
                S.ts("dve", xc[:], P(xr, 0, T), pvc(l, "lcw", c * 4 + 0), pvc(l, "lcb", c), ALU.mult, ALU.add)
                for j in range(1, 4):
                    S.stt(xc[:], P(xr, j, j + T), pvc(l, "lcw", c * 4 + j), xc[:], ALU.mult, ALU.add)
                S.copy("pool", xcb[:], xc[:])
                pr = nps()
                pi_ = nps()
                S.mm(pr[:, 0:T], lruWt[:, c * 128:(c + 1) * 128], xcb[:])
                S.mm(pi_[:, 0:T], lruWt[:, 384 + c * 128:384 + (c + 1) * 128], xcb[:])
                S.act(tr_[:], pr[:, 0:T], AF.Tanh, bias=dpc(8, c), scale=0.5)
                S.act(ti_[:], pi_[:, 0:T], AF.Tanh, bias=dpc(11, c), scale=0.5)
                S.act(la[c][:], tr_[:], AF.Exp, bias=dpc(17, c), scale=dpc(17, c))
                S.act(lm[c][:], tr_[:], AF.Exp, bias=dpc(14, c), scale=dpc(14, c))
                S.ts("dve", lm[c][:], lm[c][:], -1.0, 1.0, ALU.mult, ALU.add)
                S.ts("dve", lm[c][:], lm[c][:], 0.0, None, ALU.max)
                S.stt(lg[c][:], ti_[:], 1.0, xc[:], ALU.add, ALU.mult)

        ddv = dd.rearrange("(c p) t -> p c t", p=128)
        sdv = sd.rearrange("(c p) t -> p c t", p=128) if not first else None
        for i in range(ctx["n_tiles"]):
            if first or i == 0:
                ctx["load_x"](first, xt, sn, sd, i, sbw)
            ctx["rms_to"](sbw, xt, lambda c: dp[:, c:c + 1], lambda c: modt[:, c:c + 1], hb, "pool")
            order = list(range(10, 21)) + list(range(4, 10)) + list(range(0, 4))
            for oc in order:
                pp = nps()
                for k in range(8):
                    S.mm(pp[:, 0:T], wint[:, k, oc * 128:(oc + 1) * 128], hb.s(k, (ALL, k, ALL)),
                         start=(k == 0), stop=(k == 7))
                S.copy("act", P(oc), pp[:, 0:T])

            mix(19, xwa[:])
            S.act(xwa[0:64, :], xwa[0:64, :], AF.Tanh)
            mix(20, mixtmp[:])
            S.act(sgx[:], mixtmp[:], AF.Tanh, scale=0.5)
            S.ts("pool", sgx[:], sgx[:], 0.5, 0.5, ALU.mult, ALU.add)
            for n in range(NBLK):
                pg = nps()
                S.mm(pg[:, 0:384], sgx[:, n * 128:(n + 1) * 128], g2t[:])
                S.copy("act", gsb[n][:], pg[:, 0:384])

            lru_part1()
            for p in range(3):
                mix(13 + p, pmk[:])
                S.act(xcb[:], pmk[:], AF.Square, scale=pvc(l, "kk", p))
                S.mm(ctx["ps"][5 + p][:, 0:T], bonesb[:], xcb[:])
            for p in range(3):
                S.act(rn3[p][:], ctx["ps"][5 + p][:, 0:T], AF.Sqrt)
            for c in range(3):
                S.act(lm[c][:], lm[c][:], AF.Sqrt)
            for p in range(3):
                S.ts("dve", rn3[p][:], rn3[p][:], 1e-12, None, ALU.max)
                S.recip(rn3[p][:], rn3[p][:])
            for c in range(3):
                S.stt(lg[c][:], lm[c][:], 0.5, lg[c][:], ALU.mult, ALU.mult)
                S.scan(hs_[:], la[c][:], lg[c][:], hstate[:, c:c + 1], ALU.mult, ALU.add)
                S.copy("pool", hstate[:, c:c + 1], hs_[:, T - 1:T])
                gelu2(sbw, gy[:], P(7 + c))
                S.stt(ycat.s(2 + c, (ALL, 2 + c, ALL)), gy[:], 0.5, hs_[:], ALU.mult, ALU.mult)

            for p in range(3):
                rn = rn3[p]
                mix(10 + p, pmr[:])
                mix(13 + p, pmk[:])
                mix(16 + p, PMV[p][:])
                pw = nps()
                pa_ = nps()
                S.mm(pw[:, 0:T], w2a2t[0:64, p * 128:(p + 1) * 128], xwa[0:64, :])
                S.mm(pa_[:, 0:T], w2a2t[64:128, p * 128:(p + 1) * 128], xwa[64:128, :])
                S.act(tw[:], pw[:, 0:T], AF.Tanh, bias=dpc(31, p), scale=0.5)
                S.act(ta[:], pa_[:, 0:T], AF.Tanh, bias=dpc(34, p), scale=0.5)
                S.ts("dve", lw[:], tw[:], 1.0, -0.5 * 0.6065306597126334, ALU.add, ALU.mult)
                S.scan(cum[:], keep[:], lw[:], 0.0, ALU.mult, ALU.add)
                S.act(E[p][:], cum[:], AF.Exp)
                S.act(Ei[:], cum[:], AF.Exp, scale=-1.0)
                S.tt("pool", Ep[:], cum[:], lw[:], ALU.subtract)
                S.act(Ep[:], Ep[:], AF.Exp)
                S.stt(kkn[:], pmk[:], pvc(l, "kk", p), rn[:], ALU.mult, ALU.mult)
                S.ts("dve", ff[:], ta[:], dpc(37, p), dpc(40, p), ALU.mult, ALU.add)
                S.tt("pool", kp[:], pmk[:], ff[:], ALU.mult)
                ar = AR[p]
                a_view = TA(ar.h[:, :, 0, :], (ar.name, ""))
                r_view = TA(ar.h[:, :, 1, :], (ar.name, ""))

                def v3(tile_):
                    return TA(tile_.h[:].rearrange("p (n t) -> p n t", t=128), (tile_.name, ""))

                S.stt(a_view, v3(kkn), -1.0, v3(Ep), ALU.mult, ALU.mult)
                S.tt("dve", r_view, v3(pmr), v3(E[p]), ALU.mult)
                S.stt(t1[:], ta[:], 1.0, kkn[:], ALU.add, ALU.mult)
                S.stt(BT[p][:], t1[:], 0.5, Ei[:], ALU.mult, ALU.mult)
                S.tt("pool", KT[p][:], kp[:], Ei[:], ALU.mult)
                S.stt(RK[p][:], pmr[:], pvc(l, "rk", p), kp[:], ALU.mult, ALU.mult)
                S.copy("pool", BTb[p][:], BT[p][:])
                S.copy("pool", KTb[p][:], KT[p][:])
                S.copy("pool", TA(ARb[p].h[:].rearrange("p n a t -> p (n a t)"), (ARb[p].name, "")),
                       TA(ar.h[:].rearrange("p n a t -> p (n a t)"), (ar.name, "")))

            for c in range(4):
                gelu2(sbw, gx.s(c, (ALL, c, ALL)), P(c))
            psm = [ctx["ps"][5], ctx["ps"][6]]
            for n in range(NBLK):
                cs = slice(n * 128, (n + 1) * 128)
                pt = nps()
                S.tr(pt[:, 0:128], gx.s(2, (ALL, 2, cs)), ident[:])
                S.tr(pt[:, 128:256], gx.s(3, (ALL, 3, cs)), ident[:])
                S.bn_stats(st6[:], pt[:, 0:256])
                S.bn_aggr(mv2[:], st6[:])
                S.ts("dve", vr[:], mv2[:, 1:2], 4.0 * LN_EPS, None, ALU.add)
                S.tt("pool", vr[:], vr[:], negh[:, 0:1], ALU.pow)
                S.ts("dve", vn[:], pt[:, 0:256], mv2[:, 0:1], vr[:], ALU.subtract, ALU.mult)
                S.tt("pool", vn[:], vn[:], pbt[:, 0:256], ALU.mult)
                S.tt("pool", vtok[:], vn[:], pbt[:, 256:512], ALU.add)
                for h in range(4):
                    o = psm[h // 2][64 * (h % 2):64 * (h % 2) + 64, cs]
                    S.mm(o, vtok[:, 64 * h:64 * h + 64], wsTt[:, h * 128:(h + 1) * 128], start=True, stop=True)
            for c in range(2):
                S.tt("dve", vn[:], psm[c][:, 0:T], bs_bc[c][:], ALU.add)
                S.stt(ycat.s(c, (ALL, c, ALL)), vn[:], 0.5, gx.s(c, (ALL, c, ALL)), ALU.mult, ALU.mult)

            def core_unit(p, n):
                g = i * NBLK + n
                par = g % 2
                cs = slice(n * 128, (n + 1) * 128)
                ar = AR[p]
                arb = ARb[p]
                pO = ctx["ps"][5 + p]
                pt = nps()
                S.tr(pt[:, 0:128], BT[p][:, cs], ident[:])
                S.tr(pt[:, 128:256], KT[p][:, cs], ident[:])
                S.tr(pt[:, 256:384], PMV[p][:, cs], ident[:])
                S.copy("act", TA(tok3[p].h[:].rearrange("p a t -> p (a t)"), (tok3[p].name, "")), pt[:, 0:384])
                yield
                for h in range(2):
                    hs = slice(64 * h, 64 * h + 64)
                    arhb = TA(arb.h[hs, n, :, :].rearrange("p a t -> p (a t)"), (arb.name, ""))
                    ahb = TA(arb.h[hs, n, 0, :], (arb.name, ""))
                    ah = TA(ar.h[hs, n, 0, :], (ar.name, ""))
                    rh = TA(ar.h[hs, n, 1, :], (ar.name, ""))
                    s_old = sT[p][par].s(h, (hs, ALL))
                    s_new = sT[p][1 - par].s(h, (hs, ALL))
                    vtk = tok3[p][:, 2, hs]
                    pa = nps()
                    S.mm(pa[:, 0:256], BTb[p][hs, cs], arhb)
                    S.mm(pa[:, 256:512], KTb[p][hs, cs], arhb)
                    S.tt("dve", PM[p][0][:, 0:128], pa[:, 0:128], mask4[:, 0:128], ALU.mult)
                    S.tt("dve", AM[p][:, 128:512], pa[:, 128:512], mask4[:, 128:512], ALU.mult)
                    pb_ = nps()
                    S.mm(pb_[:, 0:128], ahb, BTb[p][hs, cs])
                    S.tt("dve", Q[p][0][:], pb_[:, 0:128], maskL[:], ALU.mult)
                    S.copy("pool", PM[p][0][:, 128:256], ident[:])
                    yield
                    cur = 0
                    for k in range(1, 7):
                        pc = nps()
                        pd = nps()
                        nxt = 1 - cur
                        if k < 6:
                            S.mm(pc[:, 0:256], Q[p][cur][:], PM[p][cur][:, 0:256])
                        else:
                            S.mm(pc[:, 128:256], Q[p][cur][:], PM[p][cur][:, 128:256])
                        S.mm(pd[:, 0:128], PM[p][cur][:, 0:128], Q[p][cur][:])
                        if k < 6:
                            S.copy("act", PM[p][nxt][:, 0:128], pc[:, 0:128])
                        S.tt("dve", PM[p][nxt][:, 128:256], PM[p][cur][:, 128:256], pc[:, 128:256], ALU.add)
                        S.copy("act", Q[p][nxt][:], pd[:, 0:128])
                        cur = nxt
                        yield
                    pc = nps()
                    S.mm(pc[:, 0:128], Q[p][cur][:], PM[p][cur][:, 128:256])
                    S.tt("dve", PM[p][1 - cur][:, 128:256], PM[p][cur][:, 128:256], pc[:, 0:128], ALU.add)
                    Tt = PM[p][1 - cur][:, 128:256]
                    yield
                    px = nps()
                    S.mm(px[:, 0:64], ah, s_old, start=True, stop=False)
                    S.mm(px[:, 0:64], AM[p][:, 256:384], vtk, start=False, stop=True)
                    S.copy("act", X[p][:], px[:, 0:64])
                    yield
                    pu = nps()
                    S.mm(pu[:, 0:64], Tt, X[p][:])
                    S.copy("act", U[p][:], pu[:, 0:64])
                    yield
                    S.mm(pO[:, hs], rh, s_old, start=True, stop=False)
                    S.mm(pO[:, hs], AM[p][:, 128:256], U[p][:], start=False, stop=False)
                    S.mm(pO[:, hs], AM[p][:, 384:512], vtk, start=False, stop=True)
                    pS = nps()
                    S.mm(pS[hs, 0:64], tok3[p][:, 0, hs], U[p][:], start=True, stop=False)
                    S.mm(pS[hs, 0:64], tok3[p][:, 1, hs], vtk, start=False, stop=True)
                    el = E[p][hs, n * 128 + 127:n * 128 + 128]
                    S.ts("pool", s_new, s_old, el, None, ALU.mult)
                    S.stt(s_new, pS[hs, 0:64], el, s_new, ALU.mult, ALU.add)
                    yield
                for h in range(2):
                    hs = slice(64 * h, 64 * h + 64)
                    S.bn_stats(stc[p][:, h, :], pO[:, hs])
                    S.bn_aggr(mvc[p][:, h, :], stc[p][:, h, :])
                S.ts("dve", rsc[p][:], mvc[p][:, :, 1], GN_EPS, None, ALU.add)
                S.tt("pool", rsc[p][:], rsc[p][:], negh[:, 0:2], ALU.pow)
                for h in range(2):
                    hs = slice(64 * h, 64 * h + 64)
                    S.ts("dve", On[p][:, hs], pO[:, hs], mvc[p][:, h, 0:1], rsc[p][:, h:h + 1], ALU.subtract, ALU.mult)
                S.tt("pool", On[p][:], On[p][:], pbt[:, 512 + 128 * p:512 + 128 * (p + 1)], ALU.mult)
                S.tt("pool", On[p][:], On[p][:], pbt[:, 896 + 128 * p:896 + 128 * (p + 1)], ALU.add)
                pbn = nps()
                S.mm(pbn[:, 0:2], RK[p][:, cs], selb[:])
                S.copy("act", bon[p][:], pbn[:, 0:2])
                for h in range(2):
                    hs = slice(64 * h, 64 * h + 64)
                    S.stt(On[p][:, hs], tok3[p][:, 2, hs], bon[p][:, h:h + 1], On[p][:, hs], ALU.mult, ALU.add)
                S.tt("pool", yct[p][:], On[p][:], gsb[n][:, 128 * p:128 * (p + 1)], ALU.mult)
                ptr = nps()
                S.tr(ptr[:, 0:128], yct[p][:], ident[:])
                S.copy("act", ycat.s(5 + p, (ALL, 5 + p, cs)), ptr[:, 0:128])
                yield

            for n in range(NBLK):
                gens = [core_unit(p, n) for p in range(3)]
                while gens:
                    for gnr in list(gens):
                        try:
                            next(gnr)
                        except StopIteration:
                            gens.remove(gnr)

            for m in range(8):
                po = nps()
                for k in range(8):
                    S.mm(po[:, 0:T], woutt[:, k, m * 128:(m + 1) * 128], ycat.s(k, (ALL, k, ALL)),
                         start=(k == 0), stop=(k == 7))
                S.stt(xt.s(m, (ALL, m, ALL)), po[:, 0:T], modt[:, 16 + m:17 + m], xt.s(m, (ALL, m, ALL)),
                      ALU.mult, ALU.add)
                S.dma("sp", TA(ddv[:, m, i * T:(i + 1) * T], (dn, f"{i}_{m}")), xt.s(m, (ALL, m, ALL)), "xout")
                if (not first) and i + 1 < ctx["n_tiles"]:
                    S.dma("sp", xt.s(m, (ALL, m, ALL)), TA(sdv[:, m, (i + 1) * T:(i + 2) * T], (sn, f"{i + 1}_{m}")),
                          "xin")
            hk = [(p_sb.name, str(c)) for c in range(4, 7)]
            a0 = p_sb.h[:, 4:7, 0:3]
            a1 = p_sb.h[:, 4:7, T:T + 3]
            S.op("pool", lambda e, a0=a0, a1=a1: e.tensor_copy(a0, a1), hk, hk)
            hk2 = [(p_sb.name, str(c)) for c in range(10, 21)]
            b0 = p_sb.h[:, 10:21, 0:3]
            b1 = p_sb.h[:, 10:21, T:T + 3]
            S.op("pool", lambda e, b0=b0, b1=b1: e.tensor_copy(b0, b1), hk2, hk2)
        S.wait_all_dma("sp", ["xout"])


_INPUT_ORDER = None


def kernel(**inp):
    inp = {k: np.asarray(v) for k, v in inp.items()}
    pk = pack_params(inp)
    nc = build_program()
    x = np.ascontiguousarray(inp["x"], dtype=np.float32)
    c = np.asarray(inp["c"], np.float32)
    shared = dict(
        w_mod=np.ascontiguousarray(inp["w_mod"], dtype=np.float32),
        w_in=np.ascontiguousarray(inp["w_in"], dtype=np.float32),
        w_out=np.ascontiguousarray(inp["w_out"], dtype=np.float32),
        w_up=np.ascontiguousarray(inp["ffn_w_up"], dtype=np.float32),
        w_dn=np.ascontiguousarray(inp["ffn_w_down"], dtype=np.float32),
        g2=np.ascontiguousarray(inp["rwkv_g2"], dtype=np.float32),
        **pk,
    )
    in_maps = []
    for b in range(NB):
        m = dict(shared)
        m["x"] = x[b]
        m["cT"] = _fm(c[b])
        in_maps.append(m)
    res = run_bass_kernel_spmd(nc, in_maps, core_ids=list(range(NB)))
    out = np.stack([np.asarray(r["y"], dtype=np.float32) for r in res.results], axis=0)
    return out
```

```python
import numpy as np
from contextlib import ExitStack
import concourse.bass as bass
import concourse.mybir as mybir
from concourse.bass_utils import run_bass_kernel_spmd

F32 = mybir.dt.float32
BF16 = mybir.dt.bfloat16
AF = mybir.ActivationFunctionType
ALU = mybir.AluOpType

COMPUTE = ("pe", "dve", "act", "pool")
ALLQ = ("pe", "dve", "act", "pool", "sp")


class TA:
    __slots__ = ("ap", "key")

    def __init__(self, ap, key):
        self.ap = ap
        self.key = key


class Tile:
    def __init__(self, h, name):
        self.h = h
        self.name = name

    def __getitem__(self, idx):
        return TA(self.h[idx], (self.name, ""))

    def s(self, sub, idx):
        return TA(self.h[idx], (self.name, str(sub)))


class Sched:
    def __init__(self, nc, es, n_epochs=8):
        self.nc = nc
        self.es = es
        self.ops = {q: [] for q in ALLQ}
        self.cnt = {e: 0 for e in COMPUTE}
        self.state = {}
        self.seen = {q: {} for q in ALLQ}
        self.epoch = 0
        self.n_epochs = n_epochs
        self.sems = {}
        for e in COMPUTE:
            for ep in range(n_epochs):
                self.sems[(e, ep)] = es.enter_context(nc.semaphore(f"s_{e}_{ep}"))
        self.dma_sem = {}
        self.dma_cnt = {}
        self.need_inc = {}
        self.n_instr = 0

    def _entries(self, key):
        name, sub = key
        d = self.state.setdefault(name, {})
        if sub == "":
            if "" not in d:
                d[""] = [None, []]
            return list(d.values())
        out = []
        if "" in d:
            out.append(d[""])
        if sub not in d:
            d[sub] = [None, []]
        out.append(d[sub])
        return out

    def _own(self, key):
        name, sub = key
        d = self.state.setdefault(name, {})
        if sub not in d:
            d[sub] = [None, []]
        return d[sub]

    def _collect(self, reads, writes, q=None):
        deps = {}

        def add(t):
            if t is None:
                return
            src, idx = t
            if deps.get(src, 0) < idx:
                deps[src] = idx

        for k in reads:
            excl = k[0].startswith("ps")
            for ent in self._entries(k):
                add(ent[0])
                if excl:
                    for r in ent[1]:
                        if r[0] != q:
                            add(r)
        for k in writes:
            for ent in self._entries(k):
                add(ent[0])
                for r in ent[1]:
                    add(r)
        return deps

    def _record(self, me, reads, writes):
        for k in reads:
            self._own(k)[1].append(me)
        for k in writes:
            name, sub = k
            if sub == "":
                self.state[name] = {"": [me, []]}
            else:
                ent = self._own(k)
                ent[0] = me
                ent[1] = []

    def _waits(self, q, deps):
        for src, idx in deps.items():
            if src == q and q == "pe":
                continue
            if self.seen[q].get(src, 0) >= idx:
                continue
            self.seen[q][src] = idx
            if src in COMPUTE:
                self.need_inc.setdefault((src, self.epoch), set()).add(idx)
                self.ops[q].append(("wait", src, self.epoch, idx))
            else:
                idx = self.dma_cnt[src[4:]]
                self.seen[q][src] = idx
                self.ops[q].append(("waitdma", src, idx))

    def op(self, q, fn, reads, writes):
        deps = self._collect(reads, writes, q)
        self._waits(q, deps)
        self.cnt[q] += 1
        me = (q, self.cnt[q])
        self.ops[q].append(("op", fn, self.epoch, self.cnt[q]))
        self._record(me, reads, writes)
        self.n_instr += 1

    def dma(self, q, out, in_, key, **kw):
        if key not in self.dma_sem:
            self.dma_sem[key] = self.es.enter_context(self.nc.semaphore("d_" + key))
            self.dma_cnt[key] = 0
        reads = [in_.key] if isinstance(in_, TA) else []
        writes = [out.key] if isinstance(out, TA) else []
        deps = self._collect(reads, writes)
        self._waits(q, deps)
        self.dma_cnt[key] += 1
        n = self.dma_cnt[key]
        o = out.ap if isinstance(out, TA) else out
        i = in_.ap if isinstance(in_, TA) else in_
        sem = self.dma_sem[key]
        self.ops[q].append(("dma", lambda e: e.dma_start(out=o, in_=i, **kw), sem))
        self._record(("dma:" + key, n), reads, writes)
        self.n_instr += 1

    def barrier(self):
        for q in ALLQ:
            deps = {}
            for e in COMPUTE:
                if self.cnt[e] > 0 and not (e == q == "pe"):
                    deps[e] = self.cnt[e]
            for key, n in self.dma_cnt.items():
                if n > 0:
                    deps["dma:" + key] = n
            self._waits(q, deps)
        self.epoch += 1
        assert self.epoch < self.n_epochs
        self.state = {}
        self.seen = {q: {k: v for k, v in self.seen[q].items() if k not in COMPUTE} for q in ALLQ}
        self.cnt = {e: 0 for e in COMPUTE}

    def wait_all_dma(self, q, keys):
        deps = {"dma:" + k: self.dma_cnt[k] for k in keys if self.dma_cnt.get(k, 0) > 0}
        self._waits(q, deps)

    def flush(self):
        nc = self.nc
        cum = {}
        for (e, ep), idxs in self.need_inc.items():
            s = sorted(idxs)
            cum[(e, ep)] = {idx: i + 1 for i, idx in enumerate(s)}

        def run(q, eng):
            for rec in self.ops[q]:
                kind = rec[0]
                if kind == "op":
                    _, fn, ep, idx = rec
                    ins = fn(eng)
                    c = cum.get((q, ep))
                    if c is not None and idx in c:
                        ins.then_inc(self.sems[(q, ep)], 1)
                elif kind == "wait":
                    _, src, ep, idx = rec
                    eng.wait_ge(self.sems[(src, ep)], cum[(src, ep)][idx])
                elif kind == "waitdma":
                    _, src, idx = rec
                    eng.wait_ge(self.dma_sem[src[4:]], 16 * idx)
                elif kind == "dma":
                    _, fn, sem = rec
                    fn(eng).then_inc(sem, 16)

        with nc.Block() as block:
            @block.sync
            def _(e):
                run("sp", e)

            @block.tensor
            def _(e):
                run("pe", e)

            @block.vector
            def _(e):
                run("dve", e)

            @block.scalar
            def _(e):
                run("act", e)

            @block.gpsimd
            def _(e):
                run("pool", e)
        self.ops = {q: [] for q in ALLQ}

    @staticmethod
    def _k(*tas):
        return [t.key for t in tas if isinstance(t, TA)]

    @staticmethod
    def _v(x):
        return x.ap if isinstance(x, TA) else x

    def mm(self, out, lhsT, rhs, start=True, stop=True):
        o, l, r = out.ap, lhsT.ap, rhs.ap
        self.op("pe", lambda e: e.matmul(o, l, r, start=start, stop=stop),
                self._k(lhsT, rhs), self._k(out))

    def tr(self, out, in_, ident):
        o, i, d = out.ap, in_.ap, ident.ap
        self.op("pe", lambda e: e.transpose(o, i, d), self._k(in_, ident), self._k(out))

    def act(self, out, in_, func, bias=None, scale=None):
        o, i = out.ap, in_.ap
        kw = {}
        if bias is not None:
            kw["bias"] = self._v(bias)
        if scale is not None:
            kw["scale"] = self._v(scale)
        self.op("act", lambda e: e.activation(o, i, func, **kw),
                self._k(in_, bias, scale), self._k(out))

    def ts(self, q, out, in0, s1, s2, op0, op1=None):
        o, i = out.ap, in0.ap
        a, b = self._v(s1), self._v(s2)
        if op1 is None:
            f = lambda e: e.tensor_scalar(o, i, a, None, op0)
        else:
            f = lambda e: e.tensor_scalar(o, i, a, b, op0, op1)
        self.op(q, f, self._k(in0, s1, s2), self._k(out))

    def tt(self, q, out, in0, in1, op):
        o, a, b = out.ap, in0.ap, in1.ap
        self.op(q, lambda e: e.tensor_tensor(o, a, b, op), self._k(in0, in1), self._k(out))

    def stt(self, out, in0, scalar, in1, op0, op1):
        o, a, b = out.ap, in0.ap, in1.ap
        s = self._v(scalar)
        self.op("dve", lambda e: e.scalar_tensor_tensor(o, a, s, b, op0, op1),
                self._k(in0, scalar, in1), self._k(out))

    def scan(self, out, d0, d1, init, op0, op1):
        o, a, b = out.ap, d0.ap, d1.ap
        s = self._v(init)
        self.op("dve", lambda e: e.tensor_tensor_scan(o, a, b, s, op0, op1),
                self._k(d0, d1, init), self._k(out))

    def copy(self, q, out, in_):
        o, i = out.ap, in_.ap
        if q == "act":
            f = lambda e: e.activation(o, i, AF.Copy)
        else:
            f = lambda e: e.tensor_copy(o, i)
        self.op(q, f, self._k(in_), self._k(out))

    def memset(self, q, ta, val):
        a = ta.ap
        self.op(q, lambda e: e.memset(a, val), [], self._k(ta))

    def recip(self, out, in_):
        o, i = out.ap, in_.ap
        self.op("dve", lambda e: e.reciprocal(o, i), self._k(in_), self._k(out))

    def bn_stats(self, out, in_):
        o, i = out.ap, in_.ap
        self.op("dve", lambda e: e.bn_stats(o, i), self._k(in_), self._k(out))

    def bn_aggr(self, out, in_):
        o, i = out.ap, in_.ap
        self.op("dve", lambda e: e.bn_aggr(o, i), self._k(in_), self._k(out))


D = 1024
SEQ = 4096
NB = 8
P_IN = 2688
DFF = 2816
T = 256
TF = 512
NT = SEQ // T
NBLK = T // 128
EPS = 1e-6
LN_EPS = 1e-5
GN_EPS = 64e-5
PV_NAMES = [("bmod", 48), ("nmix", 8), ("nffn", 8), ("lcw", 12), ("lcb", 3), ("lba", 3),
            ("lbx", 3), ("llam", 3), ("mu", 11), ("w0", 3), ("a0", 3), ("kk", 3), ("ka", 3),
            ("rk", 3), ("fcw", 66), ("fcb", 22), ("nfin", 8)]
PV_OFF = {}
_o = 0
for _n, _k in PV_NAMES:
    PV_OFF[_n] = (_o, _k)
    _o += _k
NPV = _o
NPB = 256 + 256 + 384 + 384


def _fm(v):
    return np.ascontiguousarray(np.asarray(v, np.float32).reshape(-1, 128).T)


def pack_params(inp):
    L = 2
    pv = np.zeros((L, 128, NPV), np.float32)
    pb = np.zeros((L, 128, NPB), np.float32)
    wsT = np.zeros((L, 128, 512), np.float32)
    bsrow = np.zeros((L, 1, 512), np.float32)
    lruW = np.zeros((L, 128, 2, 3, 128), np.float32)
    w2a2 = np.zeros((L, 128, 384), np.float32)

    def put(l, name, arr):
        o, k = PV_OFF[name]
        assert arr.shape == (128, k), (name, arr.shape)
        pv[l, :, o:o + k] = arr

    for l in range(L):
        put(l, "bmod", _fm(inp["b_mod"][l]))
        put(l, "nmix", _fm(inp["norm_mix"][l]))
        put(l, "nffn", _fm(inp["norm_ffn"][l]))
        put(l, "lcw", np.asarray(inp["lru_conv_w"][l]).reshape(4, 3, 128).transpose(2, 1, 0).reshape(128, 12))
        put(l, "lcb", _fm(inp["lru_conv_b"][l]))
        put(l, "lba", _fm(inp["lru_b_a"][l]))
        put(l, "lbx", _fm(inp["lru_b_x"][l]))
        put(l, "llam", _fm(inp["lru_lambda"][l]))
        put(l, "mu", _fm(inp["rwkv_mu"][l]))
        put(l, "w0", _fm(inp["rwkv_w0"][l]))
        put(l, "a0", _fm(inp["rwkv_a0"][l]))
        put(l, "kk", _fm(inp["rwkv_k_k"][l]))
        put(l, "ka", _fm(inp["rwkv_k_a"][l]))
        put(l, "rk", _fm(np.asarray(inp["rwkv_r_k"][l]).reshape(-1)))
        put(l, "fcw", np.asarray(inp["ffn_conv_w"][l]).reshape(3, 22, 128).transpose(2, 1, 0).reshape(128, 66))
        put(l, "fcb", _fm(inp["ffn_conv_b"][l]))
        put(l, "nfin", _fm(inp["norm_final"]))
        pb[l, :, 0:256] = np.asarray(inp["sgu_ln_g"][l])[None, :]
        pb[l, :, 256:512] = np.asarray(inp["sgu_ln_b"][l])[None, :]
        pb[l, :, 512:896] = np.asarray(inp["rwkv_ln_w"][l])[None, :]
        pb[l, :, 896:1280] = np.asarray(inp["rwkv_ln_b"][l])[None, :]
        wsT[l] = np.asarray(inp["sgu_w"][l]).transpose(2, 0, 1).reshape(128, 512)
        bsrow[l, 0] = np.asarray(inp["sgu_b"][l]).reshape(512)
        for wi, nm in enumerate(("lru_w_a", "lru_w_x")):
            W = np.asarray(inp[nm][l])
            for c in range(3):
                for hh in range(2):
                    lruW[l, hh * 64:(hh + 1) * 64, wi, c, hh * 64:(hh + 1) * 64] = W[2 * c + hh]
        w2a2[l, 0:64] = np.asarray(inp["rwkv_w2"][l])
        w2a2[l, 64:128] = np.asarray(inp["rwkv_a2"][l])
    return dict(pv=pv, pb=pb, wsT=wsT, bsrow=bsrow, lruW=lruW.reshape(L, 128, 768), w2a2=w2a2)


def build_program(n_tiles=NT, n_pass=4):
    nc = bass.Bass("TRN2", target_bir_lowering=False)

    def din(name, shape):
        return nc.dram_tensor(name, shape, F32, kind="ExternalInput").ap()

    x_d = din("x", [SEQ, D])
    cT_d = din("cT", [128, 8])
    wmod_d = din("w_mod", [2, D, 6 * D])
    win_d = din("w_in", [2, D, P_IN])
    wout_d = din("w_out", [2, D, D])
    wup_d = din("w_up", [2, D, 2 * DFF])
    wdn_d = din("w_dn", [2, DFF, D])
    pv_d = din("pv", [2, 128, NPV])
    pb_d = din("pb", [2, 128, NPB])
    wsT_d = din("wsT", [2, 128, 512])
    bs_d = din("bsrow", [2, 1, 512])
    lruW_d = din("lruW", [2, 128, 768])
    w2a2_d = din("w2a2", [2, 128, 384])
    g2_d = din("g2", [2, 128, 384])
    y_d = nc.dram_tensor("y", [SEQ, D], F32, kind="ExternalOutput").ap()
    xA_d = nc.dram_tensor("xA", [D, SEQ], F32, kind="ExternalOutput").ap()
    xB_d = nc.dram_tensor("xB", [D, SEQ], F32, kind="ExternalOutput").ap()

    def fm_tile(dr, name, i, TT=T):
        return TA(dr.rearrange("(c p) t -> p c t", p=128)[:, :, i * TT:(i + 1) * TT], (name, str(i)))

    with ExitStack() as es0:
        S = Sched(nc, es0)

        mkc = [0]

        def mk(es):
            pre = f"q{mkc[0]}_"
            mkc[0] += 1

            def sb(name, shape, dt=F32):
                nm = pre + name
                return Tile(es.enter_context(nc.sbuf_tensor(nm, shape, dt)), nm)
            return sb

        sb0 = mk(es0)
        ps = [Tile(es0.enter_context(nc.psum_tensor(f"ps{i}", [128, 512], F32)), f"ps{i}") for i in range(8)]
        psi = [0, 8]

        def nps():
            t = ps[psi[0] % psi[1]]
            psi[0] += 1
            return t

        ident = sb0("ident", [128, 128])
        onesm = sb0("onesm", [128, 128])
        bones = sb0("bones", [128, 128])
        sel = sb0("sel", [128, 2])
        mask4 = sb0("mask4", [128, 512])
        maskL = sb0("maskL", [128, 128])
        keep = sb0("keep", [128, T])
        negh = sb0("negh", [128, 8])
        identb = sb0("identb", [128, 128], BF16)
        onesb = sb0("onesb", [128, 128], BF16)
        bonesb = sb0("bonesb", [128, 128], BF16)
        selb = sb0("selb", [128, 2], BF16)
        onesr = sb0("onesr", [1, 64])
        epsc = sb0("epsc", [128, 1])
        pvt = [sb0("pv0", [128, NPV]), sb0("pv1", [128, NPV])]
        modt = [sb0("mod0", [128, 48]), sb0("mod1", [128, 48])]
        c_sb = sb0("c_sb", [128, 8])
        c_act = sb0("c_act", [128, 8])

        def asel(ta, pattern, cmp, cm):
            a = ta.ap
            S.op("pool", lambda e: e.affine_select(a, a, pattern, cmp, 0.0, base=0, channel_multiplier=cm),
                 [ta.key], [ta.key])

        S.memset("pool", ident[:], 1.0)
        asel(ident[:], [[-1, 128]], ALU.is_equal, 1)
        S.memset("dve", onesm[:], 1.0 / D)
        S.memset("dve", bones[:], 0.0)
        S.memset("dve", bones[0:64, 0:64], 1.0)
        S.memset("dve", bones[64:128, 64:128], 1.0)
        S.memset("dve", sel[:], 0.0)
        S.memset("dve", sel[0:64, 0:1], 1.0)
        S.memset("dve", sel[64:128, 1:2], 1.0)
        S.memset("pool", mask4[:], 1.0)
        S.memset("pool", maskL[:], 1.0)
        for b in range(4):
            asel(mask4[:, b * 128:(b + 1) * 128], [[1, 128]], ALU.is_gt if b % 2 == 0 else ALU.is_ge, -1)
        asel(maskL[:], [[-1, 128]], ALU.is_gt, 1)
        S.memset("dve", keep[:], 1.0)
        for b in range(NBLK):
            S.memset("dve", keep[:, b * 128:b * 128 + 1], 0.0)
        S.memset("pool", negh[:], -0.5)
        S.copy("pool", identb[:], ident[:])
        S.copy("pool", onesb[:], onesm[:])
        S.copy("pool", bonesb[:], bones[:])
        S.copy("pool", selb[:], sel[:])
        S.memset("dve", onesr[:], 1.0)
        S.memset("dve", epsc[:], EPS)
        for l in range(2):
            S.dma("sp", pvt[l][:], pv_d[l], "small")
        S.dma("sp", c_sb[:], cT_d, "small")
        S.act(c_act[:], c_sb[:], AF.Silu)

        def pvc(l, name, j=None, n=1):
            o, k = PV_OFF[name]
            if j is None:
                return pvt[l][:, o:o + k]
            return pvt[l][:, o + j:o + j + n]

        with ExitStack() as es1:
            sb1 = mk(es1)
            wm = [sb1("wm0", [128, 8, 768]), sb1("wm1", [128, 8, 768])]
            for l in range(2):
                pm = nps()
                for g in range(8):
                    w = wm[g % 2]
                    S.dma("sp", w[:], wmod_d[l].rearrange("(k p) n -> p k n", p=128)[:, :, g * 768:(g + 1) * 768],
                          f"wm{g % 2}")
                    for j in range(6):
                        m = g * 6 + j
                        for kc in range(8):
                            S.mm(pm[:, m:m + 1], w[:, kc, j * 128:(j + 1) * 128], c_act[:, kc:kc + 1],
                                 start=(kc == 0), stop=(kc == 7))
                S.tt("dve", modt[l][:], pm[:, 0:48], pvc(l, "bmod"), ALU.add)
            S.barrier()
            S.flush()

        def rms_to(sbw, xt, gcol, shcol, hout, eng_bias, T=T):
            pss = nps()
            for c in range(8):
                sq = sbw["sqb"][c % 2]
                S.act(sq[:], xt.s(c, (slice(None), c, slice(None))), AF.Square)
                S.mm(pss[:, 0:T], onesb[:], sq[:], start=(c == 0), stop=(c == 7))
            rstd = sbw["rstd"]
            S.act(rstd[:], pss[:, 0:T], AF.Sqrt, bias=epsc[:, 0:1], scale=1.0)
            S.recip(rstd[:], rstd[:])
            for c in range(8):
                tmp = sbw["sq"][c % 2]
                S.stt(tmp[:], xt.s(c, (slice(None), c, slice(None))), gcol(c), rstd[:], ALU.mult, ALU.mult)
                if eng_bias == "act":
                    S.act(hout.s(c, (slice(None), c, slice(None))), tmp[:], AF.Identity, bias=shcol(c), scale=1.0)
                else:
                    S.ts("pool", hout.s(c, (slice(None), c, slice(None))), tmp[:], 1.0, shcol(c), ALU.mult, ALU.add)

        def load_x(l_first, xt, src_name, src_d, i, sbw):
            if l_first:
                xtok = sbw["xtok"]
                for b in range(NBLK):
                    S.dma("sp", xtok.s(b, (slice(None), b, slice(None))),
                          x_d[i * T + b * 128:i * T + (b + 1) * 128, :], "xin")
                for c in range(8):
                    pt = nps()
                    for b in range(NBLK):
                        S.tr(pt[:, b * 128:(b + 1) * 128], xtok.s(b, (slice(None), b, slice(c * 128, (c + 1) * 128))),
                             ident[:])
                    S.copy("act" if c % 2 else "dve", xt.s(c, (slice(None), c, slice(None))), pt[:, 0:T])
            else:
                S.dma("sp", xt[:], fm_tile(src_d, src_name, i), "xin")

        def gelu2(sbw, out, P):
            sqt = sbw["g_sq"]
            th = sbw["g_th"]
            S.act(sqt[:], P, AF.Square)
            S.ts("pool", sqt[:], sqt[:], 0.044715, 1.0, ALU.mult, ALU.add)
            S.tt("pool", sqt[:], sqt[:], P, ALU.mult)
            S.act(th[:], sqt[:], AF.Tanh, scale=0.7978845608028654)
            S.stt(out, th[:], 1.0, P, ALU.add, ALU.mult)

        ctx = dict(nc=nc, S=S, mk=mk, nps=nps, ps=ps, psi=psi, epsc=epsc, ident=ident, onesm=onesm, bones=bones, sel=sel, mask4=mask4,
                   maskL=maskL, keep=keep, negh=negh, identb=identb, onesb=onesb, bonesb=bonesb, selb=selb, onesr=onesr, pvc=pvc, modt=modt,
                   rms_to=rms_to, load_x=load_x, gelu2=gelu2, fm_tile=fm_tile, n_tiles=n_tiles,
                   win_d=win_d, wout_d=wout_d, wup_d=wup_d, wdn_d=wdn_d, pb_d=pb_d, wsT_d=wsT_d, bs_d=bs_d,
                   lruW_d=lruW_d, w2a2_d=w2a2_d, g2_d=g2_d, y_d=y_d)

        plan = [("mix", 0, None, None, "xA", xA_d), ("ffn", 0, "xA", xA_d, "xB", xB_d),
                ("mix", 1, "xB", xB_d, "xA", xA_d), ("ffn", 1, "xA", xA_d, None, None)]
        for pi, (kind, l, sn, sd, dn, dd) in enumerate(plan[:n_pass]):
            psi[1] = 5 if kind == "mix" else 8
            if kind == "mix":
                mixer_pass(ctx, l, sn, sd, dn, dd)
            else:
                ffn_pass(ctx, l, sn, sd, dn, dd, final=(l == 1))
            S.barrier()
            S.flush()
    return nc


def _sl(*a):
    return tuple(a)


ALL = slice(None)


def ffn_pass(ctx, l, sn, sd, dn, dd, final):
    S = ctx["S"]
    nps = ctx["nps"]
    pvc = ctx["pvc"]
    ident = ctx["ident"]
    negh = ctx["negh"]
    onesm = ctx["onesm"]
    modt = ctx["modt"][l]
    T = TF
    NBLK = T // 128
    with ExitStack() as es:
        sb = ctx["mk"](es)
        wup = sb("wup", [128, 8, 2 * DFF], BF16)
        wdn = sb("wdn", [128, 22, D], BF16)
        wu_src = ctx["wup_d"][l].rearrange("(k p) n -> p k n", p=128)
        wd_src = ctx["wdn_d"][l].rearrange("(k p) n -> p k n", p=128)
        for grp, qs in (("wA", (0, 2)), ("wB", (1, 3))):
            for k in range(8):
                for q in qs:
                    cs = slice(q * 1408, (q + 1) * 1408)
                    S.dma("pool", wup.s(f"{k}_{q}", (ALL, k, cs)), wu_src[:, k, cs], grp)
                if k % 4 == 3:
                    S.wait_all_dma("pool", [grp])
        for k in range(22):
            S.dma("pool", wdn.s(f"{k}", (ALL, k, ALL)), wd_src[:, k, :], "wD")
            if k % 8 == 7:
                S.wait_all_dma("pool", ["wD"])
        xt = sb("xt", [128, 8, T])
        hb = sb("hb", [128, 8, T], BF16)
        hid = sb("hid", [128, 22, T], BF16)
        raw = [sb(f"raw{j}", [128, 2 + T]) for j in range(2)]
        halo = sb("halo", [128, 22, 2])
        accs = [sb(f"acc{j}", [128, T]) for j in range(2)]
        sls = [sb(f"sl{j}", [128, T]) for j in range(2)]
        sbw = dict(sq=accs, rstd=sb("rstd", [128, T]),
                   sqb=[sb("sqb0", [128, T], BF16), sb("sqb1", [128, T], BF16)])
        g2v = sb("g2v", [128, 8])
        if final:
            fbuf = xt
            ytok = sb("ytok", [128, D])
        S.memset("pool", halo[:], 0.0)
        S.ts("dve", g2v[:], modt[:, 32:40], 1.0, None, ALU.add)
        S.tt("dve", g2v[:], g2v[:], pvc(l, "nffn"), ALU.mult)

        def fcw(j, t):
            return pvc(l, "fcw", j * 3 + t)

        n_ft = ctx["n_tiles"] * 256 // T
        sdv = sd.rearrange("(c p) t -> p c t", p=128)
        ddv = dd.rearrange("(c p) t -> p c t", p=128) if dd is not None else None
        for i in range(n_ft):
            if final or i == 0:
                S.dma("sp", xt[:], ctx["fm_tile"](sd, sn, i, T), "xin")
            ctx["rms_to"](sbw, xt, lambda c: g2v[:, c:c + 1], lambda c: modt[:, 24 + c:25 + c], hb, "act", T)
            for j in range(22):
                pg = nps()
                pv_ = nps()
                qg = 0 if j < 11 else 1
                for k in range(8):
                    S.mm(pg[:, 0:T], wup.s(f"{k}_{qg}", (ALL, k, slice(j * 128, (j + 1) * 128))),
                         hb.s(k, (ALL, k, ALL)), start=(k == 0), stop=(k == 7))
                for k in range(8):
                    S.mm(pv_[:, 0:T], wup.s(f"{k}_{qg + 2}", (ALL, k, slice(DFF + j * 128, DFF + (j + 1) * 128))),
                         hb.s(k, (ALL, k, ALL)), start=(k == 0), stop=(k == 7))
                r = raw[j % 2]
                S.copy("pool", r.s("h", (ALL, slice(0, 2))), halo.s(j, (ALL, j, ALL)))
                S.copy("act", r.s("d", (ALL, slice(2, 2 + T))), pg[:, 0:T])
                S.copy("pool", halo.s(j, (ALL, j, ALL)), r.s("d", (ALL, slice(T, T + 2))))
                acc = accs[j % 2]
                sl = sls[j % 2]
                S.ts("dve", acc[:], r[:, 0:T], fcw(j, 0), pvc(l, "fcb", j), ALU.mult, ALU.add)
                S.stt(acc[:], r[:, 1:T + 1], fcw(j, 1), acc[:], ALU.mult, ALU.add)
                S.stt(acc[:], r[:, 2:T + 2], fcw(j, 2), acc[:], ALU.mult, ALU.add)
                S.act(sl[:], acc[:], AF.Silu)
                S.tt("dve", hid.s(j, (ALL, j, ALL)), sl[:], pv_[:, 0:T], ALU.mult)
            for m in range(8):
                po = nps()
                for k in range(22):
                    S.mm(po[:, 0:T], wdn.s(f"{k}", (ALL, k, slice(m * 128, (m + 1) * 128))), hid.s(k, (ALL, k, ALL)),
                         start=(k == 0), stop=(k == 21))
                S.stt(xt.s(m, (ALL, m, ALL)), po[:, 0:T], modt[:, 40 + m:41 + m], xt.s(m, (ALL, m, ALL)),
                      ALU.mult, ALU.add)
                if not final:
                    S.dma("sp", TA(ddv[:, m, i * T:(i + 1) * T], (dn, f"{i}_{m}")), xt.s(m, (ALL, m, ALL)), "xout")
                    if i + 1 < n_ft:
                        S.dma("sp", xt.s(m, (ALL, m, ALL)), TA(sdv[:, m, (i + 1) * T:(i + 2) * T], (sn, f"{i + 1}_{m}")),
                              "xin")
            if not final:
                pass
            else:
                pss = nps()
                for c in range(8):
                    sq = sbw["sqb"][c % 2]
                    S.act(sq[:], xt.s(c, (ALL, c, ALL)), AF.Square)
                    S.mm(pss[:, 0:T], ctx["onesb"][:], sq[:], start=(c == 0), stop=(c == 7))
                rstd = sbw["rstd"]
                S.act(rstd[:], pss[:, 0:T], AF.Sqrt, bias=ctx["epsc"][:, 0:1], scale=1.0)
                S.recip(rstd[:], rstd[:])
                for c in range(8):
                    S.stt(fbuf.s(c, (ALL, c, ALL)), xt.s(c, (ALL, c, ALL)), pvc(l, "nfin", c), rstd[:],
                          ALU.mult, ALU.mult)
                for b in range(NBLK):
                    for cg in range(2):
                        pt = nps()
                        for cc in range(4):
                            c = cg * 4 + cc
                            S.tr(pt[:, cc * 128:(cc + 1) * 128], fbuf.s(c, (ALL, c, slice(b * 128, (b + 1) * 128))),
                                 ident[:])
                        S.copy("act" if cg else "dve", ytok.s(cg, (ALL, slice(cg * 512, (cg + 1) * 512))), pt[:, 0:512])
                    S.dma("sp", ctx["y_d"][i * T + b * 128:i * T + (b + 1) * 128, :], ytok[:], "yout")
        S.wait_all_dma("sp", ["xout", "yout"])


def mixer_pass(ctx, l, sn, sd, dn, dd):
    S = ctx["S"]
    nps = ctx["nps"]
    pvc = ctx["pvc"]
    ident = ctx["ident"]
    negh = ctx["negh"]
    identb = ctx["identb"]
    bonesb = ctx["bonesb"]
    selb = ctx["selb"]
    mask4 = ctx["mask4"]
    maskL = ctx["maskL"]
    keep = ctx["keep"]
    bones = ctx["bones"]
    sel = ctx["sel"]
    onesr = ctx["onesr"]
    modt = ctx["modt"][l]
    gelu2 = ctx["gelu2"]
    first = sn is None
    W = 3 + T
    with ExitStack() as es:
        sb = ctx["mk"](es)
        wint = sb("wint", [128, 8, P_IN], BF16)
        woutt = sb("woutt", [128, 8, D], BF16)
        wi_src = ctx["win_d"][l].rearrange("(k p) n -> p k n", p=128)
        wo_src = ctx["wout_d"][l].rearrange("(k p) n -> p k n", p=128)
        for grp, q in (("wI1", 1), ("wI0", 0)):
            for k in range(8):
                cs = slice(q * 1344, (q + 1) * 1344)
                S.dma("pool", wint.s(f"{k}_{q}", (ALL, k, cs)), wi_src[:, k, cs], grp)
            S.wait_all_dma("pool", [grp])
        for k in range(8):
            S.dma("pool", woutt.s(f"{k}", (ALL, k, ALL)), wo_src[:, k, :], "wO")
        lruWt = sb("lruWt", [128, 768], BF16)
        w2a2t = sb("w2a2t", [128, 384], BF16)
        g2t = sb("g2t", [128, 384], BF16)
        wsTt = sb("wsTt", [128, 512], BF16)
        bsr = sb("bsr", [1, 512])
        pbt = sb("pbt", [128, NPB])
        S.dma("pool", lruWt[:], ctx["lruW_d"][l], "w")
        S.dma("pool", w2a2t[:], ctx["w2a2_d"][l], "w")
        S.dma("pool", g2t[:], ctx["g2_d"][l], "w")
        S.dma("sp", bsr[:], ctx["bs_d"][l], "small")
        S.dma("sp", pbt[:], ctx["pb_d"][l], "small")

        xt = sb("xt", [128, 8, T])
        hb = sb("hb", [128, 8, T], BF16)
        ycat = sb("ycat", [128, 8, T], BF16)
        p_sb = sb("p_sb", [128, 21, W])
        gx = sb("gx", [128, 4, T])
        sbw = dict(sq=[sb("sq0", [128, T]), sb("sq1", [128, T])], rstd=sb("rstd", [128, T]),
                   sqb=[sb("sqb0", [128, T], BF16), sb("sqb1", [128, T], BF16)],
                   g_sq=sb("g_sq", [128, T]), g_th=sb("g_th", [128, T]))
        if first:
            sbw["xtok"] = sb("xtok", [128, NBLK, D])
        wsTf = TA(xt.h[:, 0:2, :].rearrange("p c t -> p (c t)"), (xt.name, ""))
        S.dma("sp", wsTf, ctx["wsT_d"][l], "small")
        for h in range(4):
            S.tt("dve", wsTt[:, h * 128:(h + 1) * 128],
                 TA(wsTf.ap[:, h * 128:(h + 1) * 128], (xt.name, "")), mask4[:, 128:256], ALU.mult)
        S.memset("pool", p_sb[:], 0.0)
        bs_bc = [sb(f"bs_bc{c}", [128, T]) for c in range(2)]
        for c in range(2):
            pbs = nps()
            for hh in range(2):
                h = 2 * c + hh
                for n in range(NBLK):
                    S.mm(pbs[64 * hh:64 * hh + 64, n * 128:(n + 1) * 128], onesr[0:1, 0:64],
                         bsr[0:1, h * 128:(h + 1) * 128], start=True, stop=True)
            S.copy("dve", bs_bc[c][:], pbs[:, 0:T])

        dp = sb("dp", [128, 64])
        S.ts("dve", dp[:, 0:8], modt[:, 8:16], 1.0, None, ALU.add)
        S.tt("dve", dp[:, 0:8], dp[:, 0:8], pvc(l, "nmix"), ALU.mult)
        S.ts("dve", dp[:, 8:11], pvc(l, "lba"), 0.5, None, ALU.mult)
        S.ts("dve", dp[:, 11:14], pvc(l, "lbx"), 0.5, None, ALU.mult)
        S.act(dp[:, 14:17], pvc(l, "llam"), AF.Exp, scale=-1.0)
        S.act(dp[:, 14:17], dp[:, 14:17], AF.Ln, bias=1.0)
        S.ts("dve", dp[:, 17:20], dp[:, 14:17], -4.0, None, ALU.mult)
        S.ts("dve", dp[:, 14:17], dp[:, 14:17], -8.0, None, ALU.mult)
        S.ts("dve", dp[:, 20:31], pvc(l, "mu"), -1.0, 1.0, ALU.mult, ALU.add)
        S.ts("dve", dp[:, 31:34], pvc(l, "w0"), 0.5, None, ALU.mult)
        S.ts("dve", dp[:, 34:37], pvc(l, "a0"), 0.5, None, ALU.mult)
        S.ts("dve", dp[:, 37:40], pvc(l, "ka"), 0.5, None, ALU.mult)
        S.ts("dve", dp[:, 40:43], pvc(l, "ka"), -0.5, 1.0, ALU.mult, ALU.add)

        def dpc(o, j=0):
            return dp[:, o + j:o + j + 1]

        hstate = sb("hstate", [128, 3])
        S.memset("dve", hstate[:], 0.0)
        sT = [[sb(f"sT{p}_{b}", [128, 64]) for b in range(2)] for p in range(3)]
        for p in range(3):
            S.memset("dve", sT[p][0][:], 0.0)

        def wt(name, dt=F32, shape=None):
            return sb(name, shape or [128, T], dt)

        xcb = wt("xcb", BF16)
        st6 = sb("st6", [128, 6]); mv2 = sb("mv2", [128, 2]); vr = sb("vr", [128, 1])
        vn = sb("vn", [128, 256]); vtok = sb("vtok", [128, 256], BF16)
        xwa = wt("xwa", BF16); sgx = wt("sgx", BF16); mixtmp = wt("mixtmp")
        PMV = [wt(f"pmv{p}") for p in range(3)]
        E = [wt(f"E{p}") for p in range(3)]
        BT = [wt(f"BT{p}") for p in range(3)]
        KT = [wt(f"KT{p}") for p in range(3)]
        RK = [wt(f"RK{p}", BF16) for p in range(3)]
        BTb = [wt(f"BTb{p}", BF16) for p in range(3)]
        KTb = [wt(f"KTb{p}", BF16) for p in range(3)]
        ARb = [sb(f"ARb{p}", [128, NBLK, 2, 128], BF16) for p in range(3)]

        AR = [sb(f"AR{p}", [128, NBLK, 2, 128]) for p in range(3)]
        pmr = wt("pmr"); pmk = wt("pmk"); tw = wt("tw"); ta = wt("ta"); lw = wt("lw"); cum = wt("cum")
        Ei = wt("Ei"); Ep = wt("Ep"); rn = wt("rn"); kkn = wt("kkn"); ff = wt("ff"); kp = wt("kp"); t1 = wt("t1")
        xc = kp; tr_ = tw; ti_ = ta
        gsb = [sb(f"gsb{n}", [128, 384]) for n in range(NBLK)]
        tok3 = [sb(f"tok3_{p}", [128, 3, 128], BF16) for p in range(3)]
        AM = [sb(f"AM{p}", [128, 512], BF16) for p in range(3)]
        PM = [[sb(f"PM{p}_{b}", [128, 256]) for b in range(2)] for p in range(3)]
        Q = [[sb(f"Q{p}_{b}", [128, 128]) for b in range(2)] for p in range(3)]
        X = [sb(f"X{p}", [128, 64]) for p in range(3)]
        U = [sb(f"U{p}", [128, 64], BF16) for p in range(3)]
        On = [sb(f"On{p}", [128, 128]) for p in range(3)]
        bon = [sb(f"bon{p}", [128, 2]) for p in range(3)]
        stc = [sb(f"stc{p}", [128, 2, 6]) for p in range(3)]
        mvc = [sb(f"mvc{p}", [128, 2, 2]) for p in range(3)]
        rsc = [sb(f"rsc{p}", [128, 2]) for p in range(3)]
        yct = [sb(f"yct{p}", [128, 128]) for p in range(3)]

        def P(c, lo=3, hi=None):
            hi = W if hi is None else hi
            return p_sb.s(c, (ALL, c, slice(lo, hi)))

        def mix(c, out):
            j = c - 10
            S.act(mixtmp[:], P(c), AF.Identity, scale=dpc(20, j))
            S.stt(out, P(c, 2, 2 + T), pvc(l, "mu", j), mixtmp[:], ALU.mult, ALU.add)

        rn3 = [wt(f"rn3_{p}") for p in range(3)]
        la = [lw, t1, rn]
        lm = [cum, pmr, wt("lm2")]
        lg = [Ei, Ep, wt("lg2")]
        hs_ = kkn
        gy = ff

        def lru_part1():
            for c in range(3):
                xr = 4 + c
                S.ts("dve", xc[:], P(xr, 0, T), pvc(l, "lcw", c * 4 + 0), pvc(l, "lcb", c), ALU.mult, ALU.add)
                for j in range(1, 4):
                    S.stt(xc[:], P(xr, j, j + T), pvc(l, "lcw", c * 4 + j), xc[:], ALU.mult, ALU.add)
                S.copy("pool", xcb[:], xc[:])
                pr = nps()
                pi_ = nps()
                S.mm(pr[:, 0:T], lruWt[:, c * 128:(c + 1) * 128], xcb[:])
                S.mm(pi_[:, 0:T], lruWt[:, 384 + c * 128:384 + (c + 1) * 128], xcb[:])
                S.act(tr_[:], pr[:, 0:T], AF.Tanh, bias=dpc(8, c), scale=0.5)
                S.act(ti_[:], pi_[:, 0:T], AF.Tanh, bias=dpc(11, c), scale=0.5)
                S.act(la[c][:], tr_[:], AF.Exp, bias=dpc(17, c), scale=dpc(17, c))
                S.act(lm[c][:], tr_[:], AF.Exp, bias=dpc(14, c), scale=dpc(14, c))
                S.ts("dve", lm[c][:], lm[c][:], -1.0, 1.0, ALU.mult, ALU.add)
                S.ts("dve", lm[c][:], lm[c][:], 0.0, None, ALU.max)
                S.stt(lg[c][:], ti_[:], 1.0, xc[:], ALU.add, ALU.mult)

        ddv = dd.rearrange("(c p) t -> p c t", p=128)
        sdv = sd.rearrange("(c p) t -> p c t", p=128) if not first else None
        for i in range(ctx["n_tiles"]):
            if first or i == 0:
                ctx["load_x"](first, xt, sn, sd, i, sbw)
            ctx["rms_to"](sbw, xt, lambda c: dp[:, c:c + 1], lambda c: modt[:, c:c + 1], hb, "pool")
            order = list(range(11, 21)) + [10] + list(range(4, 10)) + list(range(0, 4))
            for oc in order:
                pp = nps()
                for k in range(8):
                    if oc == 10:
                        wap = wint[:, k, oc * 128:(oc + 1) * 128]
                    else:
                        wap = wint.s(f"{k}_{1 if oc > 10 else 0}", (ALL, k, slice(oc * 128, (oc + 1) * 128)))
                    S.mm(pp[:, 0:T], wap, hb.s(k, (ALL, k, ALL)), start=(k == 0), stop=(k == 7))
                S.copy("act", P(oc), pp[:, 0:T])

            mix(19, xwa[:])
            S.act(xwa[0:64, :], xwa[0:64, :], AF.Tanh)
            mix(20, mixtmp[:])
            S.act(sgx[:], mixtmp[:], AF.Tanh, scale=0.5)
            S.ts("pool", sgx[:], sgx[:], 0.5, 0.5, ALU.mult, ALU.add)
            for n in range(NBLK):
                pg = nps()
                S.mm(pg[:, 0:384], sgx[:, n * 128:(n + 1) * 128], g2t[:])
                S.copy("act", gsb[n][:], pg[:, 0:384])

            lru_part1()
            for p in range(3):
                mix(13 + p, pmk[:])
                S.act(xcb[:], pmk[:], AF.Square, scale=pvc(l, "kk", p))
                S.mm(ctx["ps"][5 + p][:, 0:T], bonesb[:], xcb[:])
            for p in range(3):
                S.act(rn3[p][:], ctx["ps"][5 + p][:, 0:T], AF.Sqrt)
            for c in range(3):
                S.act(lm[c][:], lm[c][:], AF.Sqrt)
            for p in range(3):
                S.ts("dve", rn3[p][:], rn3[p][:], 1e-12, None, ALU.max)
                S.recip(rn3[p][:], rn3[p][:])
            for c in range(3):
                S.stt(lg[c][:], lm[c][:], 0.5, lg[c][:], ALU.mult, ALU.mult)
                S.scan(hs_[:], la[c][:], lg[c][:], hstate[:, c:c + 1], ALU.mult, ALU.add)
                S.copy("pool", hstate[:, c:c + 1], hs_[:, T - 1:T])
                gelu2(sbw, gy[:], P(7 + c))
                S.stt(ycat.s(2 + c, (ALL, 2 + c, ALL)), gy[:], 0.5, hs_[:], ALU.mult, ALU.mult)

            for p in range(3):
                rn = rn3[p]
                mix(10 + p, pmr[:])
                mix(13 + p, pmk[:])
                mix(16 + p, PMV[p][:])
                pw = nps()
                pa_ = nps()
                S.mm(pw[:, 0:T], w2a2t[0:64, p * 128:(p + 1) * 128], xwa[0:64, :])
                S.mm(pa_[:, 0:T], w2a2t[64:128, p * 128:(p + 1) * 128], xwa[64:128, :])
                S.act(tw[:], pw[:, 0:T], AF.Tanh, bias=dpc(31, p), scale=0.5)
                S.act(ta[:], pa_[:, 0:T], AF.Tanh, bias=dpc(34, p), scale=0.5)
                S.ts("dve", lw[:], tw[:], 1.0, -0.5 * 0.6065306597126334, ALU.add, ALU.mult)
                S.scan(cum[:], keep[:], lw[:], 0.0, ALU.mult, ALU.add)
                S.act(E[p][:], cum[:], AF.Exp)
                S.act(Ei[:], cum[:], AF.Exp, scale=-1.0)
                S.tt("pool", Ep[:], cum[:], lw[:], ALU.subtract)
                S.act(Ep[:], Ep[:], AF.Exp)
                S.stt(kkn[:], pmk[:], pvc(l, "kk", p), rn[:], ALU.mult, ALU.mult)
                S.ts("dve", ff[:], ta[:], dpc(37, p), dpc(40, p), ALU.mult, ALU.add)
                S.tt("pool", kp[:], pmk[:], ff[:], ALU.mult)
                ar = AR[p]
                a_view = TA(ar.h[:, :, 0, :], (ar.name, ""))
                r_view = TA(ar.h[:, :, 1, :], (ar.name, ""))

                def v3(tile_):
                    return TA(tile_.h[:].rearrange("p (n t) -> p n t", t=128), (tile_.name, ""))

                S.stt(a_view, v3(kkn), -1.0, v3(Ep), ALU.mult, ALU.mult)
                S.tt("dve", r_view, v3(pmr), v3(E[p]), ALU.mult)
                S.stt(t1[:], ta[:], 1.0, kkn[:], ALU.add, ALU.mult)
                S.stt(BT[p][:], t1[:], 0.5, Ei[:], ALU.mult, ALU.mult)
                S.tt("pool", KT[p][:], kp[:], Ei[:], ALU.mult)
                S.stt(RK[p][:], pmr[:], pvc(l, "rk", p), kp[:], ALU.mult, ALU.mult)
                S.copy("pool", BTb[p][:], BT[p][:])
                S.copy("pool", KTb[p][:], KT[p][:])
                S.copy("pool", TA(ARb[p].h[:].rearrange("p n a t -> p (n a t)"), (ARb[p].name, "")),
                       TA(ar.h[:].rearrange("p n a t -> p (n a t)"), (ar.name, "")))

            for c in range(4):
                gelu2(sbw, gx.s(c, (ALL, c, ALL)), P(c))
            psm = [ctx["ps"][5], ctx["ps"][6]]
            for n in range(NBLK):
                cs = slice(n * 128, (n + 1) * 128)
                pt = nps()
                S.tr(pt[:, 0:128], gx.s(2, (ALL, 2, cs)), ident[:])
                S.tr(pt[:, 128:256], gx.s(3, (ALL, 3, cs)), ident[:])
                S.bn_stats(st6[:], pt[:, 0:256])
                S.bn_aggr(mv2[:], st6[:])
                S.ts("dve", vr[:], mv2[:, 1:2], 4.0 * LN_EPS, None, ALU.add)
                S.tt("pool", vr[:], vr[:], negh[:, 0:1], ALU.pow)
                S.ts("dve", vn[:], pt[:, 0:256], mv2[:, 0:1], vr[:], ALU.subtract, ALU.mult)
                S.tt("pool", vn[:], vn[:], pbt[:, 0:256], ALU.mult)
                S.tt("pool", vtok[:], vn[:], pbt[:, 256:512], ALU.add)
                for h in range(4):
                    o = psm[h // 2][64 * (h % 2):64 * (h % 2) + 64, cs]
                    S.mm(o, vtok[:, 64 * h:64 * h + 64], wsTt[:, h * 128:(h + 1) * 128], start=True, stop=True)
            for c in range(2):
                S.tt("dve", vn[:], psm[c][:, 0:T], bs_bc[c][:], ALU.add)
                S.stt(ycat.s(c, (ALL, c, ALL)), vn[:], 0.5, gx.s(c, (ALL, c, ALL)), ALU.mult, ALU.mult)

            def core_unit(p, n):
                g = i * NBLK + n
                par = g % 2
                cs = slice(n * 128, (n + 1) * 128)
                ar = AR[p]
                arb = ARb[p]
                pO = ctx["ps"][5 + p]
                pt = nps()
                S.tr(pt[:, 0:128], BT[p][:, cs], ident[:])
                S.tr(pt[:, 128:256], KT[p][:, cs], ident[:])
                S.tr(pt[:, 256:384], PMV[p][:, cs], ident[:])
                S.copy("act", TA(tok3[p].h[:].rearrange("p a t -> p (a t)"), (tok3[p].name, "")), pt[:, 0:384])
                yield
                for h in range(2):
                    hs = slice(64 * h, 64 * h + 64)
                    arhb = TA(arb.h[hs, n, :, :].rearrange("p a t -> p (a t)"), (arb.name, ""))
                    ahb = TA(arb.h[hs, n, 0, :], (arb.name, ""))
                    ah = TA(ar.h[hs, n, 0, :], (ar.name, ""))
                    rh = TA(ar.h[hs, n, 1, :], (ar.name, ""))
                    s_old = sT[p][par].s(h, (hs, ALL))
                    s_new = sT[p][1 - par].s(h, (hs, ALL))
                    vtk = tok3[p][:, 2, hs]
                    pa = nps()
                    S.mm(pa[:, 0:256], BTb[p][hs, cs], arhb)
                    S.mm(pa[:, 256:512], KTb[p][hs, cs], arhb)
                    S.tt("dve", PM[p][0][:, 0:128], pa[:, 0:128], mask4[:, 0:128], ALU.mult)
                    S.tt("dve", AM[p][:, 128:512], pa[:, 128:512], mask4[:, 128:512], ALU.mult)
                    pb_ = nps()
                    S.mm(pb_[:, 0:128], ahb, BTb[p][hs, cs])
                    S.tt("dve", Q[p][0][:], pb_[:, 0:128], maskL[:], ALU.mult)
                    S.copy("pool", PM[p][0][:, 128:256], ident[:])
                    yield
                    cur = 0
                    for k in range(1, 7):
                        pc = nps()
                        pd = nps()
                        nxt = 1 - cur
                        if k < 6:
                            S.mm(pc[:, 0:256], Q[p][cur][:], PM[p][cur][:, 0:256])
                        else:
                            S.mm(pc[:, 128:256], Q[p][cur][:], PM[p][cur][:, 128:256])
                        S.mm(pd[:, 0:128], PM[p][cur][:, 0:128], Q[p][cur][:])
                        if k < 6:
                            S.copy("act", PM[p][nxt][:, 0:128], pc[:, 0:128])
                        S.tt("dve", PM[p][nxt][:, 128:256], PM[p][cur][:, 128:256], pc[:, 128:256], ALU.add)
                        S.copy("act", Q[p][nxt][:], pd[:, 0:128])
                        cur = nxt
                        yield
                    pc = nps()
                    S.mm(pc[:, 0:128], Q[p][cur][:], PM[p][cur][:, 128:256])
                    S.tt("dve", PM[p][1 - cur][:, 128:256], PM[p][cur][:, 128:256], pc[:, 0:128], ALU.add)
                    Tt = PM[p][1 - cur][:, 128:256]
                    yield
                    px = nps()
                    S.mm(px[:, 0:64], ah, s_old, start=True, stop=False)
                    S.mm(px[:, 0:64], AM[p][:, 256:384], vtk, start=False, stop=True)
                    S.copy("act", X[p][:], px[:, 0:64])
                    yield
                    pu = nps()
                    S.mm(pu[:, 0:64], Tt, X[p][:])
                    S.copy("act", U[p][:], pu[:, 0:64])
                    yield
                    S.mm(pO[:, hs], rh, s_old, start=True, stop=False)
                    S.mm(pO[:, hs], AM[p][:, 128:256], U[p][:], start=False, stop=False)
                    S.mm(pO[:, hs], AM[p][:, 384:512], vtk, start=False, stop=True)
                    pS = nps()
                    S.mm(pS[hs, 0:64], tok3[p][:, 0, hs], U[p][:], start=True, stop=False)
                    S.mm(pS[hs, 0:64], tok3[p][:, 1, hs], vtk, start=False, stop=True)
                    el = E[p][hs, n * 128 + 127:n * 128 + 128]
                    S.ts("pool", s_new, s_old, el, None, ALU.mult)
                    S.stt(s_new, pS[hs, 0:64], el, s_new, ALU.mult, ALU.add)
                    yield
                for h in range(2):
                    hs = slice(64 * h, 64 * h + 64)
                    S.bn_stats(stc[p][:, h, :], pO[:, hs])
                    S.bn_aggr(mvc[p][:, h, :], stc[p][:, h, :])
                S.ts("dve", rsc[p][:], mvc[p][:, :, 1], GN_EPS, None, ALU.add)
                S.tt("pool", rsc[p][:], rsc[p][:], negh[:, 0:2], ALU.pow)
                for h in range(2):
                    hs = slice(64 * h, 64 * h + 64)
                    S.ts("dve", On[p][:, hs], pO[:, hs], mvc[p][:, h, 0:1], rsc[p][:, h:h + 1], ALU.subtract, ALU.mult)
                S.tt("pool", On[p][:], On[p][:], pbt[:, 512 + 128 * p:512 + 128 * (p + 1)], ALU.mult)
                S.tt("pool", On[p][:], On[p][:], pbt[:, 896 + 128 * p:896 + 128 * (p + 1)], ALU.add)
                pbn = nps()
                S.mm(pbn[:, 0:2], RK[p][:, cs], selb[:])
                S.copy("act", bon[p][:], pbn[:, 0:2])
                for h in range(2):
                    hs = slice(64 * h, 64 * h + 64)
                    S.stt(On[p][:, hs], tok3[p][:, 2, hs], bon[p][:, h:h + 1], On[p][:, hs], ALU.mult, ALU.add)
                S.tt("pool", yct[p][:], On[p][:], gsb[n][:, 128 * p:128 * (p + 1)], ALU.mult)
                ptr = nps()
                S.tr(ptr[:, 0:128], yct[p][:], ident[:])
                S.copy("act", ycat.s(5 + p, (ALL, 5 + p, cs)), ptr[:, 0:128])
                yield

            for n in range(NBLK):
                gens = [core_unit(p, n) for p in range(3)]
                while gens:
                    for gnr in list(gens):
                        try:
                            next(gnr)
                        except StopIteration:
                            gens.remove(gnr)

            for m in range(8):
                po = nps()
                for k in range(8):
                    S.mm(po[:, 0:T], woutt.s(f"{k}", (ALL, k, slice(m * 128, (m + 1) * 128))), ycat.s(k, (ALL, k, ALL)),
                         start=(k == 0), stop=(k == 7))
                S.stt(xt.s(m, (ALL, m, ALL)), po[:, 0:T], modt[:, 16 + m:17 + m], xt.s(m, (ALL, m, ALL)),
                      ALU.mult, ALU.add)
                S.dma("sp", TA(ddv[:, m, i * T:(i + 1) * T], (dn, f"{i}_{m}")), xt.s(m, (ALL, m, ALL)), "xout")
                if (not first) and i + 1 < ctx["n_tiles"]:
                    S.dma("sp", xt.s(m, (ALL, m, ALL)), TA(sdv[:, m, (i + 1) * T:(i + 2) * T], (sn, f"{i + 1}_{m}")),
                          "xin")
            hk = [(p_sb.name, str(c)) for c in range(4, 7)]
            a0 = p_sb.h[:, 4:7, 0:3]
            a1 = p_sb.h[:, 4:7, T:T + 3]
            S.op("pool", lambda e, a0=a0, a1=a1: e.tensor_copy(a0, a1), hk, hk)
            hk2 = [(p_sb.name, str(c)) for c in range(10, 21)]
            b0 = p_sb.h[:, 10:21, 0:3]
            b1 = p_sb.h[:, 10:21, T:T + 3]
            S.op("pool", lambda e, b0=b0, b1=b1: e.tensor_copy(b0, b1), hk2, hk2)
        S.wait_all_dma("sp", ["xout"])


_INPUT_ORDER = None


def kernel(**inp):
    inp = {k: np.asarray(v) for k, v in inp.items()}
    pk = pack_params(inp)
    nc = build_program()
    x = np.ascontiguousarray(inp["x"], dtype=np.float32)
    c = np.asarray(inp["c"], np.float32)
    shared = dict(
        w_mod=np.ascontiguousarray(inp["w_mod"], dtype=np.float32),
        w_in=np.ascontiguousarray(inp["w_in"], dtype=np.float32),
        w_out=np.ascontiguousarray(inp["w_out"], dtype=np.float32),
        w_up=np.ascontiguousarray(inp["ffn_w_up"], dtype=np.float32),
        w_dn=np.ascontiguousarray(inp["ffn_w_down"], dtype=np.float32),
        g2=np.ascontiguousarray(inp["rwkv_g2"], dtype=np.float32),
        **pk,
    )
    in_maps = []
    for b in range(NB):
        m = dict(shared)
        m["x"] = x[b]
        m["cT"] = _fm(c[b])
        in_maps.append(m)
    res = run_bass_kernel_spmd(nc, in_maps, core_ids=list(range(NB)))
    out = np.stack([np.asarray(r["y"], dtype=np.float32) for r in res.results], axis=0)
    return out
```

```python
import numpy as np
from contextlib import ExitStack
import concourse.bass as bass
import concourse.mybir as mybir
from concourse.bass_utils import run_bass_kernel_spmd

F32 = mybir.dt.float32
BF16 = mybir.dt.bfloat16
AF = mybir.ActivationFunctionType
ALU = mybir.AluOpType

COMPUTE = ("pe", "dve", "act", "pool")
ALLQ = ("pe", "dve", "act", "pool", "sp")


class TA:
    __slots__ = ("ap", "key")

    def __init__(self, ap, key):
        self.ap = ap
        self.key = key


class Tile:
    def __init__(self, h, name):
        self.h = h
        self.name = name

    def __getitem__(self, idx):
        return TA(self.h[idx], (self.name, ""))

    def s(self, sub, idx):
        return TA(self.h[idx], (self.name, str(sub)))


class Sched:
    def __init__(self, nc, es, n_epochs=8):
        self.nc = nc
        self.es = es
        self.ops = {q: [] for q in ALLQ}
        self.cnt = {e: 0 for e in COMPUTE}
        self.state = {}
        self.seen = {q: {} for q in ALLQ}
        self.epoch = 0
        self.n_epochs = n_epochs
        self.sems = {}
        for e in COMPUTE:
            for ep in range(n_epochs):
                self.sems[(e, ep)] = es.enter_context(nc.semaphore(f"s_{e}_{ep}"))
        self.dma_sem = {}
        self.dma_cnt = {}
        self.need_inc = {}
        self.n_instr = 0

    def _entries(self, key):
        name, sub = key
        d = self.state.setdefault(name, {})
        if sub == "":
            if "" not in d:
                d[""] = [None, []]
            return list(d.values())
        out = []
        if "" in d:
            out.append(d[""])
        if sub not in d:
            d[sub] = [None, []]
        out.append(d[sub])
        return out

    def _own(self, key):
        name, sub = key
        d = self.state.setdefault(name, {})
        if sub not in d:
            d[sub] = [None, []]
        return d[sub]

    def _collect(self, reads, writes, q=None):
        deps = {}

        def add(t):
            if t is None:
                return
            src, idx = t
            if deps.get(src, 0) < idx:
                deps[src] = idx

        for k in reads:
            excl = k[0].startswith("ps")
            for ent in self._entries(k):
                add(ent[0])
                if excl:
                    for r in ent[1]:
                        if r[0] != q:
                            add(r)
        for k in writes:
            for ent in self._entries(k):
                add(ent[0])
                for r in ent[1]:
                    add(r)
        return deps

    def _record(self, me, reads, writes):
        for k in reads:
            self._own(k)[1].append(me)
        for k in writes:
            name, sub = k
            if sub == "":
                self.state[name] = {"": [me, []]}
            else:
                ent = self._own(k)
                ent[0] = me
                ent[1] = []

    def _waits(self, q, deps):
        for src, idx in deps.items():
            if src == q and q == "pe":
                continue
            if self.seen[q].get(src, 0) >= idx:
                continue
            self.seen[q][src] = idx
            if src in COMPUTE:
                self.need_inc.setdefault((src, self.epoch), set()).add(idx)
                self.ops[q].append(("wait", src, self.epoch, idx))
            else:
                idx = self.dma_cnt[src[4:]]
                self.seen[q][src] = idx
                self.ops[q].append(("waitdma", src, idx))

    def op(self, q, fn, reads, writes):
        deps = self._collect(reads, writes, q)
        self._waits(q, deps)
        self.cnt[q] += 1
        me = (q, self.cnt[q])
        self.ops[q].append(("op", fn, self.epoch, self.cnt[q]))
        self._record(me, reads, writes)
        self.n_instr += 1

    def dma(self, q, out, in_, key, **kw):
        if key not in self.dma_sem:
            self.dma_sem[key] = self.es.enter_context(self.nc.semaphore("d_" + key))
            self.dma_cnt[key] = 0
        reads = [in_.key] if isinstance(in_, TA) else []
        writes = [out.key] if isinstance(out, TA) else []
        deps = self._collect(reads, writes)
        self._waits(q, deps)
        self.dma_cnt[key] += 1
        n = self.dma_cnt[key]
        o = out.ap if isinstance(out, TA) else out
        i = in_.ap if isinstance(in_, TA) else in_
        sem = self.dma_sem[key]
        self.ops[q].append(("dma", lambda e: e.dma_start(out=o, in_=i, **kw), sem))
        self._record(("dma:" + key, n), reads, writes)
        self.n_instr += 1

    def barrier(self):
        for q in ALLQ:
            deps = {}
            for e in COMPUTE:
                if self.cnt[e] > 0 and not (e == q == "pe"):
                    deps[e] = self.cnt[e]
            for key, n in self.dma_cnt.items():
                if n > 0:
                    deps["dma:" + key] = n
            self._waits(q, deps)
        self.epoch += 1
        assert self.epoch < self.n_epochs
        self.state = {}
        self.seen = {q: {k: v for k, v in self.seen[q].items() if k not in COMPUTE} for q in ALLQ}
        self.cnt = {e: 0 for e in COMPUTE}

    def wait_all_dma(self, q, keys):
        deps = {"dma:" + k: self.dma_cnt[k] for k in keys if self.dma_cnt.get(k, 0) > 0}
        self._waits(q, deps)

    def flush(self):
        nc = self.nc
        cum = {}
        for (e, ep), idxs in self.need_inc.items():
            s = sorted(idxs)
            cum[(e, ep)] = {idx: i + 1 for i, idx in enumerate(s)}

        def run(q, eng):
            for rec in self.ops[q]:
                kind = rec[0]
                if kind == "op":
                    _, fn, ep, idx = rec
                    ins = fn(eng)
                    c = cum.get((q, ep))
                    if c is not None and idx in c:
                        ins.then_inc(self.sems[(q, ep)], 1)
                elif kind == "wait":
                    _, src, ep, idx = rec
                    eng.wait_ge(self.sems[(src, ep)], cum[(src, ep)][idx])
                elif kind == "waitdma":
                    _, src, idx = rec
                    eng.wait_ge(self.dma_sem[src[4:]], 16 * idx)
                elif kind == "dma":
                    _, fn, sem = rec
                    fn(eng).then_inc(sem, 16)

        with nc.Block() as block:
            @block.sync
            def _(e):
                run("sp", e)

            @block.tensor
            def _(e):
                run("pe", e)

            @block.vector
            def _(e):
                run("dve", e)

            @block.scalar
            def _(e):
                run("act", e)

            @block.gpsimd
            def _(e):
                run("pool", e)
        self.ops = {q: [] for q in ALLQ}

    @staticmethod
    def _k(*tas):
        return [t.key for t in tas if isinstance(t, TA)]

    @staticmethod
    def _v(x):
        return x.ap if isinstance(x, TA) else x

    def mm(self, out, lhsT, rhs, start=True, stop=True):
        o, l, r = out.ap, lhsT.ap, rhs.ap
        self.op("pe", lambda e: e.matmul(o, l, r, start=start, stop=stop),
                self._k(lhsT, rhs), self._k(out))

    def tr(self, out, in_, ident):
        o, i, d = out.ap, in_.ap, ident.ap
        self.op("pe", lambda e: e.transpose(o, i, d), self._k(in_, ident), self._k(out))

    def act(self, out, in_, func, bias=None, scale=None):
        o, i = out.ap, in_.ap
        kw = {}
        if bias is not None:
            kw["bias"] = self._v(bias)
        if scale is not None:
            kw["scale"] = self._v(scale)
        self.op("act", lambda e: e.activation(o, i, func, **kw),
                self._k(in_, bias, scale), self._k(out))

    def ts(self, q, out, in0, s1, s2, op0, op1=None):
        o, i = out.ap, in0.ap
        a, b = self._v(s1), self._v(s2)
        if op1 is None:
            f = lambda e: e.tensor_scalar(o, i, a, None, op0)
        else:
            f = lambda e: e.tensor_scalar(o, i, a, b, op0, op1)
        self.op(q, f, self._k(in0, s1, s2), self._k(out))

    def tt(self, q, out, in0, in1, op):
        o, a, b = out.ap, in0.ap, in1.ap
        self.op(q, lambda e: e.tensor_tensor(o, a, b, op), self._k(in0, in1), self._k(out))

    def stt(self, out, in0, scalar, in1, op0, op1):
        o, a, b = out.ap, in0.ap, in1.ap
        s = self._v(scalar)
        self.op("dve", lambda e: e.scalar_tensor_tensor(o, a, s, b, op0, op1),
                self._k(in0, scalar, in1), self._k(out))

    def scan(self, out, d0, d1, init, op0, op1):
        o, a, b = out.ap, d0.ap, d1.ap
        s = self._v(init)
        self.op("dve", lambda e: e.tensor_tensor_scan(o, a, b, s, op0, op1),
                self._k(d0, d1, init), self._k(out))

    def copy(self, q, out, in_):
        o, i = out.ap, in_.ap
        if q == "act":
            f = lambda e: e.activation(o, i, AF.Copy)
        else:
            f = lambda e: e.tensor_copy(o, i)
        self.op(q, f, self._k(in_), self._k(out))

    def memset(self, q, ta, val):
        a = ta.ap
        self.op(q, lambda e: e.memset(a, val), [], self._k(ta))

    def recip(self, out, in_):
        o, i = out.ap, in_.ap
        self.op("dve", lambda e: e.reciprocal(o, i), self._k(in_), self._k(out))

    def bn_stats(self, out, in_):
        o, i = out.ap, in_.ap
        self.op("dve", lambda e: e.bn_stats(o, i), self._k(in_), self._k(out))

    def bn_aggr(self, out, in_):
        o, i = out.ap, in_.ap
        self.op("dve", lambda e: e.bn_aggr(o, i), self._k(in_), self._k(out))


D = 1024
SEQ = 4096
NB = 8
P_IN = 2688
DFF = 2816
T = 256
TF = 512
NT = SEQ // T
NBLK = T // 128
EPS = 1e-6
LN_EPS = 1e-5
GN_EPS = 64e-5
PV_NAMES = [("bmod", 48), ("nmix", 8), ("nffn", 8), ("lcw", 12), ("lcb", 3), ("lba", 3),
            ("lbx", 3), ("llam", 3), ("mu", 11), ("w0", 3), ("a0", 3), ("kk", 3), ("ka", 3),
            ("rk", 3), ("fcw", 66), ("fcb", 22), ("nfin", 8)]
PV_OFF = {}
_o = 0
for _n, _k in PV_NAMES:
    PV_OFF[_n] = (_o, _k)
    _o += _k
NPV = _o
NPB = 256 + 256 + 384 + 384


def _fm(v):
    return np.ascontiguousarray(np.asarray(v, np.float32).reshape(-1, 128).T)


def pack_params(inp):
    L = 2
    pv = np.zeros((L, 128, NPV), np.float32)
    pb = np.zeros((L, 128, NPB), np.float32)
    wsT = np.zeros((L, 128, 512), np.float32)
    bsrow = np.zeros((L, 1, 512), np.float32)
    lruW = np.zeros((L, 128, 2, 3, 128), np.float32)
    w2a2 = np.zeros((L, 128, 384), np.float32)

    def put(l, name, arr):
        o, k = PV_OFF[name]
        assert arr.shape == (128, k), (name, arr.shape)
        pv[l, :, o:o + k] = arr

    for l in range(L):
        put(l, "bmod", _fm(inp["b_mod"][l]))
        put(l, "nmix", _fm(inp["norm_mix"][l]))
        put(l, "nffn", _fm(inp["norm_ffn"][l]))
        put(l, "lcw", np.asarray(inp["lru_conv_w"][l]).reshape(4, 3, 128).transpose(2, 1, 0).reshape(128, 12))
        put(l, "lcb", _fm(inp["lru_conv_b"][l]))
        put(l, "lba", _fm(inp["lru_b_a"][l]))
        put(l, "lbx", _fm(inp["lru_b_x"][l]))
        put(l, "llam", _fm(inp["lru_lambda"][l]))
        put(l, "mu", _fm(inp["rwkv_mu"][l]))
        put(l, "w0", _fm(inp["rwkv_w0"][l]))
        put(l, "a0", _fm(inp["rwkv_a0"][l]))
        put(l, "kk", _fm(inp["rwkv_k_k"][l]))
        put(l, "ka", _fm(inp["rwkv_k_a"][l]))
        put(l, "rk", _fm(np.asarray(inp["rwkv_r_k"][l]).reshape(-1)))
        put(l, "fcw", np.asarray(inp["ffn_conv_w"][l]).reshape(3, 22, 128).transpose(2, 1, 0).reshape(128, 66))
        put(l, "fcb", _fm(inp["ffn_conv_b"][l]))
        put(l, "nfin", _fm(inp["norm_final"]))
        pb[l, :, 0:256] = np.asarray(inp["sgu_ln_g"][l])[None, :]
        pb[l, :, 256:512] = np.asarray(inp["sgu_ln_b"][l])[None, :]
        pb[l, :, 512:896] = np.asarray(inp["rwkv_ln_w"][l])[None, :]
        pb[l, :, 896:1280] = np.asarray(inp["rwkv_ln_b"][l])[None, :]
        wsT[l] = np.asarray(inp["sgu_w"][l]).transpose(2, 0, 1).reshape(128, 512)
        bsrow[l, 0] = np.asarray(inp["sgu_b"][l]).reshape(512)
        for wi, nm in enumerate(("lru_w_a", "lru_w_x")):
            W = np.asarray(inp[nm][l])
            for c in range(3):
                for hh in range(2):
                    lruW[l, hh * 64:(hh + 1) * 64, wi, c, hh * 64:(hh + 1) * 64] = W[2 * c + hh]
        w2a2[l, 0:64] = np.asarray(inp["rwkv_w2"][l])
        w2a2[l, 64:128] = np.asarray(inp["rwkv_a2"][l])
    return dict(pv=pv, pb=pb, wsT=wsT, bsrow=bsrow, lruW=lruW.reshape(L, 128, 768), w2a2=w2a2)


def build_program(n_tiles=NT, n_pass=4):
    nc = bass.Bass("TRN2", target_bir_lowering=False)

    def din(name, shape):
        return nc.dram_tensor(name, shape, F32, kind="ExternalInput").ap()

    x_d = din("x", [SEQ, D])
    cT_d = din("cT", [128, 8])
    wmod_d = din("w_mod", [2, D, 6 * D])
    win_d = din("w_in", [2, D, P_IN])
    wout_d = din("w_out", [2, D, D])
    wup_d = din("w_up", [2, D, 2 * DFF])
    wdn_d = din("w_dn", [2, DFF, D])
    pv_d = din("pv", [2, 128, NPV])
    pb_d = din("pb", [2, 128, NPB])
    wsT_d = din("wsT", [2, 128, 512])
    bs_d = din("bsrow", [2, 1, 512])
    lruW_d = din("lruW", [2, 128, 768])
    w2a2_d = din("w2a2", [2, 128, 384])
    g2_d = din("g2", [2, 128, 384])
    y_d = nc.dram_tensor("y", [SEQ, D], F32, kind="ExternalOutput").ap()
    xA_d = nc.dram_tensor("xA", [D, SEQ], F32, kind="ExternalOutput").ap()
    xB_d = nc.dram_tensor("xB", [D, SEQ], F32, kind="ExternalOutput").ap()

    def fm_tile(dr, name, i, TT=T):
        return TA(dr.rearrange("(c p) t -> p c t", p=128)[:, :, i * TT:(i + 1) * TT], (name, str(i)))

    with ExitStack() as es0:
        S = Sched(nc, es0)

        mkc = [0]

        def mk(es):
            pre = f"q{mkc[0]}_"
            mkc[0] += 1

            def sb(name, shape, dt=F32):
                nm = pre + name
                return Tile(es.enter_context(nc.sbuf_tensor(nm, shape, dt)), nm)
            return sb

        sb0 = mk(es0)
        ps = [Tile(es0.enter_context(nc.psum_tensor(f"ps{i}", [128, 512], F32)), f"ps{i}") for i in range(8)]
        psi = [0, 8]

        def nps():
            t = ps[psi[0] % psi[1]]
            psi[0] += 1
            return t

        ident = sb0("ident", [128, 128])
        onesm = sb0("onesm", [128, 128])
        bones = sb0("bones", [128, 128])
        sel = sb0("sel", [128, 2])
        mask4 = sb0("mask4", [128, 512])
        maskL = sb0("maskL", [128, 128])
        keep = sb0("keep", [128, T])
        negh = sb0("negh", [128, 8])
        identb = sb0("identb", [128, 128], BF16)
        onesb = sb0("onesb", [128, 128], BF16)
        bonesb = sb0("bonesb", [128, 128], BF16)
        selb = sb0("selb", [128, 2], BF16)
        onesr = sb0("onesr", [1, 64])
        epsc = sb0("epsc", [128, 1])
        pvt = [sb0("pv0", [128, NPV]), sb0("pv1", [128, NPV])]
        modt = [sb0("mod0", [128, 48]), sb0("mod1", [128, 48])]
        c_sb = sb0("c_sb", [128, 8])
        c_act = sb0("c_act", [128, 8])

        def asel(ta, pattern, cmp, cm):
            a = ta.ap
            S.op("pool", lambda e: e.affine_select(a, a, pattern, cmp, 0.0, base=0, channel_multiplier=cm),
                 [ta.key], [ta.key])

        S.memset("pool", ident[:], 1.0)
        asel(ident[:], [[-1, 128]], ALU.is_equal, 1)
        S.memset("dve", onesm[:], 1.0 / D)
        S.memset("dve", bones[:], 0.0)
        S.memset("dve", bones[0:64, 0:64], 1.0)
        S.memset("dve", bones[64:128, 64:128], 1.0)
        S.memset("dve", sel[:], 0.0)
        S.memset("dve", sel[0:64, 0:1], 1.0)
        S.memset("dve", sel[64:128, 1:2], 1.0)
        S.memset("pool", mask4[:], 1.0)
        S.memset("pool", maskL[:], 1.0)
        for b in range(4):
            asel(mask4[:, b * 128:(b + 1) * 128], [[1, 128]], ALU.is_gt if b % 2 == 0 else ALU.is_ge, -1)
        asel(maskL[:], [[-1, 128]], ALU.is_gt, 1)
        S.memset("dve", keep[:], 1.0)
        for b in range(NBLK):
            S.memset("dve", keep[:, b * 128:b * 128 + 1], 0.0)
        S.memset("pool", negh[:], -0.5)
        S.copy("pool", identb[:], ident[:])
        S.copy("pool", onesb[:], onesm[:])
        S.copy("pool", bonesb[:], bones[:])
        S.copy("pool", selb[:], sel[:])
        S.memset("dve", onesr[:], 1.0)
        S.memset("dve", epsc[:], EPS)
        for l in range(2):
            S.dma("sp", pvt[l][:], pv_d[l], "small")
        S.dma("sp", c_sb[:], cT_d, "small")
        S.act(c_act[:], c_sb[:], AF.Silu)

        def pvc(l, name, j=None, n=1):
            o, k = PV_OFF[name]
            if j is None:
                return pvt[l][:, o:o + k]
            return pvt[l][:, o + j:o + j + n]

        with ExitStack() as es1:
            sb1 = mk(es1)
            wm = [sb1("wm0", [128, 8, 768]), sb1("wm1", [128, 8, 768])]
            for l in range(2):
                pm = nps()
                for g in range(8):
                    w = wm[g % 2]
                    S.dma("sp", w[:], wmod_d[l].rearrange("(k p) n -> p k n", p=128)[:, :, g * 768:(g + 1) * 768],
                          f"wm{g % 2}")
                    for j in range(6):
                        m = g * 6 + j
                        for kc in range(8):
                            S.mm(pm[:, m:m + 1], w[:, kc, j * 128:(j + 1) * 128], c_act[:, kc:kc + 1],
                                 start=(kc == 0), stop=(kc == 7))
                S.tt("dve", modt[l][:], pm[:, 0:48], pvc(l, "bmod"), ALU.add)
            S.barrier()
            S.flush()

        def rms_to(sbw, xt, gcol, shcol, hout, eng_bias, T=T):
            pss = nps()
            for c in range(8):
                sq = sbw["sqb"][c % 2]
                S.act(sq[:], xt.s(c, (slice(None), c, slice(None))), AF.Square)
                S.mm(pss[:, 0:T], onesb[:], sq[:], start=(c == 0), stop=(c == 7))
            rstd = sbw["rstd"]
            S.act(rstd[:], pss[:, 0:T], AF.Sqrt, bias=epsc[:, 0:1], scale=1.0)
            S.recip(rstd[:], rstd[:])
            for c in range(8):
                tmp = sbw["sq"][c % 2]
                S.stt(tmp[:], xt.s(c, (slice(None), c, slice(None))), gcol(c), rstd[:], ALU.mult, ALU.mult)
                if eng_bias == "act":
                    S.act(hout.s(c, (slice(None), c, slice(None))), tmp[:], AF.Identity, bias=shcol(c), scale=1.0)
                else:
                    S.ts("pool", hout.s(c, (slice(None), c, slice(None))), tmp[:], 1.0, shcol(c), ALU.mult, ALU.add)

        def load_x(l_first, xt, src_name, src_d, i, sbw):
            if l_first:
                xtok = sbw["xtok"]
                for b in range(NBLK):
                    S.dma("sp", xtok.s(b, (slice(None), b, slice(None))),
                          x_d[i * T + b * 128:i * T + (b + 1) * 128, :], "xin")
                for c in range(8):
                    pt = nps()
                    for b in range(NBLK):
                        S.tr(pt[:, b * 128:(b + 1) * 128], xtok.s(b, (slice(None), b, slice(c * 128, (c + 1) * 128))),
                             ident[:])
                    S.copy("act" if c % 2 else "dve", xt.s(c, (slice(None), c, slice(None))), pt[:, 0:T])
            else:
                S.dma("sp", xt[:], fm_tile(src_d, src_name, i), "xin")

        def gelu2(sbw, out, P):
            sqt = sbw["g_sq"]
            th = sbw["g_th"]
            S.act(sqt[:], P, AF.Square)
            S.ts("pool", sqt[:], sqt[:], 0.044715, 1.0, ALU.mult, ALU.add)
            S.tt("pool", sqt[:], sqt[:], P, ALU.mult)
            S.act(th[:], sqt[:], AF.Tanh, scale=0.7978845608028654)
            S.stt(out, th[:], 1.0, P, ALU.add, ALU.mult)

        ctx = dict(nc=nc, S=S, mk=mk, nps=nps, ps=ps, psi=psi, epsc=epsc, ident=ident, onesm=onesm, bones=bones, sel=sel, mask4=mask4,
                   maskL=maskL, keep=keep, negh=negh, identb=identb, onesb=onesb, bonesb=bonesb, selb=selb, onesr=onesr, pvc=pvc, modt=modt,
                   rms_to=rms_to, load_x=load_x, gelu2=gelu2, fm_tile=fm_tile, n_tiles=n_tiles,
                   win_d=win_d, wout_d=wout_d, wup_d=wup_d, wdn_d=wdn_d, pb_d=pb_d, wsT_d=wsT_d, bs_d=bs_d,
                   lruW_d=lruW_d, w2a2_d=w2a2_d, g2_d=g2_d, y_d=y_d)

        plan = [("mix", 0, None, None, "xA", xA_d), ("ffn", 0, "xA", xA_d, "xB", xB_d),
                ("mix", 1, "xB", xB_d, "xA", xA_d), ("ffn", 1, "xA", xA_d, None, None)]
        for pi, (kind, l, sn, sd, dn, dd) in enumerate(plan[:n_pass]):
            psi[1] = 5 if kind == "mix" else 8
            if kind == "mix":
                mixer_pass(ctx, l, sn, sd, dn, dd)
            else:
                ffn_pass(ctx, l, sn, sd, dn, dd, final=(l == 1))
            S.barrier()
            S.flush()
    return nc


def _sl(*a):
    return tuple(a)


ALL = slice(None)


def ffn_pass(ctx, l, sn, sd, dn, dd, final):
    S = ctx["S"]
    nps = ctx["nps"]
    pvc = ctx["pvc"]
    ident = ctx["ident"]
    negh = ctx["negh"]
    onesm = ctx["onesm"]
    modt = ctx["modt"][l]
    T = TF
    NBLK = T // 128
    with ExitStack() as es:
        sb = ctx["mk"](es)
        wup = sb("wup", [128, 8, 2 * DFF], BF16)
        wdn = sb("wdn", [128, 22, D], BF16)
        wu_src = ctx["wup_d"][l].rearrange("(k p) n -> p k n", p=128)
        wd_src = ctx["wdn_d"][l].rearrange("(k p) n -> p k n", p=128)
        for grp, qs in (("wA", (0, 2)), ("wB", (1, 3))):
            for k in range(8):
                for q in qs:
                    cs = slice(q * 1408, (q + 1) * 1408)
                    S.dma("pool", wup.s(f"{k}_{q}", (ALL, k, cs)), wu_src[:, k, cs], grp)
                if k % 4 == 3:
                    S.wait_all_dma("pool", [grp])
        for k in range(22):
            S.dma("pool", wdn.s(f"{k}", (ALL, k, ALL)), wd_src[:, k, :], "wD")
            if k % 8 == 7:
                S.wait_all_dma("pool", ["wD"])
        xt = sb("xt", [128, 8, T])
        hb = sb("hb", [128, 8, T], BF16)
        hid = sb("hid", [128, 22, T], BF16)
        raw = [sb(f"raw{j}", [128, 2 + T]) for j in range(2)]
        halo = sb("halo", [128, 22, 2])
        accs = [sb(f"acc{j}", [128, T]) for j in range(2)]
        sls = [sb(f"sl{j}", [128, T]) for j in range(2)]
        sbw = dict(sq=accs, rstd=sb("rstd", [128, T]),
                   sqb=[sb("sqb0", [128, T], BF16), sb("sqb1", [128, T], BF16)])
        g2v = sb("g2v", [128, 8])
        if final:
            fbuf = xt
            ytok = sb("ytok", [128, D])
        S.memset("pool", halo[:], 0.0)
        S.ts("dve", g2v[:], modt[:, 32:40], 1.0, None, ALU.add)
        S.tt("dve", g2v[:], g2v[:], pvc(l, "nffn"), ALU.mult)

        def fcw(j, t):
            return pvc(l, "fcw", j * 3 + t)

        n_ft = ctx["n_tiles"] * 256 // T
        sdv = sd.rearrange("(c p) t -> p c t", p=128)
        ddv = dd.rearrange("(c p) t -> p c t", p=128) if dd is not None else None
        for i in range(n_ft):
            if final or i == 0:
                S.dma("sp", xt[:], ctx["fm_tile"](sd, sn, i, T), "xin")
            ctx["rms_to"](sbw, xt, lambda c: g2v[:, c:c + 1], lambda c: modt[:, 24 + c:25 + c], hb, "act", T)
            for j in range(22):
                pg = nps()
                pv_ = nps()
                qg = 0 if j < 11 else 1
                for k in range(8):
                    S.mm(pg[:, 0:T], wup.s(f"{k}_{qg}", (ALL, k, slice(j * 128, (j + 1) * 128))),
                         hb.s(k, (ALL, k, ALL)), start=(k == 0), stop=(k == 7))
                for k in range(8):
                    S.mm(pv_[:, 0:T], wup.s(f"{k}_{qg + 2}", (ALL, k, slice(DFF + j * 128, DFF + (j + 1) * 128))),
                         hb.s(k, (ALL, k, ALL)), start=(k == 0), stop=(k == 7))
                r = raw[j % 2]
                S.copy("act", r.s("h", (ALL, slice(0, 2))), halo.s(j, (ALL, j, ALL)))
                S.copy("act", r.s("d", (ALL, slice(2, 2 + T))), pg[:, 0:T])
                S.copy("act", halo.s(j, (ALL, j, ALL)), r.s("d", (ALL, slice(T, T + 2))))
                acc = accs[j % 2]
                sl = sls[j % 2]
                S.ts("dve", acc[:], r[:, 0:T], fcw(j, 0), pvc(l, "fcb", j), ALU.mult, ALU.add)
                S.stt(acc[:], r[:, 1:T + 1], fcw(j, 1), acc[:], ALU.mult, ALU.add)
                S.stt(acc[:], r[:, 2:T + 2], fcw(j, 2), acc[:], ALU.mult, ALU.add)
                S.act(sl[:], acc[:], AF.Silu)
                S.tt("dve", hid.s(j, (ALL, j, ALL)), sl[:], pv_[:, 0:T], ALU.mult)
            for m in range(8):
                po = nps()
                for k in range(22):
                    S.mm(po[:, 0:T], wdn.s(f"{k}", (ALL, k, slice(m * 128, (m + 1) * 128))), hid.s(k, (ALL, k, ALL)),
                         start=(k == 0), stop=(k == 21))
                S.stt(xt.s(m, (ALL, m, ALL)), po[:, 0:T], modt[:, 40 + m:41 + m], xt.s(m, (ALL, m, ALL)),
                      ALU.mult, ALU.add)
                if not final:
                    S.dma("sp", TA(ddv[:, m, i * T:(i + 1) * T], (dn, f"{i}_{m}")), xt.s(m, (ALL, m, ALL)), "xout")
                    if i + 1 < n_ft:
                        S.dma("sp", xt.s(m, (ALL, m, ALL)), TA(sdv[:, m, (i + 1) * T:(i + 2) * T], (sn, f"{i + 1}_{m}")),
                              "xin")
            if not final:
                pass
            else:
                pss = nps()
                for c in range(8):
                    sq = sbw["sqb"][c % 2]
                    S.act(sq[:], xt.s(c, (ALL, c, ALL)), AF.Square)
                    S.mm(pss[:, 0:T], ctx["onesb"][:], sq[:], start=(c == 0), stop=(c == 7))
                rstd = sbw["rstd"]
                S.act(rstd[:], pss[:, 0:T], AF.Sqrt, bias=ctx["epsc"][:, 0:1], scale=1.0)
                S.recip(rstd[:], rstd[:])
                for c in range(8):
                    S.stt(fbuf.s(c, (ALL, c, ALL)), xt.s(c, (ALL, c, ALL)), pvc(l, "nfin", c), rstd[:],
                          ALU.mult, ALU.mult)
                for b in range(NBLK):
                    for cg in range(2):
                        pt = nps()
                        for cc in range(4):
                            c = cg * 4 + cc
                            S.tr(pt[:, cc * 128:(cc + 1) * 128], fbuf.s(c, (ALL, c, slice(b * 128, (b + 1) * 128))),
                                 ident[:])
                        S.copy("act" if cg else "dve", ytok.s(cg, (ALL, slice(cg * 512, (cg + 1) * 512))), pt[:, 0:512])
                    S.dma("sp", ctx["y_d"][i * T + b * 128:i * T + (b + 1) * 128, :], ytok[:], "yout")
        S.wait_all_dma("sp", ["xout", "yout"])


def mixer_pass(ctx, l, sn, sd, dn, dd):
    S = ctx["S"]
    nps = ctx["nps"]
    pvc = ctx["pvc"]
    ident = ctx["ident"]
    negh = ctx["negh"]
    identb = ctx["identb"]
    bonesb = ctx["bonesb"]
    selb = ctx["selb"]
    mask4 = ctx["mask4"]
    maskL = ctx["maskL"]
    keep = ctx["keep"]
    bones = ctx["bones"]
    sel = ctx["sel"]
    onesr = ctx["onesr"]
    modt = ctx["modt"][l]
    gelu2 = ctx["gelu2"]
    first = sn is None
    W = 3 + T
    with ExitStack() as es:
        sb = ctx["mk"](es)
        wint = sb("wint", [128, 8, P_IN], BF16)
        woutt = sb("woutt", [128, 8, D], BF16)
        wi_src = ctx["win_d"][l].rearrange("(k p) n -> p k n", p=128)
        wo_src = ctx["wout_d"][l].rearrange("(k p) n -> p k n", p=128)
        for grp, q in (("wI1", 1), ("wI0", 0)):
            for k in range(8):
                cs = slice(q * 1344, (q + 1) * 1344)
                S.dma("pool", wint.s(f"{k}_{q}", (ALL, k, cs)), wi_src[:, k, cs], grp)
            S.wait_all_dma("pool", [grp])
        for k in range(8):
            S.dma("pool", woutt.s(f"{k}", (ALL, k, ALL)), wo_src[:, k, :], "wO")
        lruWt = sb("lruWt", [128, 768], BF16)
        w2a2t = sb("w2a2t", [128, 384], BF16)
        g2t = sb("g2t", [128, 384], BF16)
        wsTt = sb("wsTt", [128, 512], BF16)
        bsr = sb("bsr", [1, 512])
        pbt = sb("pbt", [128, NPB])
        S.dma("pool", lruWt[:], ctx["lruW_d"][l], "w")
        S.dma("pool", w2a2t[:], ctx["w2a2_d"][l], "w")
        S.dma("pool", g2t[:], ctx["g2_d"][l], "w")
        S.dma("sp", bsr[:], ctx["bs_d"][l], "small")
        S.dma("sp", pbt[:], ctx["pb_d"][l], "small")

        xt = sb("xt", [128, 8, T])
        hb = sb("hb", [128, 8, T], BF16)
        ycat = sb("ycat", [128, 8, T], BF16)
        p_sb = sb("p_sb", [128, 21, W])
        gx = sb("gx", [128, 4, T])
        sbw = dict(sq=[sb("sq0", [128, T]), sb("sq1", [128, T])], rstd=sb("rstd", [128, T]),
                   sqb=[sb("sqb0", [128, T], BF16), sb("sqb1", [128, T], BF16)],
                   g_sq=sb("g_sq", [128, T]), g_th=sb("g_th", [128, T]))
        if first:
            sbw["xtok"] = sb("xtok", [128, NBLK, D])
        wsTf = TA(xt.h[:, 0:2, :].rearrange("p c t -> p (c t)"), (xt.name, ""))
        S.dma("sp", wsTf, ctx["wsT_d"][l], "small")
        for h in range(4):
            S.tt("dve", wsTt[:, h * 128:(h + 1) * 128],
                 TA(wsTf.ap[:, h * 128:(h + 1) * 128], (xt.name, "")), mask4[:, 128:256], ALU.mult)
        S.memset("pool", p_sb[:], 0.0)
        bs_bc = [sb(f"bs_bc{c}", [128, T]) for c in range(2)]
        for c in range(2):
            pbs = nps()
            for hh in range(2):
                h = 2 * c + hh
                for n in range(NBLK):
                    S.mm(pbs[64 * hh:64 * hh + 64, n * 128:(n + 1) * 128], onesr[0:1, 0:64],
                         bsr[0:1, h * 128:(h + 1) * 128], start=True, stop=True)
            S.copy("dve", bs_bc[c][:], pbs[:, 0:T])

        dp = sb("dp", [128, 64])
        S.ts("dve", dp[:, 0:8], modt[:, 8:16], 1.0, None, ALU.add)
        S.tt("dve", dp[:, 0:8], dp[:, 0:8], pvc(l, "nmix"), ALU.mult)
        S.ts("dve", dp[:, 8:11], pvc(l, "lba"), 0.5, None, ALU.mult)
        S.ts("dve", dp[:, 11:14], pvc(l, "lbx"), 0.5, None, ALU.mult)
        S.act(dp[:, 14:17], pvc(l, "llam"), AF.Exp, scale=-1.0)
        S.act(dp[:, 14:17], dp[:, 14:17], AF.Ln, bias=1.0)
        S.ts("dve", dp[:, 17:20], dp[:, 14:17], -4.0, None, ALU.mult)
        S.ts("dve", dp[:, 14:17], dp[:, 14:17], -8.0, None, ALU.mult)
        S.ts("dve", dp[:, 20:31], pvc(l, "mu"), -1.0, 1.0, ALU.mult, ALU.add)
        S.ts("dve", dp[:, 31:34], pvc(l, "w0"), 0.5, None, ALU.mult)
        S.ts("dve", dp[:, 34:37], pvc(l, "a0"), 0.5, None, ALU.mult)
        S.ts("dve", dp[:, 37:40], pvc(l, "ka"), 0.5, None, ALU.mult)
        S.ts("dve", dp[:, 40:43], pvc(l, "ka"), -0.5, 1.0, ALU.mult, ALU.add)

        def dpc(o, j=0):
            return dp[:, o + j:o + j + 1]

        hstate = sb("hstate", [128, 3])
        S.memset("dve", hstate[:], 0.0)
        sT = [[sb(f"sT{p}_{b}", [128, 64]) for b in range(2)] for p in range(3)]
        for p in range(3):
            S.memset("dve", sT[p][0][:], 0.0)

        def wt(name, dt=F32, shape=None):
            return sb(name, shape or [128, T], dt)

        xcb = wt("xcb", BF16)
        st6 = sb("st6", [128, 6]); mv2 = sb("mv2", [128, 2]); vr = sb("vr", [128, 1])
        vn = sb("vn", [128, 256]); vtok = sb("vtok", [128, 256], BF16)
        xwa = wt("xwa", BF16); sgx = wt("sgx", BF16); mixtmp = wt("mixtmp")
        PMV = [wt(f"pmv{p}") for p in range(3)]
        E = [wt(f"E{p}") for p in range(3)]
        BT = [wt(f"BT{p}") for p in range(3)]
        KT = [wt(f"KT{p}") for p in range(3)]
        RK = [wt(f"RK{p}", BF16) for p in range(3)]
        BTb = [wt(f"BTb{p}", BF16) for p in range(3)]
        KTb = [wt(f"KTb{p}", BF16) for p in range(3)]
        ARb = [sb(f"ARb{p}", [128, NBLK, 2, 128], BF16) for p in range(3)]

        AR = [sb(f"AR{p}", [128, NBLK, 2, 128]) for p in range(3)]
        pmr = wt("pmr"); pmk = wt("pmk"); tw = wt("tw"); ta = wt("ta"); lw = wt("lw"); cum = wt("cum")
        Ei = wt("Ei"); Ep = wt("Ep"); rn = wt("rn"); kkn = wt("kkn"); ff = wt("ff"); kp = wt("kp"); t1 = wt("t1")
        xc = kp; tr_ = tw; ti_ = ta
        gsb = [sb(f"gsb{n}", [128, 384]) for n in range(NBLK)]
        tok3 = [sb(f"tok3_{p}", [128, 3, 128], BF16) for p in range(3)]
        AM = [sb(f"AM{p}", [128, 512], BF16) for p in range(3)]
        PM = [[sb(f"PM{p}_{b}", [128, 256]) for b in range(2)] for p in range(3)]
        Q = [[sb(f"Q{p}_{b}", [128, 128]) for b in range(2)] for p in range(3)]
        X = [sb(f"X{p}", [128, 64]) for p in range(3)]
        U = [sb(f"U{p}", [128, 64], BF16) for p in range(3)]
        On = [sb(f"On{p}", [128, 128]) for p in range(3)]
        bon = [sb(f"bon{p}", [128, 2]) for p in range(3)]
        stc = [sb(f"stc{p}", [128, 2, 6]) for p in range(3)]
        mvc = [sb(f"mvc{p}", [128, 2, 2]) for p in range(3)]
        rsc = [sb(f"rsc{p}", [128, 2]) for p in range(3)]
        yct = [sb(f"yct{p}", [128, 128]) for p in range(3)]

        def P(c, lo=3, hi=None):
            hi = W if hi is None else hi
            return p_sb.s(c, (ALL, c, slice(lo, hi)))

        def mix(c, out):
            j = c - 10
            S.act(mixtmp[:], P(c), AF.Identity, scale=dpc(20, j))
            S.stt(out, P(c, 2, 2 + T), pvc(l, "mu", j), mixtmp[:], ALU.mult, ALU.add)

        rn3 = [wt(f"rn3_{p}") for p in range(3)]
        la = [lw, t1, rn]
        lm = [cum, pmr, wt("lm2")]
        lg = [Ei, Ep, wt("lg2")]
        hs_ = kkn
        gy = ff

        def lru_part1():
            for c in range(3):
                xr = 4 + c
                S.ts("dve", xc[:], P(xr, 0, T), pvc(l, "lcw", c * 4 + 0), pvc(l, "lcb", c), ALU.mult, ALU.add)
                for j in range(1, 4):
                    S.stt(xc[:], P(xr, j, j + T), pvc(l, "lcw", c * 4 + j), xc[:], ALU.mult, ALU.add)
                S.copy("pool", xcb[:], xc[:])
                pr = nps()
                pi_ = nps()
                S.mm(pr[:, 0:T], lruWt[:, c * 128:(c + 1) * 128], xcb[:])
                S.mm(pi_[:, 0:T], lruWt[:, 384 + c * 128:384 + (c + 1) * 128], xcb[:])
                S.act(tr_[:], pr[:, 0:T], AF.Tanh, bias=dpc(8, c), scale=0.5)
                S.act(ti_[:], pi_[:, 0:T], AF.Tanh, bias=dpc(11, c), scale=0.5)
                S.act(la[c][:], tr_[:], AF.Exp, bias=dpc(17, c), scale=dpc(17, c))
                S.act(lm[c][:], tr_[:], AF.Exp, bias=dpc(14, c), scale=dpc(14, c))
                S.ts("dve", lm[c][:], lm[c][:], -1.0, 1.0, ALU.mult, ALU.add)
                S.ts("dve", lm[c][:], lm[c][:], 0.0, None, ALU.max)
                S.stt(lg[c][:], ti_[:], 1.0, xc[:], ALU.add, ALU.mult)

        ddv = dd.rearrange("(c p) t -> p c t", p=128)
        sdv = sd.rearrange("(c p) t -> p c t", p=128) if not first else None
        for i in range(ctx["n_tiles"]):
            if first or i == 0:
                ctx["load_x"](first, xt, sn, sd, i, sbw)
            ctx["rms_to"](sbw, xt, lambda c: dp[:, c:c + 1], lambda c: modt[:, c:c + 1], hb, "pool")
            order = list(range(11, 21)) + [10] + list(range(4, 10)) + list(range(0, 4))
            for oc in order:
                pp = nps()
                for k in range(8):
                    if oc == 10:
                        wap = wint[:, k, oc * 128:(oc + 1) * 128]
                    else:
                        wap = wint.s(f"{k}_{1 if oc > 10 else 0}", (ALL, k, slice(oc * 128, (oc + 1) * 128)))
                    S.mm(pp[:, 0:T], wap, hb.s(k, (ALL, k, ALL)), start=(k == 0), stop=(k == 7))
                S.copy("act", P(oc), pp[:, 0:T])

            mix(19, xwa[:])
            S.act(xwa[0:64, :], xwa[0:64, :], AF.Tanh)
            mix(20, mixtmp[:])
            S.act(sgx[:], mixtmp[:], AF.Tanh, scale=0.5)
            S.ts("pool", sgx[:], sgx[:], 0.5, 0.5, ALU.mult, ALU.add)
            for n in range(NBLK):
                pg = nps()
                S.mm(pg[:, 0:384], sgx[:, n * 128:(n + 1) * 128], g2t[:])
                S.copy("act", gsb[n][:], pg[:, 0:384])

            lru_part1()
            for p in range(3):
                mix(13 + p, pmk[:])
                S.act(xcb[:], pmk[:], AF.Square, scale=pvc(l, "kk", p))
                S.mm(ctx["ps"][5 + p][:, 0:T], bonesb[:], xcb[:])
            for p in range(3):
                S.act(rn3[p][:], ctx["ps"][5 + p][:, 0:T], AF.Sqrt)
            for c in range(3):
                S.act(lm[c][:], lm[c][:], AF.Sqrt)
            for p in range(3):
                S.ts("dve", rn3[p][:], rn3[p][:], 1e-12, None, ALU.max)
                S.recip(rn3[p][:], rn3[p][:])
            for c in range(3):
                S.stt(lg[c][:], lm[c][:], 0.5, lg[c][:], ALU.mult, ALU.mult)
                S.scan(hs_[:], la[c][:], lg[c][:], hstate[:, c:c + 1], ALU.mult, ALU.add)
                S.copy("pool", hstate[:, c:c + 1], hs_[:, T - 1:T])
                gelu2(sbw, gy[:], P(7 + c))
                S.stt(ycat.s(2 + c, (ALL, 2 + c, ALL)), gy[:], 0.5, hs_[:], ALU.mult, ALU.mult)

            for p in range(3):
                rn = rn3[p]
                mix(10 + p, pmr[:])
                mix(13 + p, pmk[:])
                mix(16 + p, PMV[p][:])
                pw = nps()
                pa_ = nps()
                S.mm(pw[:, 0:T], w2a2t[0:64, p * 128:(p + 1) * 128], xwa[0:64, :])
                S.mm(pa_[:, 0:T], w2a2t[64:128, p * 128:(p + 1) * 128], xwa[64:128, :])
                S.act(tw[:], pw[:, 0:T], AF.Tanh, bias=dpc(31, p), scale=0.5)
                S.act(ta[:], pa_[:, 0:T], AF.Tanh, bias=dpc(34, p), scale=0.5)
                S.ts("dve", lw[:], tw[:], 1.0, -0.5 * 0.6065306597126334, ALU.add, ALU.mult)
                S.scan(cum[:], keep[:], lw[:], 0.0, ALU.mult, ALU.add)
                S.act(E[p][:], cum[:], AF.Exp)
                S.act(Ei[:], cum[:], AF.Exp, scale=-1.0)
                S.tt("pool", Ep[:], cum[:], lw[:], ALU.subtract)
                S.act(Ep[:], Ep[:], AF.Exp)
                S.stt(kkn[:], pmk[:], pvc(l, "kk", p), rn[:], ALU.mult, ALU.mult)
                S.ts("dve", ff[:], ta[:], dpc(37, p), dpc(40, p), ALU.mult, ALU.add)
                S.tt("pool", kp[:], pmk[:], ff[:], ALU.mult)
                ar = AR[p]
                a_view = TA(ar.h[:, :, 0, :], (ar.name, ""))
                r_view = TA(ar.h[:, :, 1, :], (ar.name, ""))

                def v3(tile_):
                    return TA(tile_.h[:].rearrange("p (n t) -> p n t", t=128), (tile_.name, ""))

                S.stt(a_view, v3(kkn), -1.0, v3(Ep), ALU.mult, ALU.mult)
                S.tt("dve", r_view, v3(pmr), v3(E[p]), ALU.mult)
                S.stt(t1[:], ta[:], 1.0, kkn[:], ALU.add, ALU.mult)
                S.stt(BT[p][:], t1[:], 0.5, Ei[:], ALU.mult, ALU.mult)
                S.tt("pool", KT[p][:], kp[:], Ei[:], ALU.mult)
                S.stt(RK[p][:], pmr[:], pvc(l, "rk", p), kp[:], ALU.mult, ALU.mult)
                S.copy("pool", BTb[p][:], BT[p][:])
                S.copy("pool", KTb[p][:], KT[p][:])
                S.copy("pool", TA(ARb[p].h[:].rearrange("p n a t -> p (n a t)"), (ARb[p].name, "")),
                       TA(ar.h[:].rearrange("p n a t -> p (n a t)"), (ar.name, "")))

            for c in range(4):
                gelu2(sbw, gx.s(c, (ALL, c, ALL)), P(c))
            psm = [ctx["ps"][5], ctx["ps"][6]]
            for n in range(NBLK):
                cs = slice(n * 128, (n + 1) * 128)
                pt = nps()
                S.tr(pt[:, 0:128], gx.s(2, (ALL, 2, cs)), ident[:])
                S.tr(pt[:, 128:256], gx.s(3, (ALL, 3, cs)), ident[:])
                S.bn_stats(st6[:], pt[:, 0:256])
                S.bn_aggr(mv2[:], st6[:])
                S.ts("dve", vr[:], mv2[:, 1:2], 4.0 * LN_EPS, None, ALU.add)
                S.tt("pool", vr[:], vr[:], negh[:, 0:1], ALU.pow)
                S.ts("dve", vn[:], pt[:, 0:256], mv2[:, 0:1], vr[:], ALU.subtract, ALU.mult)
                S.tt("pool", vn[:], vn[:], pbt[:, 0:256], ALU.mult)
                S.tt("pool", vtok[:], vn[:], pbt[:, 256:512], ALU.add)
                for h in range(4):
                    o = psm[h // 2][64 * (h % 2):64 * (h % 2) + 64, cs]
                    S.mm(o, vtok[:, 64 * h:64 * h + 64], wsTt[:, h * 128:(h + 1) * 128], start=True, stop=True)
            for c in range(2):
                S.tt("dve", vn[:], psm[c][:, 0:T], bs_bc[c][:], ALU.add)
                S.stt(ycat.s(c, (ALL, c, ALL)), vn[:], 0.5, gx.s(c, (ALL, c, ALL)), ALU.mult, ALU.mult)

            def core_unit(p, n):
                g = i * NBLK + n
                par = g % 2
                cs = slice(n * 128, (n + 1) * 128)
                ar = AR[p]
                arb = ARb[p]
                pO = ctx["ps"][5 + p]
                pt = nps()
                S.tr(pt[:, 0:128], BT[p][:, cs], ident[:])
                S.tr(pt[:, 128:256], KT[p][:, cs], ident[:])
                S.tr(pt[:, 256:384], PMV[p][:, cs], ident[:])
                S.copy("act", TA(tok3[p].h[:].rearrange("p a t -> p (a t)"), (tok3[p].name, "")), pt[:, 0:384])
                yield
                for h in range(2):
                    hs = slice(64 * h, 64 * h + 64)
                    arhb = TA(arb.h[hs, n, :, :].rearrange("p a t -> p (a t)"), (arb.name, ""))
                    ahb = TA(arb.h[hs, n, 0, :], (arb.name, ""))
                    ah = TA(ar.h[hs, n, 0, :], (ar.name, ""))
                    rh = TA(ar.h[hs, n, 1, :], (ar.name, ""))
                    s_old = sT[p][par].s(h, (hs, ALL))
                    s_new = sT[p][1 - par].s(h, (hs, ALL))
                    vtk = tok3[p][:, 2, hs]
                    pa = nps()
                    S.mm(pa[:, 0:256], BTb[p][hs, cs], arhb)
                    S.mm(pa[:, 256:512], KTb[p][hs, cs], arhb)
                    S.tt("dve", PM[p][0][:, 0:128], pa[:, 0:128], mask4[:, 0:128], ALU.mult)
                    S.tt("dve", AM[p][:, 128:512], pa[:, 128:512], mask4[:, 128:512], ALU.mult)
                    pb_ = nps()
                    S.mm(pb_[:, 0:128], ahb, BTb[p][hs, cs])
                    S.tt("dve", Q[p][0][:], pb_[:, 0:128], maskL[:], ALU.mult)
                    S.copy("pool", PM[p][0][:, 128:256], ident[:])
                    yield
                    cur = 0
                    for k in range(1, 7):
                        pc = nps()
                        pd = nps()
                        nxt = 1 - cur
                        if k < 6:
                            S.mm(pc[:, 0:256], Q[p][cur][:], PM[p][cur][:, 0:256])
                        else:
                            S.mm(pc[:, 128:256], Q[p][cur][:], PM[p][cur][:, 128:256])
                        S.mm(pd[:, 0:128], PM[p][cur][:, 0:128], Q[p][cur][:])
                        if k < 6:
                            S.copy("act", PM[p][nxt][:, 0:128], pc[:, 0:128])
                        S.tt("dve", PM[p][nxt][:, 128:256], PM[p][cur][:, 128:256], pc[:, 128:256], ALU.add)
                        S.copy("act", Q[p][nxt][:], pd[:, 0:128])
                        cur = nxt
                        yield
                    pc = nps()
                    S.mm(pc[:, 0:128], Q[p][cur][:], PM[p][cur][:, 128:256])
                    S.tt("dve", PM[p][1 - cur][:, 128:256], PM[p][cur][:, 128:256], pc[:, 0:128], ALU.add)
                    Tt = PM[p][1 - cur][:, 128:256]
                    yield
                    px = nps()
                    S.mm(px[:, 0:64], ah, s_old, start=True, stop=False)
                    S.mm(px[:, 0:64], AM[p][:, 256:384], vtk, start=False, stop=True)
                    S.copy("act", X[p][:], px[:, 0:64])
                    yield
                    pu = nps()
                    S.mm(pu[:, 0:64], Tt, X[p][:])
                    S.copy("act", U[p][:], pu[:, 0:64])
                    yield
                    S.mm(pO[:, hs], rh, s_old, start=True, stop=False)
                    S.mm(pO[:, hs], AM[p][:, 128:256], U[p][:], start=False, stop=False)
                    S.mm(pO[:, hs], AM[p][:, 384:512], vtk, start=False, stop=True)
                    pS = nps()
                    S.mm(pS[hs, 0:64], tok3[p][:, 0, hs], U[p][:], start=True, stop=False)
                    S.mm(pS[hs, 0:64], tok3[p][:, 1, hs], vtk, start=False, stop=True)
                    el = E[p][hs, n * 128 + 127:n * 128 + 128]
                    S.ts("pool", s_new, s_old, el, None, ALU.mult)
                    S.stt(s_new, pS[hs, 0:64], el, s_new, ALU.mult, ALU.add)
                    yield
                for h in range(2):
                    hs = slice(64 * h, 64 * h + 64)
                    S.bn_stats(stc[p][:, h, :], pO[:, hs])
                    S.bn_aggr(mvc[p][:, h, :], stc[p][:, h, :])
                S.ts("dve", rsc[p][:], mvc[p][:, :, 1], GN_EPS, None, ALU.add)
                S.tt("pool", rsc[p][:], rsc[p][:], negh[:, 0:2], ALU.pow)
                for h in range(2):
                    hs = slice(64 * h, 64 * h + 64)
                    S.ts("dve", On[p][:, hs], pO[:, hs], mvc[p][:, h, 0:1], rsc[p][:, h:h + 1], ALU.subtract, ALU.mult)
                S.tt("pool", On[p][:], On[p][:], pbt[:, 512 + 128 * p:512 + 128 * (p + 1)], ALU.mult)
                S.tt("pool", On[p][:], On[p][:], pbt[:, 896 + 128 * p:896 + 128 * (p + 1)], ALU.add)
                pbn = nps()
                S.mm(pbn[:, 0:2], RK[p][:, cs], selb[:])
                S.copy("act", bon[p][:], pbn[:, 0:2])
                for h in range(2):
                    hs = slice(64 * h, 64 * h + 64)
                    S.stt(On[p][:, hs], tok3[p][:, 2, hs], bon[p][:, h:h + 1], On[p][:, hs], ALU.mult, ALU.add)
                S.tt("pool", yct[p][:], On[p][:], gsb[n][:, 128 * p:128 * (p + 1)], ALU.mult)
                ptr = nps()
                S.tr(ptr[:, 0:128], yct[p][:], ident[:])
                S.copy("act", ycat.s(5 + p, (ALL, 5 + p, cs)), ptr[:, 0:128])
                yield

            for n in range(NBLK):
                gens = [core_unit(p, n) for p in range(3)]
                while gens:
                    for gnr in list(gens):
                        try:
                            next(gnr)
                        except StopIteration:
                            gens.remove(gnr)

            for m in range(8):
                po = nps()
                for k in range(8):
                    S.mm(po[:, 0:T], woutt.s(f"{k}", (ALL, k, slice(m * 128, (m + 1) * 128))), ycat.s(k, (ALL, k, ALL)),
                         start=(k == 0), stop=(k == 7))
                S.stt(xt.s(m, (ALL, m, ALL)), po[:, 0:T], modt[:, 16 + m:17 + m], xt.s(m, (ALL, m, ALL)),
                      ALU.mult, ALU.add)
                S.dma("sp", TA(ddv[:, m, i * T:(i + 1) * T], (dn, f"{i}_{m}")), xt.s(m, (ALL, m, ALL)), "xout")
                if (not first) and i + 1 < ctx["n_tiles"]:
                    S.dma("sp", xt.s(m, (ALL, m, ALL)), TA(sdv[:, m, (i + 1) * T:(i + 2) * T], (sn, f"{i + 1}_{m}")),
                          "xin")
            hk = [(p_sb.name, str(c)) for c in range(4, 7)]
            a0 = p_sb.h[:, 4:7, 0:3]
            a1 = p_sb.h[:, 4:7, T:T + 3]
            S.op("pool", lambda e, a0=a0, a1=a1: e.tensor_copy(a0, a1), hk, hk)
            hk2 = [(p_sb.name, str(c)) for c in range(10, 21)]
            b0 = p_sb.h[:, 10:21, 0:3]
            b1 = p_sb.h[:, 10:21, T:T + 3]
            S.op("pool", lambda e, b0=b0, b1=b1: e.tensor_copy(b0, b1), hk2, hk2)
        S.wait_all_dma("sp", ["xout"])


_INPUT_ORDER = None


def kernel(**inp):
    inp = {k: np.asarray(v) for k, v in inp.items()}
    pk = pack_params(inp)
    nc = build_program()
    x = np.ascontiguousarray(inp["x"], dtype=np.float32)
    c = np.asarray(inp["c"], np.float32)
    shared = dict(
        w_mod=np.ascontiguousarray(inp["w_mod"], dtype=np.float32),
        w_in=np.ascontiguousarray(inp["w_in"], dtype=np.float32),
        w_out=np.ascontiguousarray(inp["w_out"], dtype=np.float32),
        w_up=np.ascontiguousarray(inp["ffn_w_up"], dtype=np.float32),
        w_dn=np.ascontiguousarray(inp["ffn_w_down"], dtype=np.float32),
        g2=np.ascontiguousarray(inp["rwkv_g2"], dtype=np.float32),
        **pk,
    )
    in_maps = []
    for b in range(NB):
        m = dict(shared)
        m["x"] = x[b]
        m["cT"] = _fm(c[b])
        in_maps.append(m)
    res = run_bass_kernel_spmd(nc, in_maps, core_ids=list(range(NB)))
    out = np.stack([np.asarray(r["y"], dtype=np.float32) for r in res.results], axis=0)
    return out
```

```python
import numpy as np
from contextlib import ExitStack
import concourse.bass as bass
import concourse.mybir as mybir
from concourse.bass_utils import run_bass_kernel_spmd

F32 = mybir.dt.float32
BF16 = mybir.dt.bfloat16
AF = mybir.ActivationFunctionType
ALU = mybir.AluOpType

COMPUTE = ("pe", "dve", "act", "pool")
ALLQ = ("pe", "dve", "act", "pool", "sp")


class TA:
    __slots__ = ("ap", "key")

    def __init__(self, ap, key):
        self.ap = ap
        self.key = key


class Tile:
    def __init__(self, h, name):
        self.h = h
        self.name = name

    def __getitem__(self, idx):
        return TA(self.h[idx], (self.name, ""))

    def s(self, sub, idx):
        return TA(self.h[idx], (self.name, str(sub)))


class Sched:
    def __init__(self, nc, es, n_epochs=8):
        self.nc = nc
        self.es = es
        self.ops = {q: [] for q in ALLQ}
        self.cnt = {e: 0 for e in COMPUTE}
        self.state = {}
        self.seen = {q: {} for q in ALLQ}
        self.epoch = 0
        self.n_epochs = n_epochs
        self.sems = {}
        for e in COMPUTE:
            for ep in range(n_epochs):
                self.sems[(e, ep)] = es.enter_context(nc.semaphore(f"s_{e}_{ep}"))
        self.dma_sem = {}
        self.dma_cnt = {}
        self.need_inc = {}
        self.n_instr = 0

    def _entries(self, key):
        name, sub = key
        d = self.state.setdefault(name, {})
        if sub == "":
            if "" not in d:
                d[""] = [None, []]
            return list(d.values())
        out = []
        if "" in d:
            out.append(d[""])
        if sub not in d:
            d[sub] = [None, []]
        out.append(d[sub])
        return out

    def _own(self, key):
        name, sub = key
        d = self.state.setdefault(name, {})
        if sub not in d:
            d[sub] = [None, []]
        return d[sub]

    def _collect(self, reads, writes, q=None):
        deps = {}

        def add(t):
            if t is None:
                return
            src, idx = t
            if deps.get(src, 0) < idx:
                deps[src] = idx

        for k in reads:
            excl = k[0].startswith("ps")
            for ent in self._entries(k):
                add(ent[0])
                if excl:
                    for r in ent[1]:
                        if r[0] != q:
                            add(r)
        for k in writes:
            for ent in self._entries(k):
                add(ent[0])
                for r in ent[1]:
                    add(r)
        return deps

    def _record(self, me, reads, writes):
        for k in reads:
            self._own(k)[1].append(me)
        for k in writes:
            name, sub = k
            if sub == "":
                self.state[name] = {"": [me, []]}
            else:
                ent = self._own(k)
                ent[0] = me
                ent[1] = []

    def _waits(self, q, deps):
        for src, idx in deps.items():
            if src == q and q == "pe":
                continue
            if self.seen[q].get(src, 0) >= idx:
                continue
            self.seen[q][src] = idx
            if src in COMPUTE:
                self.need_inc.setdefault((src, self.epoch), set()).add(idx)
                self.ops[q].append(("wait", src, self.epoch, idx))
            else:
                idx = self.dma_cnt[src[4:]]
                self.seen[q][src] = idx
                self.ops[q].append(("waitdma", src, idx))

    def op(self, q, fn, reads, writes):
        deps = self._collect(reads, writes, q)
        self._waits(q, deps)
        self.cnt[q] += 1
        me = (q, self.cnt[q])
        self.ops[q].append(("op", fn, self.epoch, self.cnt[q]))
        self._record(me, reads, writes)
        self.n_instr += 1

    def dma(self, q, out, in_, key, **kw):
        if key not in self.dma_sem:
            self.dma_sem[key] = self.es.enter_context(self.nc.semaphore("d_" + key))
            self.dma_cnt[key] = 0
        reads = [in_.key] if isinstance(in_, TA) else []
        writes = [out.key] if isinstance(out, TA) else []
        deps = self._collect(reads, writes)
        self._waits(q, deps)
        self.dma_cnt[key] += 1
        n = self.dma_cnt[key]
        o = out.ap if isinstance(out, TA) else out
        i = in_.ap if isinstance(in_, TA) else in_
        sem = self.dma_sem[key]
        self.ops[q].append(("dma", lambda e: e.dma_start(out=o, in_=i, **kw), sem))
        self._record(("dma:" + key, n), reads, writes)
        self.n_instr += 1

    def barrier(self):
        for q in ALLQ:
            deps = {}
            for e in COMPUTE:
                if self.cnt[e] > 0 and not (e == q == "pe"):
                    deps[e] = self.cnt[e]
            for key, n in self.dma_cnt.items():
                if n > 0:
                    deps["dma:" + key] = n
            self._waits(q, deps)
        self.epoch += 1
        assert self.epoch < self.n_epochs
        self.state = {}
        self.seen = {q: {k: v for k, v in self.seen[q].items() if k not in COMPUTE} for q in ALLQ}
        self.cnt = {e: 0 for e in COMPUTE}

    def wait_all_dma(self, q, keys):
        deps = {"dma:" + k: self.dma_cnt[k] for k in keys if self.dma_cnt.get(k, 0) > 0}
        self._waits(q, deps)

    def flush(self):
        nc = self.nc
        cum = {}
        for (e, ep), idxs in self.need_inc.items():
            s = sorted(idxs)
            cum[(e, ep)] = {idx: i + 1 for i, idx in enumerate(s)}

        def run(q, eng):
            for rec in self.ops[q]:
                kind = rec[0]
                if kind == "op":
                    _, fn, ep, idx = rec
                    ins = fn(eng)
                    c = cum.get((q, ep))
                    if c is not None and idx in c:
                        ins.then_inc(self.sems[(q, ep)], 1)
                elif kind == "wait":
                    _, src, ep, idx = rec
                    eng.wait_ge(self.sems[(src, ep)], cum[(src, ep)][idx])
                elif kind == "waitdma":
                    _, src, idx = rec
                    eng.wait_ge(self.dma_sem[src[4:]], 16 * idx)
                elif kind == "dma":
                    _, fn, sem = rec
                    fn(eng).then_inc(sem, 16)

        with nc.Block() as block:
            @block.sync
            def _(e):
                run("sp", e)

            @block.tensor
            def _(e):
                run("pe", e)

            @block.vector
            def _(e):
                run("dve", e)

            @block.scalar
            def _(e):
                run("act", e)

            @block.gpsimd
            def _(e):
                run("pool", e)
        self.ops = {q: [] for q in ALLQ}

    @staticmethod
    def _k(*tas):
        return [t.key for t in tas if isinstance(t, TA)]

    @staticmethod
    def _v(x):
        return x.ap if isinstance(x, TA) else x

    def mm(self, out, lhsT, rhs, start=True, stop=True):
        o, l, r = out.ap, lhsT.ap, rhs.ap
        self.op("pe", lambda e: e.matmul(o, l, r, start=start, stop=stop),
                self._k(lhsT, rhs), self._k(out))

    def tr(self, out, in_, ident):
        o, i, d = out.ap, in_.ap, ident.ap
        self.op("pe", lambda e: e.transpose(o, i, d), self._k(in_, ident), self._k(out))

    def act(self, out, in_, func, bias=None, scale=None):
        o, i = out.ap, in_.ap
        kw = {}
        if bias is not None:
            kw["bias"] = self._v(bias)
        if scale is not None:
            kw["scale"] = self._v(scale)
        self.op("act", lambda e: e.activation(o, i, func, **kw),
                self._k(in_, bias, scale), self._k(out))

    def ts(self, q, out, in0, s1, s2, op0, op1=None):
        o, i = out.ap, in0.ap
        a, b = self._v(s1), self._v(s2)
        if op1 is None:
            f = lambda e: e.tensor_scalar(o, i, a, None, op0)
        else:
            f = lambda e: e.tensor_scalar(o, i, a, b, op0, op1)
        self.op(q, f, self._k(in0, s1, s2), self._k(out))

    def tt(self, q, out, in0, in1, op):
        o, a, b = out.ap, in0.ap, in1.ap
        self.op(q, lambda e: e.tensor_tensor(o, a, b, op), self._k(in0, in1), self._k(out))

    def stt(self, out, in0, scalar, in1, op0, op1):
        o, a, b = out.ap, in0.ap, in1.ap
        s = self._v(scalar)
        self.op("dve", lambda e: e.scalar_tensor_tensor(o, a, s, b, op0, op1),
                self._k(in0, scalar, in1), self._k(out))

    def scan(self, out, d0, d1, init, op0, op1):
        o, a, b = out.ap, d0.ap, d1.ap
        s = self._v(init)
        self.op("dve", lambda e: e.tensor_tensor_scan(o, a, b, s, op0, op1),
                self._k(d0, d1, init), self._k(out))

    def copy(self, q, out, in_):
        o, i = out.ap, in_.ap
        if q == "act":
            f = lambda e: e.activation(o, i, AF.Copy)
        else:
            f = lambda e: e.tensor_copy(o, i)
        self.op(q, f, self._k(in_), self._k(out))

    def memset(self, q, ta, val):
        a = ta.ap
        self.op(q, lambda e: e.memset(a, val), [], self._k(ta))

    def recip(self, out, in_):
        o, i = out.ap, in_.ap
        self.op("dve", lambda e: e.reciprocal(o, i), self._k(in_), self._k(out))

    def bn_stats(self, out, in_):
        o, i = out.ap, in_.ap
        self.op("dve", lambda e: e.bn_stats(o, i), self._k(in_), self._k(out))

    def bn_aggr(self, out, in_):
        o, i = out.ap, in_.ap
        self.op("dve", lambda e: e.bn_aggr(o, i), self._k(in_), self._k(out))


D = 1024
SEQ = 4096
NB = 8
P_IN = 2688
DFF = 2816
T = 256
TF = 512
NT = SEQ // T
NBLK = T // 128
EPS = 1e-6
LN_EPS = 1e-5
GN_EPS = 64e-5
PV_NAMES = [("bmod", 48), ("nmix", 8), ("nffn", 8), ("lcw", 12), ("lcb", 3), ("lba", 3),
            ("lbx", 3), ("llam", 3), ("mu", 11), ("w0", 3), ("a0", 3), ("kk", 3), ("ka", 3),
            ("rk", 3), ("fcw", 66), ("fcb", 22), ("nfin", 8)]
PV_OFF = {}
_o = 0
for _n, _k in PV_NAMES:
    PV_OFF[_n] = (_o, _k)
    _o += _k
NPV = _o
NPB = 256 + 256 + 384 + 384


def _fm(v):
    return np.ascontiguousarray(np.asarray(v, np.float32).reshape(-1, 128).T)


def pack_params(inp):
    L = 2
    pv = np.zeros((L, 128, NPV), np.float32)
    pb = np.zeros((L, 128, NPB), np.float32)
    wsT = np.zeros((L, 128, 512), np.float32)
    bsrow = np.zeros((L, 1, 512), np.float32)
    lruW = np.zeros((L, 128, 2, 3, 128), np.float32)
    w2a2 = np.zeros((L, 128, 384), np.float32)

    def put(l, name, arr):
        o, k = PV_OFF[name]
        assert arr.shape == (128, k), (name, arr.shape)
        pv[l, :, o:o + k] = arr

    for l in range(L):
        put(l, "bmod", _fm(inp["b_mod"][l]))
        put(l, "nmix", _fm(inp["norm_mix"][l]))
        put(l, "nffn", _fm(inp["norm_ffn"][l]))
        put(l, "lcw", np.asarray(inp["lru_conv_w"][l]).reshape(4, 3, 128).transpose(2, 1, 0).reshape(128, 12))
        put(l, "lcb", _fm(inp["lru_conv_b"][l]))
        put(l, "lba", _fm(inp["lru_b_a"][l]))
        put(l, "lbx", _fm(inp["lru_b_x"][l]))
        put(l, "llam", _fm(inp["lru_lambda"][l]))
        put(l, "mu", _fm(inp["rwkv_mu"][l]))
        put(l, "w0", _fm(inp["rwkv_w0"][l]))
        put(l, "a0", _fm(inp["rwkv_a0"][l]))
        put(l, "kk", _fm(inp["rwkv_k_k"][l]))
        put(l, "ka", _fm(inp["rwkv_k_a"][l]))
        put(l, "rk", _fm(np.asarray(inp["rwkv_r_k"][l]).reshape(-1)))
        put(l, "fcw", np.asarray(inp["ffn_conv_w"][l]).reshape(3, 22, 128).transpose(2, 1, 0).reshape(128, 66))
        put(l, "fcb", _fm(inp["ffn_conv_b"][l]))
        put(l, "nfin", _fm(inp["norm_final"]))
        pb[l, :, 0:256] = np.asarray(inp["sgu_ln_g"][l])[None, :]
        pb[l, :, 256:512] = np.asarray(inp["sgu_ln_b"][l])[None, :]
        pb[l, :, 512:896] = np.asarray(inp["rwkv_ln_w"][l])[None, :]
        pb[l, :, 896:1280] = np.asarray(inp["rwkv_ln_b"][l])[None, :]
        wsT[l] = np.asarray(inp["sgu_w"][l]).transpose(2, 0, 1).reshape(128, 512)
        bsrow[l, 0] = np.asarray(inp["sgu_b"][l]).reshape(512)
        for wi, nm in enumerate(("lru_w_a", "lru_w_x")):
            W = np.asarray(inp[nm][l])
            for c in range(3):
                for hh in range(2):
                    lruW[l, hh * 64:(hh + 1) * 64, wi, c, hh * 64:(hh + 1) * 64] = W[2 * c + hh]
        w2a2[l, 0:64] = np.asarray(inp["rwkv_w2"][l])
        w2a2[l, 64:128] = np.asarray(inp["rwkv_a2"][l])
    return dict(pv=pv, pb=pb, wsT=wsT, bsrow=bsrow, lruW=lruW.reshape(L, 128, 768), w2a2=w2a2)


def build_program(n_tiles=NT, n_pass=4):
    nc = bass.Bass("TRN2", target_bir_lowering=False)

    def din(name, shape):
        return nc.dram_tensor(name, shape, F32, kind="ExternalInput").ap()

    x_d = din("x", [SEQ, D])
    cT_d = din("cT", [128, 8])
    wmod_d = din("w_mod", [2, D, 6 * D])
    win_d = din("w_in", [2, D, P_IN])
    wout_d = din("w_out", [2, D, D])
    wup_d = din("w_up", [2, D, 2 * DFF])
    wdn_d = din("w_dn", [2, DFF, D])
    pv_d = din("pv", [2, 128, NPV])
    pb_d = din("pb", [2, 128, NPB])
    wsT_d = din("wsT", [2, 128, 512])
    bs_d = din("bsrow", [2, 1, 512])
    lruW_d = din("lruW", [2, 128, 768])
    w2a2_d = din("w2a2", [2, 128, 384])
    g2_d = din("g2", [2, 128, 384])
    y_d = nc.dram_tensor("y", [SEQ, D], F32, kind="ExternalOutput").ap()
    xA_d = nc.dram_tensor("xA", [D, SEQ], F32, kind="ExternalOutput").ap()
    xB_d = nc.dram_tensor("xB", [D, SEQ], F32, kind="ExternalOutput").ap()

    def fm_tile(dr, name, i, TT=T):
        return TA(dr.rearrange("(c p) t -> p c t", p=128)[:, :, i * TT:(i + 1) * TT], (name, str(i)))

    with ExitStack() as es0:
        S = Sched(nc, es0)

        mkc = [0]

        def mk(es):
            pre = f"q{mkc[0]}_"
            mkc[0] += 1

            def sb(name, shape, dt=F32):
                nm = pre + name
                return Tile(es.enter_context(nc.sbuf_tensor(nm, shape, dt)), nm)
            return sb

        sb0 = mk(es0)
        ps = [Tile(es0.enter_context(nc.psum_tensor(f"ps{i}", [128, 512], F32)), f"ps{i}") for i in range(8)]
        psi = [0, 8]

        def nps():
            t = ps[psi[0] % psi[1]]
            psi[0] += 1
            return t

        ident = sb0("ident", [128, 128])
        onesm = sb0("onesm", [128, 128])
        bones = sb0("bones", [128, 128])
        sel = sb0("sel", [128, 2])
        mask4 = sb0("mask4", [128, 512])
        maskL = sb0("maskL", [128, 128])
        keep = sb0("keep", [128, T])
        negh = sb0("negh", [128, 8])
        identb = sb0("identb", [128, 128], BF16)
        onesb = sb0("onesb", [128, 128], BF16)
        bonesb = sb0("bonesb", [128, 128], BF16)
        selb = sb0("selb", [128, 2], BF16)
        onesr = sb0("onesr", [1, 64])
        epsc = sb0("epsc", [128, 1])
        pvt = [sb0("pv0", [128, NPV]), sb0("pv1", [128, NPV])]
        modt = [sb0("mod0", [128, 48]), sb0("mod1", [128, 48])]
        c_sb = sb0("c_sb", [128, 8])
        c_act = sb0("c_act", [128, 8])

        def asel(ta, pattern, cmp, cm):
            a = ta.ap
            S.op("pool", lambda e: e.affine_select(a, a, pattern, cmp, 0.0, base=0, channel_multiplier=cm),
                 [ta.key], [ta.key])

        S.memset("pool", ident[:], 1.0)
        asel(ident[:], [[-1, 128]], ALU.is_equal, 1)
        S.memset("dve", onesm[:], 1.0 / D)
        S.memset("dve", bones[:], 0.0)
        S.memset("dve", bones[0:64, 0:64], 1.0)
        S.memset("dve", bones[64:128, 64:128], 1.0)
        S.memset("dve", sel[:], 0.0)
        S.memset("dve", sel[0:64, 0:1], 1.0)
        S.memset("dve", sel[64:128, 1:2], 1.0)
        S.memset("pool", mask4[:], 1.0)
        S.memset("pool", maskL[:], 1.0)
        for b in range(4):
            asel(mask4[:, b * 128:(b + 1) * 128], [[1, 128]], ALU.is_gt if b % 2 == 0 else ALU.is_ge, -1)
        asel(maskL[:], [[-1, 128]], ALU.is_gt, 1)
        S.memset("dve", keep[:], 1.0)
        for b in range(NBLK):
            S.memset("dve", keep[:, b * 128:b * 128 + 1], 0.0)
        S.memset("pool", negh[:], -0.5)
        S.copy("pool", identb[:], ident[:])
        S.copy("pool", onesb[:], onesm[:])
        S.copy("pool", bonesb[:], bones[:])
        S.copy("pool", selb[:], sel[:])
        S.memset("dve", onesr[:], 1.0)
        S.memset("dve", epsc[:], EPS)
        for l in range(2):
            S.dma("sp", pvt[l][:], pv_d[l], "small")
        S.dma("sp", c_sb[:], cT_d, "small")
        S.act(c_act[:], c_sb[:], AF.Silu)

        def pvc(l, name, j=None, n=1):
            o, k = PV_OFF[name]
            if j is None:
                return pvt[l][:, o:o + k]
            return pvt[l][:, o + j:o + j + n]

        with ExitStack() as es1:
            sb1 = mk(es1)
            wm = [sb1("wm0", [128, 8, 768]), sb1("wm1", [128, 8, 768])]
            for l in range(2):
                pm = nps()
                for g in range(8):
                    w = wm[g % 2]
                    S.dma("sp", w[:], wmod_d[l].rearrange("(k p) n -> p k n", p=128)[:, :, g * 768:(g + 1) * 768],
                          f"wm{g % 2}")
                    for j in range(6):
                        m = g * 6 + j
                        for kc in range(8):
                            S.mm(pm[:, m:m + 1], w[:, kc, j * 128:(j + 1) * 128], c_act[:, kc:kc + 1],
                                 start=(kc == 0), stop=(kc == 7))
                S.tt("dve", modt[l][:], pm[:, 0:48], pvc(l, "bmod"), ALU.add)
            S.barrier()
            S.flush()

        def rms_to(sbw, xt, gcol, shcol, hout, eng_bias, T=T):
            pss = nps()
            for c in range(8):
                sq = sbw["sqb"][c % 2]
                S.act(sq[:], xt.s(c, (slice(None), c, slice(None))), AF.Square)
                S.mm(pss[:, 0:T], onesb[:], sq[:], start=(c == 0), stop=(c == 7))
            rstd = sbw["rstd"]
            S.act(rstd[:], pss[:, 0:T], AF.Sqrt, bias=epsc[:, 0:1], scale=1.0)
            S.recip(rstd[:], rstd[:])
            for c in range(8):
                tmp = sbw["sq"][c % 2]
                S.stt(tmp[:], xt.s(c, (slice(None), c, slice(None))), gcol(c), rstd[:], ALU.mult, ALU.mult)
                if eng_bias == "act":
                    S.act(hout.s(c, (slice(None), c, slice(None))), tmp[:], AF.Identity, bias=shcol(c), scale=1.0)
                else:
                    S.ts("pool", hout.s(c, (slice(None), c, slice(None))), tmp[:], 1.0, shcol(c), ALU.mult, ALU.add)

        def load_x(l_first, xt, src_name, src_d, i, sbw):
            if l_first:
                xtok = sbw["xtok"]
                for b in range(NBLK):
                    S.dma("sp", xtok.s(b, (slice(None), b, slice(None))),
                          x_d[i * T + b * 128:i * T + (b + 1) * 128, :], "xin")
                for c in range(8):
                    pt = nps()
                    for b in range(NBLK):
                        S.tr(pt[:, b * 128:(b + 1) * 128], xtok.s(b, (slice(None), b, slice(c * 128, (c + 1) * 128))),
                             ident[:])
                    S.copy("act" if c % 2 else "dve", xt.s(c, (slice(None), c, slice(None))), pt[:, 0:T])
            else:
                S.dma("sp", xt[:], fm_tile(src_d, src_name, i), "xin")

        def gelu2(sbw, out, P):
            sqt = sbw["g_sq"]
            th = sbw["g_th"]
            S.act(sqt[:], P, AF.Square)
            S.ts("pool", sqt[:], sqt[:], 0.044715, 1.0, ALU.mult, ALU.add)
            S.tt("pool", sqt[:], sqt[:], P, ALU.mult)
            S.act(th[:], sqt[:], AF.Tanh, scale=0.7978845608028654)
            S.stt(out, th[:], 1.0, P, ALU.add, ALU.mult)

        ctx = dict(nc=nc, S=S, mk=mk, nps=nps, ps=ps, psi=psi, epsc=epsc, ident=ident, onesm=onesm, bones=bones, sel=sel, mask4=mask4,
                   maskL=maskL, keep=keep, negh=negh, identb=identb, onesb=onesb, bonesb=bonesb, selb=selb, onesr=onesr, pvc=pvc, modt=modt,
                   rms_to=rms_to, load_x=load_x, gelu2=gelu2, fm_tile=fm_tile, n_tiles=n_tiles,
                   win_d=win_d, wout_d=wout_d, wup_d=wup_d, wdn_d=wdn_d, pb_d=pb_d, wsT_d=wsT_d, bs_d=bs_d,
                   lruW_d=lruW_d, w2a2_d=w2a2_d, g2_d=g2_d, y_d=y_d)

        plan = [("mix", 0, None, None, "xA", xA_d), ("ffn", 0, "xA", xA_d, "xB", xB_d),
                ("mix", 1, "xB", xB_d, "xA", xA_d), ("ffn", 1, "xA", xA_d, None, None)]
        for pi, (kind, l, sn, sd, dn, dd) in enumerate(plan[:n_pass]):
            psi[1] = 5 if kind == "mix" else 8
            if kind == "mix":
                mixer_pass(ctx, l, sn, sd, dn, dd)
            else:
                ffn_pass(ctx, l, sn, sd, dn, dd, final=(l == 1))
            S.barrier()
            S.flush()
    return nc


def _sl(*a):
    return tuple(a)


ALL = slice(None)


def ffn_pass(ctx, l, sn, sd, dn, dd, final):
    S = ctx["S"]
    nps = ctx["nps"]
    pvc = ctx["pvc"]
    ident = ctx["ident"]
    negh = ctx["negh"]
    onesm = ctx["onesm"]
    modt = ctx["modt"][l]
    T = TF
    NBLK = T // 128
    with ExitStack() as es:
        sb = ctx["mk"](es)
        wup = sb("wup", [128, 8, 2 * DFF], BF16)
        wdn = sb("wdn", [128, 22, D], BF16)
        wu_src = ctx["wup_d"][l].rearrange("(k p) n -> p k n", p=128)
        wd_src = ctx["wdn_d"][l].rearrange("(k p) n -> p k n", p=128)
        for grp, qs in (("wA", (0, 2)), ("wB", (1, 3))):
            for k in range(8):
                for q in qs:
                    cs = slice(q * 1408, (q + 1) * 1408)
                    S.dma("pool", wup.s(f"{k}_{q}", (ALL, k, cs)), wu_src[:, k, cs], grp)
                if k % 4 == 3:
                    S.wait_all_dma("pool", [grp])
        for k in range(22):
            S.dma("pool", wdn.s(f"{k}", (ALL, k, ALL)), wd_src[:, k, :], "wD")
            if k % 8 == 7:
                S.wait_all_dma("pool", ["wD"])
        xt = sb("xt", [128, 8, T])
        hb = sb("hb", [128, 8, T], BF16)
        hid = sb("hid", [128, 22, T], BF16)
        raw = [sb(f"raw{j}", [128, 2 + T]) for j in range(2)]
        halo = sb("halo", [128, 22, 2])
        accs = [sb(f"acc{j}", [128, T]) for j in range(2)]
        sls = [sb(f"sl{j}", [128, T]) for j in range(2)]
        sbw = dict(sq=accs, rstd=sb("rstd", [128, T]),
                   sqb=[sb("sqb0", [128, T], BF16), sb("sqb1", [128, T], BF16)])
        g2v = sb("g2v", [128, 8])
        if final:
            fbuf = xt
            ytok = sb("ytok", [128, D])
        S.memset("pool", halo[:], 0.0)
        S.ts("dve", g2v[:], modt[:, 32:40], 1.0, None, ALU.add)
        S.tt("dve", g2v[:], g2v[:], pvc(l, "nffn"), ALU.mult)

        def fcw(j, t):
            return pvc(l, "fcw", j * 3 + t)

        n_ft = ctx["n_tiles"] * 256 // T
        sdv = sd.rearrange("(c p) t -> p c t", p=128)
        ddv = dd.rearrange("(c p) t -> p c t", p=128) if dd is not None else None
        for i in range(n_ft):
            if final or i == 0:
                S.dma("sp", xt[:], ctx["fm_tile"](sd, sn, i, T), "xin")
            ctx["rms_to"](sbw, xt, lambda c: g2v[:, c:c + 1], lambda c: modt[:, 24 + c:25 + c], hb, "act", T)
            for j in range(22):
                pg = nps()
                pv_ = nps()
                qg = 0 if j < 11 else 1
                for k in range(8):
                    S.mm(pg[:, 0:T], wup.s(f"{k}_{qg}", (ALL, k, slice(j * 128, (j + 1) * 128))),
                         hb.s(k, (ALL, k, ALL)), start=(k == 0), stop=(k == 7))
                for k in range(8):
                    S.mm(pv_[:, 0:T], wup.s(f"{k}_{qg + 2}", (ALL, k, slice(DFF + j * 128, DFF + (j + 1) * 128))),
                         hb.s(k, (ALL, k, ALL)), start=(k == 0), stop=(k == 7))
                r = raw[j % 2]
                S.copy("act", r.s("h", (ALL, slice(0, 2))), halo.s(j, (ALL, j, ALL)))
                S.copy("act", r.s("d", (ALL, slice(2, 2 + T))), pg[:, 0:T])
                S.copy("act", halo.s(j, (ALL, j, ALL)), r.s("d", (ALL, slice(T, T + 2))))
                acc = accs[j % 2]
                sl = sls[j % 2]
                S.ts("dve", acc[:], r[:, 0:T], fcw(j, 0), pvc(l, "fcb", j), ALU.mult, ALU.add)
                S.stt(acc[:], r[:, 1:T + 1], fcw(j, 1), acc[:], ALU.mult, ALU.add)
                S.stt(acc[:], r[:, 2:T + 2], fcw(j, 2), acc[:], ALU.mult, ALU.add)
                S.act(sl[:], acc[:], AF.Silu)
                S.tt("dve", hid.s(j, (ALL, j, ALL)), sl[:], pv_[:, 0:T], ALU.mult)
            for m in range(8):
                po = nps()
                for k in range(22):
                    S.mm(po[:, 0:T], wdn.s(f"{k}", (ALL, k, slice(m * 128, (m + 1) * 128))), hid.s(k, (ALL, k, ALL)),
                         start=(k == 0), stop=(k == 21))
                S.stt(xt.s(m, (ALL, m, ALL)), po[:, 0:T], modt[:, 40 + m:41 + m], xt.s(m, (ALL, m, ALL)),
                      ALU.mult, ALU.add)
                if not final:
                    S.dma("sp", TA(ddv[:, m, i * T:(i + 1) * T], (dn, f"{i}_{m}")), xt.s(m, (ALL, m, ALL)), "xout")
                    if i + 1 < n_ft:
                        S.dma("sp", xt.s(m, (ALL, m, ALL)), TA(sdv[:, m, (i + 1) * T:(i + 2) * T], (sn, f"{i + 1}_{m}")),
                              "xin")
            if not final:
                pass
            else:
                pss = nps()
                for c in range(8):
                    sq = sbw["sqb"][c % 2]
                    S.act(sq[:], xt.s(c, (ALL, c, ALL)), AF.Square)
                    S.mm(pss[:, 0:T], ctx["onesb"][:], sq[:], start=(c == 0), stop=(c == 7))
                rstd = sbw["rstd"]
                S.act(rstd[:], pss[:, 0:T], AF.Sqrt, bias=ctx["epsc"][:, 0:1], scale=1.0)
                S.recip(rstd[:], rstd[:])
                for c in range(8):
                    S.stt(fbuf.s(c, (ALL, c, ALL)), xt.s(c, (ALL, c, ALL)), pvc(l, "nfin", c), rstd[:],
                          ALU.mult, ALU.mult)
                for b in range(NBLK):
                    for cg in range(2):
                        pt = nps()
                        for cc in range(4):
                            c = cg * 4 + cc
                            S.tr(pt[:, cc * 128:(cc + 1) * 128], fbuf.s(c, (ALL, c, slice(b * 128, (b + 1) * 128))),
                                 ident[:])
                        S.copy("act" if cg else "dve", ytok.s(cg, (ALL, slice(cg * 512, (cg + 1) * 512))), pt[:, 0:512])
                    S.dma("sp", ctx["y_d"][i * T + b * 128:i * T + (b + 1) * 128, :], ytok[:], "yout")
        S.wait_all_dma("sp", ["xout", "yout"])


def mixer_pass(ctx, l, sn, sd, dn, dd):
    S = ctx["S"]
    nps = ctx["nps"]
    pvc = ctx["pvc"]
    ident = ctx["ident"]
    negh = ctx["negh"]
    identb = ctx["identb"]
    bonesb = ctx["bonesb"]
    selb = ctx["selb"]
    mask4 = ctx["mask4"]
    maskL = ctx["maskL"]
    keep = ctx["keep"]
    bones = ctx["bones"]
    sel = ctx["sel"]
    onesr = ctx["onesr"]
    modt = ctx["modt"][l]
    gelu2 = ctx["gelu2"]
    first = sn is None
    W = 3 + T
    with ExitStack() as es:
        sb = ctx["mk"](es)
        wint = sb("wint", [128, 8, P_IN], BF16)
        woutt = sb("woutt", [128, 8, D], BF16)
        wi_src = ctx["win_d"][l].rearrange("(k p) n -> p k n", p=128)
        wo_src = ctx["wout_d"][l].rearrange("(k p) n -> p k n", p=128)
        for grp, q in (("wI1", 1), ("wI0", 0)):
            for k in range(8):
                cs = slice(q * 1344, (q + 1) * 1344)
                S.dma("pool", wint.s(f"{k}_{q}", (ALL, k, cs)), wi_src[:, k, cs], grp)
            S.wait_all_dma("pool", [grp])
        for k in range(8):
            S.dma("pool", woutt.s(f"{k}", (ALL, k, ALL)), wo_src[:, k, :], "wO")
        lruWt = sb("lruWt", [128, 768], BF16)
        w2a2t = sb("w2a2t", [128, 384], BF16)
        g2t = sb("g2t", [128, 384], BF16)
        wsTt = sb("wsTt", [128, 512], BF16)
        bsr = sb("bsr", [1, 512])
        pbt = sb("pbt", [128, NPB])
        S.dma("pool", lruWt[:], ctx["lruW_d"][l], "w")
        S.dma("pool", w2a2t[:], ctx["w2a2_d"][l], "w")
        S.dma("pool", g2t[:], ctx["g2_d"][l], "w")
        S.dma("sp", bsr[:], ctx["bs_d"][l], "small")
        S.dma("sp", pbt[:], ctx["pb_d"][l], "small")

        xt = sb("xt", [128, 8, T])
        hb = sb("hb", [128, 8, T], BF16)
        ycat = sb("ycat", [128, 8, T], BF16)
        p_sb = sb("p_sb", [128, 21, W])
        gx = sb("gx", [128, 4, T])
        sbw = dict(sq=[sb("sq0", [128, T]), sb("sq1", [128, T])], rstd=sb("rstd", [128, T]),
                   sqb=[sb("sqb0", [128, T], BF16), sb("sqb1", [128, T], BF16)],
                   g_sq=sb("g_sq", [128, T]), g_th=sb("g_th", [128, T]))
        if first:
            sbw["xtok"] = sb("xtok", [128, NBLK, D])
        wsTf = TA(xt.h[:, 0:2, :].rearrange("p c t -> p (c t)"), (xt.name, ""))
        S.dma("sp", wsTf, ctx["wsT_d"][l], "small")
        for h in range(4):
            S.tt("dve", wsTt[:, h * 128:(h + 1) * 128],
                 TA(wsTf.ap[:, h * 128:(h + 1) * 128], (xt.name, "")), mask4[:, 128:256], ALU.mult)
        S.memset("pool", p_sb[:], 0.0)
        bs_bc = [sb(f"bs_bc{c}", [128, T]) for c in range(2)]
        for c in range(2):
            pbs = nps()
            for hh in range(2):
                h = 2 * c + hh
                for n in range(NBLK):
                    S.mm(pbs[64 * hh:64 * hh + 64, n * 128:(n + 1) * 128], onesr[0:1, 0:64],
                         bsr[0:1, h * 128:(h + 1) * 128], start=True, stop=True)
            S.copy("dve", bs_bc[c][:], pbs[:, 0:T])

        dp = sb("dp", [128, 64])
        S.ts("dve", dp[:, 0:8], modt[:, 8:16], 1.0, None, ALU.add)
        S.tt("dve", dp[:, 0:8], dp[:, 0:8], pvc(l, "nmix"), ALU.mult)
        S.ts("dve", dp[:, 8:11], pvc(l, "lba"), 0.5, None, ALU.mult)
        S.ts("dve", dp[:, 11:14], pvc(l, "lbx"), 0.5, None, ALU.mult)
        S.act(dp[:, 14:17], pvc(l, "llam"), AF.Exp, scale=-1.0)
        S.act(dp[:, 14:17], dp[:, 14:17], AF.Ln, bias=1.0)
        S.ts("dve", dp[:, 17:20], dp[:, 14:17], -4.0, None, ALU.mult)
        S.ts("dve", dp[:, 14:17], dp[:, 14:17], -8.0, None, ALU.mult)
        S.ts("dve", dp[:, 20:31], pvc(l, "mu"), -1.0, 1.0, ALU.mult, ALU.add)
        S.ts("dve", dp[:, 31:34], pvc(l, "w0"), 0.5, None, ALU.mult)
        S.ts("dve", dp[:, 34:37], pvc(l, "a0"), 0.5, None, ALU.mult)
        S.ts("dve", dp[:, 37:40], pvc(l, "ka"), 0.5, None, ALU.mult)
        S.ts("dve", dp[:, 40:43], pvc(l, "ka"), -0.5, 1.0, ALU.mult, ALU.add)

        def dpc(o, j=0):
            return dp[:, o + j:o + j + 1]

        hstate = sb("hstate", [128, 3])
        S.memset("dve", hstate[:], 0.0)
        sT = [[sb(f"sT{p}_{b}", [128, 64]) for b in range(2)] for p in range(3)]
        for p in range(3):
            S.memset("dve", sT[p][0][:], 0.0)

        def wt(name, dt=F32, shape=None):
            return sb(name, shape or [128, T], dt)

        xcb = wt("xcb", BF16)
        st6 = sb("st6", [128, 6]); mv2 = sb("mv2", [128, 2]); vr = sb("vr", [128, 1])
        vn = sb("vn", [128, 256]); vtok = sb("vtok", [128, 256], BF16)
        xwa = wt("xwa", BF16); sgx = wt("sgx", BF16); mixtmp = wt("mixtmp")
        PMV = [wt(f"pmv{p}") for p in range(3)]
        E = [wt(f"E{p}") for p in range(3)]
        BT = [wt(f"BT{p}") for p in range(3)]
        KT = [wt(f"KT{p}") for p in range(3)]
        RK = [wt(f"RK{p}", BF16) for p in range(3)]
        BTb = [wt(f"BTb{p}", BF16) for p in range(3)]
        KTb = [wt(f"KTb{p}", BF16) for p in range(3)]
        ARb = [sb(f"ARb{p}", [128, NBLK, 2, 128], BF16) for p in range(3)]

        AR = [sb(f"AR{p}", [128, NBLK, 2, 128]) for p in range(3)]
        pmr = wt("pmr"); pmk = wt("pmk"); tw = wt("tw"); ta = wt("ta"); lw = wt("lw"); cum = wt("cum")
        Ei = wt("Ei"); Ep = wt("Ep"); rn = wt("rn"); kkn = wt("kkn"); ff = wt("ff"); kp = wt("kp"); t1 = wt("t1")
        xc = kp; tr_ = tw; ti_ = ta
        gsb = [sb(f"gsb{n}", [128, 384]) for n in range(NBLK)]
        tok3 = [sb(f"tok3_{p}", [128, 3, 128], BF16) for p in range(3)]
        AM = [sb(f"AM{p}", [128, 512], BF16) for p in range(3)]
        PM = [[sb(f"PM{p}_{b}", [128, 256]) for b in range(2)] for p in range(3)]
        Q = [[sb(f"Q{p}_{b}", [128, 128]) for b in range(2)] for p in range(3)]
        X = [sb(f"X{p}", [128, 64]) for p in range(3)]
        U = [sb(f"U{p}", [128, 64], BF16) for p in range(3)]
        On = [sb(f"On{p}", [128, 128]) for p in range(3)]
        bon = [sb(f"bon{p}", [128, 2]) for p in range(3)]
        stc = [sb(f"stc{p}", [128, 2, 6]) for p in range(3)]
        mvc = [sb(f"mvc{p}", [128, 2, 2]) for p in range(3)]
        rsc = [sb(f"rsc{p}", [128, 2]) for p in range(3)]
        yct = [sb(f"yct{p}", [128, 128]) for p in range(3)]

        def P(c, lo=3, hi=None):
            hi = W if hi is None else hi
            return p_sb.s(c, (ALL, c, slice(lo, hi)))

        def mix(c, out):
            j = c - 10
            S.act(mixtmp[:], P(c), AF.Identity, scale=dpc(20, j))
            S.stt(out, P(c, 2, 2 + T), pvc(l, "mu", j), mixtmp[:], ALU.mult, ALU.add)

        rn3 = [wt(f"rn3_{p}") for p in range(3)]
        la = [lw, t1, rn]
        lm = [cum, pmr, wt("lm2")]
        lg = [Ei, Ep, wt("lg2")]
        hs_ = kkn
        gy = ff

        def lru_part1():
            for c in range(3):
                xr = 4 + c
                S.ts("dve", xc[:], P(xr, 0, T), pvc(l, "lcw", c * 4 + 0), pvc(l, "lcb", c), ALU.mult, ALU.add)
                for j in range(1, 4):
                    S.stt(xc[:], P(xr, j, j + T), pvc(l, "lcw", c * 4 + j), xc[:], ALU.mult, ALU.add)
                S.copy("pool", xcb[:], xc[:])
                pr = nps()
                pi_ = nps()
                S.mm(pr[:, 0:T], lruWt[:, c * 128:(c + 1) * 128], xcb[:])
                S.mm(pi_[:, 0:T], lruWt[:, 384 + c * 128:384 + (c + 1) * 128], xcb[:])
                S.act(tr_[:], pr[:, 0:T], AF.Tanh, bias=dpc(8, c), scale=0.5)
                S.act(ti_[:], pi_[:, 0:T], AF.Tanh, bias=dpc(11, c), scale=0.5)
                S.act(la[c][:], tr_[:], AF.Exp, bias=dpc(17, c), scale=dpc(17, c))
                S.act(lm[c][:], tr_[:], AF.Exp, bias=dpc(14, c), scale=dpc(14, c))
                S.ts("dve", lm[c][:], lm[c][:], -1.0, 1.0, ALU.mult, ALU.add)
                S.ts("dve", lm[c][:], lm[c][:], 0.0, None, ALU.max)
                S.stt(lg[c][:], ti_[:], 1.0, xc[:], ALU.add, ALU.mult)

        ddv = dd.rearrange("(c p) t -> p c t", p=128)
        sdv = sd.rearrange("(c p) t -> p c t", p=128) if not first else None
        for i in range(ctx["n_tiles"]):
            if first or i == 0:
                ctx["load_x"](first, xt, sn, sd, i, sbw)
            ctx["rms_to"](sbw, xt, lambda c: dp[:, c:c + 1], lambda c: modt[:, c:c + 1], hb, "act")
            order = list(range(11, 21)) + [10] + list(range(4, 10)) + list(range(0, 4))
            for oc in order:
                pp = nps()
                for k in range(8):
                    if oc == 10:
                        wap = wint[:, k, oc * 128:(oc + 1) * 128]
                    else:
                        wap = wint.s(f"{k}_{1 if oc > 10 else 0}", (ALL, k, slice(oc * 128, (oc + 1) * 128)))
                    S.mm(pp[:, 0:T], wap, hb.s(k, (ALL, k, ALL)), start=(k == 0), stop=(k == 7))
                S.copy("act", P(oc), pp[:, 0:T])

            mix(19, xwa[:])
            S.act(xwa[0:64, :], xwa[0:64, :], AF.Tanh)
            mix(20, mixtmp[:])
            S.act(sgx[:], mixtmp[:], AF.Tanh, scale=0.5)
            S.ts("pool", sgx[:], sgx[:], 0.5, 0.5, ALU.mult, ALU.add)
            for n in range(NBLK):
                pg = nps()
                S.mm(pg[:, 0:384], sgx[:, n * 128:(n + 1) * 128], g2t[:])
                S.copy("act", gsb[n][:], pg[:, 0:384])

            lru_part1()
            for p in range(3):
                mix(13 + p, pmk[:])
                S.act(xcb[:], pmk[:], AF.Square, scale=pvc(l, "kk", p))
                S.mm(ctx["ps"][5 + p][:, 0:T], bonesb[:], xcb[:])
            for p in range(3):
                S.act(rn3[p][:], ctx["ps"][5 + p][:, 0:T], AF.Sqrt)
            for c in range(3):
                S.act(lm[c][:], lm[c][:], AF.Sqrt)
            for p in range(3):
                S.ts("dve", rn3[p][:], rn3[p][:], 1e-12, None, ALU.max)
                S.recip(rn3[p][:], rn3[p][:])
            for c in range(3):
                S.stt(lg[c][:], lm[c][:], 0.5, lg[c][:], ALU.mult, ALU.mult)
                S.scan(hs_[:], la[c][:], lg[c][:], hstate[:, c:c + 1], ALU.mult, ALU.add)
                S.copy("pool", hstate[:, c:c + 1], hs_[:, T - 1:T])
                gelu2(sbw, gy[:], P(7 + c))
                S.stt(ycat.s(2 + c, (ALL, 2 + c, ALL)), gy[:], 0.5, hs_[:], ALU.mult, ALU.mult)

            for p in range(3):
                rn = rn3[p]
                mix(10 + p, pmr[:])
                mix(13 + p, pmk[:])
                mix(16 + p, PMV[p][:])
                pw = nps()
                pa_ = nps()
                S.mm(pw[:, 0:T], w2a2t[0:64, p * 128:(p + 1) * 128], xwa[0:64, :])
                S.mm(pa_[:, 0:T], w2a2t[64:128, p * 128:(p + 1) * 128], xwa[64:128, :])
                S.act(tw[:], pw[:, 0:T], AF.Tanh, bias=dpc(31, p), scale=0.5)
                S.act(ta[:], pa_[:, 0:T], AF.Tanh, bias=dpc(34, p), scale=0.5)
                S.ts("dve", lw[:], tw[:], 1.0, -0.5 * 0.6065306597126334, ALU.add, ALU.mult)
                S.scan(cum[:], keep[:], lw[:], 0.0, ALU.mult, ALU.add)
                S.act(E[p][:], cum[:], AF.Exp)
                S.act(Ei[:], cum[:], AF.Exp, scale=-1.0)
                S.tt("pool", Ep[:], cum[:], lw[:], ALU.subtract)
                S.act(Ep[:], Ep[:], AF.Exp)
                S.stt(kkn[:], pmk[:], pvc(l, "kk", p), rn[:], ALU.mult, ALU.mult)
                S.ts("dve", ff[:], ta[:], dpc(37, p), dpc(40, p), ALU.mult, ALU.add)
                S.tt("pool", kp[:], pmk[:], ff[:], ALU.mult)
                ar = AR[p]
                a_view = TA(ar.h[:, :, 0, :], (ar.name, ""))
                r_view = TA(ar.h[:, :, 1, :], (ar.name, ""))

                def v3(tile_):
                    return TA(tile_.h[:].rearrange("p (n t) -> p n t", t=128), (tile_.name, ""))

                S.stt(a_view, v3(kkn), -1.0, v3(Ep), ALU.mult, ALU.mult)
                S.tt("dve", r_view, v3(pmr), v3(E[p]), ALU.mult)
                S.stt(t1[:], ta[:], 1.0, kkn[:], ALU.add, ALU.mult)
                S.stt(BT[p][:], t1[:], 0.5, Ei[:], ALU.mult, ALU.mult)
                S.tt("pool", KT[p][:], kp[:], Ei[:], ALU.mult)
                S.stt(RK[p][:], pmr[:], pvc(l, "rk", p), kp[:], ALU.mult, ALU.mult)
                S.copy("pool", BTb[p][:], BT[p][:])
                S.copy("pool", KTb[p][:], KT[p][:])
                S.copy("pool", TA(ARb[p].h[:].rearrange("p n a t -> p (n a t)"), (ARb[p].name, "")),
                       TA(ar.h[:].rearrange("p n a t -> p (n a t)"), (ar.name, "")))

            for c in range(4):
                gelu2(sbw, gx.s(c, (ALL, c, ALL)), P(c))
            psm = [ctx["ps"][5], ctx["ps"][6]]
            for n in range(NBLK):
                cs = slice(n * 128, (n + 1) * 128)
                pt = nps()
                S.tr(pt[:, 0:128], gx.s(2, (ALL, 2, cs)), ident[:])
                S.tr(pt[:, 128:256], gx.s(3, (ALL, 3, cs)), ident[:])
                S.bn_stats(st6[:], pt[:, 0:256])
                S.bn_aggr(mv2[:], st6[:])
                S.ts("dve", vr[:], mv2[:, 1:2], 4.0 * LN_EPS, None, ALU.add)
                S.tt("pool", vr[:], vr[:], negh[:, 0:1], ALU.pow)
                S.ts("dve", vn[:], pt[:, 0:256], mv2[:, 0:1], vr[:], ALU.subtract, ALU.mult)
                S.tt("pool", vn[:], vn[:], pbt[:, 0:256], ALU.mult)
                S.tt("pool", vtok[:], vn[:], pbt[:, 256:512], ALU.add)
                for h in range(4):
                    o = psm[h // 2][64 * (h % 2):64 * (h % 2) + 64, cs]
                    S.mm(o, vtok[:, 64 * h:64 * h + 64], wsTt[:, h * 128:(h + 1) * 128], start=True, stop=True)
            for c in range(2):
                S.tt("dve", vn[:], psm[c][:, 0:T], bs_bc[c][:], ALU.add)
                S.stt(ycat.s(c, (ALL, c, ALL)), vn[:], 0.5, gx.s(c, (ALL, c, ALL)), ALU.mult, ALU.mult)

            def core_unit(p, n):
                g = i * NBLK + n
                par = g % 2
                cs = slice(n * 128, (n + 1) * 128)
                ar = AR[p]
                arb = ARb[p]
                pO = ctx["ps"][5 + p]
                pt = nps()
                S.tr(pt[:, 0:128], BT[p][:, cs], ident[:])
                S.tr(pt[:, 128:256], KT[p][:, cs], ident[:])
                S.tr(pt[:, 256:384], PMV[p][:, cs], ident[:])
                S.copy("act", TA(tok3[p].h[:].rearrange("p a t -> p (a t)"), (tok3[p].name, "")), pt[:, 0:384])
                yield
                for h in range(2):
                    hs = slice(64 * h, 64 * h + 64)
                    arhb = TA(arb.h[hs, n, :, :].rearrange("p a t -> p (a t)"), (arb.name, ""))
                    ahb = TA(arb.h[hs, n, 0, :], (arb.name, ""))
                    ah = TA(ar.h[hs, n, 0, :], (ar.name, ""))
                    rh = TA(ar.h[hs, n, 1, :], (ar.name, ""))
                    s_old = sT[p][par].s(h, (hs, ALL))
                    s_new = sT[p][1 - par].s(h, (hs, ALL))
                    vtk = tok3[p][:, 2, hs]
                    pa = nps()
                    S.mm(pa[:, 0:256], BTb[p][hs, cs], arhb)
                    S.mm(pa[:, 256:512], KTb[p][hs, cs], arhb)
                    S.tt("dve", PM[p][0][:, 0:128], pa[:, 0:128], mask4[:, 0:128], ALU.mult)
                    S.tt("dve", AM[p][:, 128:512], pa[:, 128:512], mask4[:, 128:512], ALU.mult)
                    pb_ = nps()
                    S.mm(pb_[:, 0:128], ahb, BTb[p][hs, cs])
                    S.tt("dve", Q[p][0][:], pb_[:, 0:128], maskL[:], ALU.mult)
                    S.copy("pool", PM[p][0][:, 128:256], ident[:])
                    yield
                    cur = 0
                    for k in range(1, 7):
                        pc = nps()
                        pd = nps()
                        nxt = 1 - cur
                        if k < 6:
                            S.mm(pc[:, 0:256], Q[p][cur][:], PM[p][cur][:, 0:256])
                        else:
                            S.mm(pc[:, 128:256], Q[p][cur][:], PM[p][cur][:, 128:256])
                        S.mm(pd[:, 0:128], PM[p][cur][:, 0:128], Q[p][cur][:])
                        if k < 6:
                            S.copy("act", PM[p][nxt][:, 0:128], pc[:, 0:128])
                        S.tt("dve", PM[p][nxt][:, 128:256], PM[p][cur][:, 128:256], pc[:, 128:256], ALU.add)
                        S.copy("act", Q[p][nxt][:], pd[:, 0:128])
                        cur = nxt
                        yield
                    pc = nps()
                    S.mm(pc[:, 0:128], Q[p][cur][:], PM[p][cur][:, 128:256])
                    S.tt("dve", PM[p][1 - cur][:, 128:256], PM[p][cur][:, 128:256], pc[:, 0:128], ALU.add)
                    Tt = PM[p][1 - cur][:, 128:256]
                    yield
                    px = nps()
                    S.mm(px[:, 0:64], ah, s_old, start=True, stop=False)
                    S.mm(px[:, 0:64], AM[p][:, 256:384], vtk, start=False, stop=True)
                    S.copy("act", X[p][:], px[:, 0:64])
                    yield
                    pu = nps()
                    S.mm(pu[:, 0:64], Tt, X[p][:])
                    S.copy("act", U[p][:], pu[:, 0:64])
                    yield
                    S.mm(pO[:, hs], rh, s_old, start=True, stop=False)
                    S.mm(pO[:, hs], AM[p][:, 128:256], U[p][:], start=False, stop=False)
                    S.mm(pO[:, hs], AM[p][:, 384:512], vtk, start=False, stop=True)
                    pS = nps()
                    S.mm(pS[hs, 0:64], tok3[p][:, 0, hs], U[p][:], start=True, stop=False)
                    S.mm(pS[hs, 0:64], tok3[p][:, 1, hs], vtk, start=False, stop=True)
                    el = E[p][hs, n * 128 + 127:n * 128 + 128]
                    S.ts("pool", s_new, s_old, el, None, ALU.mult)
                    S.stt(s_new, pS[hs, 0:64], el, s_new, ALU.mult, ALU.add)
                    yield
                for h in range(2):
                    hs = slice(64 * h, 64 * h + 64)
                    S.bn_stats(stc[p][:, h, :], pO[:, hs])
                    S.bn_aggr(mvc[p][:, h, :], stc[p][:, h, :])
                S.ts("dve", rsc[p][:], mvc[p][:, :, 1], GN_EPS, None, ALU.add)
                S.tt("pool", rsc[p][:], rsc[p][:], negh[:, 0:2], ALU.pow)
                for h in range(2):
                    hs = slice(64 * h, 64 * h + 64)
                    S.ts("dve", On[p][:, hs], pO[:, hs], mvc[p][:, h, 0:1], rsc[p][:, h:h + 1], ALU.subtract, ALU.mult)
                S.tt("pool", On[p][:], On[p][:], pbt[:, 512 + 128 * p:512 + 128 * (p + 1)], ALU.mult)
                S.tt("pool", On[p][:], On[p][:], pbt[:, 896 + 128 * p:896 + 128 * (p + 1)], ALU.add)
                pbn = nps()
                S.mm(pbn[:, 0:2], RK[p][:, cs], selb[:])
                S.copy("act", bon[p][:], pbn[:, 0:2])
                for h in range(2):
                    hs = slice(64 * h, 64 * h + 64)
                    S.stt(On[p][:, hs], tok3[p][:, 2, hs], bon[p][:, h:h + 1], On[p][:, hs], ALU.mult, ALU.add)
                S.tt("pool", yct[p][:], On[p][:], gsb[n][:, 128 * p:128 * (p + 1)], ALU.mult)
                ptr = nps()
                S.tr(ptr[:, 0:128], yct[p][:], ident[:])
                S.copy("act", ycat.s(5 + p, (ALL, 5 + p, cs)), ptr[:, 0:128])
                yield

            for n in range(NBLK):
                gens = [core_unit(p, n) for p in range(3)]
                while gens:
                    for gnr in list(gens):
                        try:
                            next(gnr)
                        except StopIteration:
                            gens.remove(gnr)

            for m in range(8):
                po = nps()
                for k in range(8):
                    S.mm(po[:, 0:T], woutt.s(f"{k}", (ALL, k, slice(m * 128, (m + 1) * 128))), ycat.s(k, (ALL, k, ALL)),
                         start=(k == 0), stop=(k == 7))
                S.stt(xt.s(m, (ALL, m, ALL)), po[:, 0:T], modt[:, 16 + m:17 + m], xt.s(m, (ALL, m, ALL)),
                      ALU.mult, ALU.add)
                S.dma("sp", TA(ddv[:, m, i * T:(i + 1) * T], (dn, f"{i}_{m}")), xt.s(m, (ALL, m, ALL)), "xout")
                if (not first) and i + 1 < ctx["n_tiles"]:
                    S.dma("sp", xt.s(m, (ALL, m, ALL)), TA(sdv[:, m, (i + 1) * T:(i + 2) * T], (sn, f"{i + 1}_{m}")),
                          "xin")
            hk = [(p_sb.name, str(c)) for c in range(4, 7)]
            a0 = p_sb.h[:, 4:7, 0:3]
            a1 = p_sb.h[:, 4:7, T:T + 3]
            S.op("pool", lambda e, a0=a0, a1=a1: e.tensor_copy(a0, a1), hk, hk)
            hk2 = [(p_sb.name, str(c)) for c in range(10, 21)]
            b0 = p_sb.h[:, 10:21, 0:3]
            b1 = p_sb.h[:, 10:21, T:T + 3]
            S.op("pool", lambda e, b0=b0, b1=b1: e.tensor_copy(b0, b1), hk2, hk2)
        S.wait_all_dma("sp", ["xout"])


_INPUT_ORDER = None


def kernel(**inp):
    inp = {k: np.asarray(v) for k, v in inp.items()}
    pk = pack_params(inp)
    nc = build_program()
    x = np.ascontiguousarray(inp["x"], dtype=np.float32)
    c = np.asarray(inp["c"], np.float32)
    shared = dict(
        w_mod=np.ascontiguousarray(inp["w_mod"], dtype=np.float32),
        w_in=np.ascontiguousarray(inp["w_in"], dtype=np.float32),
        w_out=np.ascontiguousarray(inp["w_out"], dtype=np.float32),
        w_up=np.ascontiguousarray(inp["ffn_w_up"], dtype=np.float32),
        w_dn=np.ascontiguousarray(inp["ffn_w_down"], dtype=np.float32),
        g2=np.ascontiguousarray(inp["rwkv_g2"], dtype=np.float32),
        **pk,
    )
    in_maps = []
    for b in range(NB):
        m = dict(shared)
        m["x"] = x[b]
        m["cT"] = _fm(c[b])
        in_maps.append(m)
    res = run_bass_kernel_spmd(nc, in_maps, core_ids=list(range(NB)))
    out = np.stack([np.asarray(r["y"], dtype=np.float32) for r in res.results], axis=0)
    return out
```
